# Optimizing a Trainium2 kernel written in Bass

```python
import jax, jax.numpy as jnp
from jax import lax
import numpy as np

D_MODEL = 1024
BATCH = 4
SEQ = 4096
DEPTH = 2

CTX_LEN = 256
GRID_W = 64

A_HEADS = 4
A_DK = 128
A_DV = 128
A_WIDTH = A_HEADS * A_DV
A_CHUNK = 64
B_GROUPS = 4
B_GW = 64
B_WIDTH = B_GROUPS * B_GW
B_CHUNK = 128
C_GROUPS = 4
C_WIDTH = 256
C_KERNEL = 31
MIX_WIDTH = A_WIDTH + B_WIDTH + C_WIDTH
IN_COLS = 5 * A_WIDTH + 2 * B_WIDTH + 2 * C_WIDTH
SPLITS = (A_WIDTH, 2 * A_WIDTH, 3 * A_WIDTH, 4 * A_WIDTH, 5 * A_WIDTH,
          5 * A_WIDTH + B_WIDTH, 5 * A_WIDTH + 2 * B_WIDTH, 5 * A_WIDTH + 2 * B_WIDTH + C_WIDTH)
D_FF = -(-8 * D_MODEL // (3 * 256)) * 256
EPS = 1e-6

kernel_name = 'hybrid_hgrn2_gmlp_conformer_dit'


def _rms_norm(x, g):
    x32 = x.astype(jnp.float32)
    y = x32 * lax.rsqrt(jnp.mean(x32 * x32, axis=-1, keepdims=True) + EPS)
    return (y * g.astype(jnp.float32)).astype(x.dtype)


def _group_layer_norm(x, g, b, groups):
    shp = x.shape
    x32 = x.astype(jnp.float32).reshape(shp[:-1] + (groups, shp[-1] // groups))
    mu = jnp.mean(x32, axis=-1, keepdims=True)
    var = jnp.mean(jnp.square(x32 - mu), axis=-1, keepdims=True)
    y = ((x32 - mu) * lax.rsqrt(var + EPS)).reshape(shp)
    return (y * g.astype(jnp.float32) + b.astype(jnp.float32)).astype(x.dtype)


def _modulate(h, shift, scale):
    return h * (1 + scale) + shift


def _hgrn2_chunk_scan(q, log_f, k, v, s0):
    bsz, t_len, h, _ = q.shape
    dv = v.shape[-1]
    n = t_len // A_CHUNK

    def to_chunks(a):
        return a.reshape(bsz, n, A_CHUNK, h, a.shape[-1]).transpose(1, 0, 3, 2, 4)

    mask = jnp.tril(jnp.ones((A_CHUNK, A_CHUNK), dtype=bool))[:, :, None]

    def step(s, inp):
        qc, lfc, kc, vc = inp
        b = jnp.cumsum(lfc, axis=2)
        inter = jnp.einsum('bhtk,bhkv->bhtv', qc * jnp.exp(b), s)
        diff = b[:, :, :, None, :] - b[:, :, None, :, :]
        decay = jnp.where(mask, jnp.exp(jnp.where(mask, diff, 0.0)), 0.0)
        scores = jnp.einsum('bhtk,bhtsk,bhsk->bhts', qc, decay, kc)
        intra = jnp.einsum('bhts,bhsv->bhtv', scores, vc)
        b_last = b[:, :, -1, :]
        k_dec = kc * jnp.exp(b_last[:, :, None, :] - b)
        s_new = jnp.exp(b_last)[..., None] * s + jnp.einsum('bhsk,bhsv->bhkv', k_dec, vc)
        return s_new, inter + intra

    s_t, o = lax.scan(step, s0, (to_chunks(q), to_chunks(log_f), to_chunks(k), to_chunks(v)))
    o = o.transpose(1, 0, 3, 2, 4).reshape(bsz, t_len, h, dv)
    return o, s_t


def _bidir_hgrn2(q, zf_fwd, zf_bwd, vi, lb, s0_f, s0_b):
    bsz, t_len, _ = q.shape
    lb = lb.astype(jnp.float32)

    def heads(a):
        return a.astype(jnp.float32).reshape(bsz, t_len, A_HEADS, -1)

    def gates(z, lb_d):
        z32 = z.astype(jnp.float32)
        f = lb_d + (1.0 - lb_d) * jax.nn.sigmoid(z32)
        k = (1.0 - lb_d) * jax.nn.sigmoid(-z32)
        return heads(jnp.log(f)), heads(k)

    qh, vh = heads(q), heads(vi)
    lf_f, k_f = gates(zf_fwd, lb[0])
    lf_b, k_b = gates(zf_bwd, lb[1])
    o_f, s_f = _hgrn2_chunk_scan(qh, lf_f, k_f, vh, s0_f)
    o_b, s_b = _hgrn2_chunk_scan(qh[:, ::-1], lf_b[:, ::-1], k_b[:, ::-1], vh[:, ::-1], s0_b)
    return o_f + o_b[:, ::-1], s_f, s_b


def _chunk_token_mlp(u, v, ln_g, ln_b, w_s, b_s):
    bsz, t_len, _ = u.shape
    n = t_len // B_CHUNK
    u = jax.nn.gelu(u)
    v = _group_layer_norm(jax.nn.gelu(v), ln_g, ln_b, B_GROUPS)
    vg = v.reshape(bsz, n, B_CHUNK, B_GROUPS, B_GW)
    mixed = jnp.einsum('gts,bnsgd->bntgd', w_s, vg) + b_s.T[:, :, None]
    return u * mixed.reshape(bsz, t_len, B_WIDTH)


def _conv_module(a, gate, dw_w, dw_b, ln_g, ln_b, pw_w, pw_b, n_seg):
    z = a * jax.nn.sigmoid(gate)
    bsz, t_len, ch = z.shape
    zs = z.reshape(bsz * n_seg, t_len // n_seg, ch)
    zs = lax.conv_general_dilated(zs, dw_w[:, None, :], window_strides=(1,),
                                  padding=[(C_KERNEL // 2, C_KERNEL // 2)],
                                  dimension_numbers=('NWC', 'WIO', 'NWC'),
                                  feature_group_count=ch)
    z = zs.reshape(bsz, t_len, ch) + dw_b
    z = jax.nn.silu(_group_layer_norm(z, ln_g, ln_b, C_GROUPS))
    return z @ pw_w + pw_b


def _token_mixer(p, lw, lb, s0_f, s0_b, n_seg):
    q, zf, zb, vi, g, u, v, ca, cg = jnp.split(p, SPLITS, axis=-1)
    bsz, t_len, _ = p.shape
    o, s_f, s_b = _bidir_hgrn2(q, zf, zb, vi, lb, s0_f, s0_b)
    o = o * lax.rsqrt(jnp.mean(o * o, axis=-1, keepdims=True) + EPS)
    o = o.reshape(bsz, t_len, A_WIDTH) * lw['hgrn_onorm_g'].astype(jnp.float32)
    ya = (o * jax.nn.silu(g.astype(jnp.float32))).astype(p.dtype)
    yb = _chunk_token_mlp(u, v, lw['gmlp_ln_g'], lw['gmlp_ln_b'], lw['gmlp_w_s'], lw['gmlp_b_s'])
    yc = _conv_module(ca, cg, lw['conv_dw_w'], lw['conv_dw_b'], lw['conv_ln_g'], lw['conv_ln_b'],
                      lw['conv_pw_w'], lw['conv_pw_b'], n_seg)
    y = jnp.concatenate([ya, yb, yc], axis=-1) @ lw['w_out']
    return y, s_f, s_b


def _swiglu(h, w13, w2):
    a, b = jnp.split(h @ w13, 2, axis=-1)
    return (jax.nn.silu(a) * b) @ w2


def _layer(x, ctx, c, c_ctx, lw, lb, rows, last):
    mod = jax.nn.silu(c) @ lw['ada_w'] + lw['ada_b']
    mod_ctx = jax.nn.silu(c_ctx) @ lw['ada_w'] + lw['ada_b']
    sh1, sc1, g1, sh2, sc2, g2 = jnp.split(mod[:, None, :], 6, axis=-1)
    csh1, csc1, cg1, csh2, csc2, cg2 = jnp.split(mod_ctx, 6)
    zero = jnp.zeros((x.shape[0], A_HEADS, A_DK, A_DV), jnp.float32)
    pc = _modulate(_rms_norm(ctx, lw['norm1_g']), csh1, csc1) @ lw['w_in']
    if last:
        qc, zfc, zbc, vic = jnp.split(pc[..., :4 * A_WIDTH], 4, axis=-1)
        _, s_f, s_b = _bidir_hgrn2(qc, zfc, zbc, vic, lb, zero, zero)
    else:
        yc, s_f, s_b = _token_mixer(pc, lw, lb, zero, zero, 1)
        ctx = ctx + cg1 * yc
        hc = _modulate(_rms_norm(ctx, lw['norm2_g']), csh2, csc2)
        ctx = ctx + cg2 * _swiglu(hc, lw['ffn_w13'], lw['ffn_w2'])
    px = _modulate(_rms_norm(x, lw['norm1_g']), sh1, sc1) @ lw['w_in']
    y, _, _ = _token_mixer(px, lw, lb, s_f, s_b, rows)
    x = x + g1 * y
    hx = _modulate(_rms_norm(x, lw['norm2_g']), sh2, sc2)
    x = x + g2 * _swiglu(hx, lw['ffn_w13'], lw['ffn_w2'])
    return x, ctx


def setup_inputs(seed: int = 0) -> dict:
    key = jax.random.key(seed)
    ks = jax.random.split(key, 25)
    f32 = jnp.float32
    L, D = DEPTH, D_MODEL

    def nrm(k, shape, scale):
        return jax.random.normal(k, shape, f32) * scale

    def gain(k, shape):
        return 1.0 + 0.01 * jax.random.normal(k, shape, f32)

    return {
        'x': nrm(ks[0], (BATCH, SEQ, D), 1.0),
        'c': nrm(ks[1], (BATCH, D), 1.0),
        'ctx': nrm(ks[2], (BATCH, CTX_LEN, D), 1.0),
        'c_ctx': nrm(ks[3], (D,), 1.0),
        'ada_w': nrm(ks[4], (L, D, 6 * D), 0.5 * D ** -0.5),
        'ada_b': nrm(ks[5], (L, 6 * D), 0.01),
        'norm1_g': gain(ks[6], (L, D)),
        'w_in': nrm(ks[7], (L, D, IN_COLS), D ** -0.5),
        'hgrn_lb_logits': nrm(ks[8], (L, 2, A_WIDTH), 1.0),
        'hgrn_onorm_g': gain(ks[9], (L, A_WIDTH)),
        'gmlp_ln_g': gain(ks[10], (L, B_WIDTH)),
        'gmlp_ln_b': nrm(ks[11], (L, B_WIDTH), 0.01),
        'gmlp_w_s': nrm(ks[12], (L, B_GROUPS, B_CHUNK, B_CHUNK), B_CHUNK ** -0.5),
        'gmlp_b_s': gain(ks[13], (L, B_GROUPS, B_CHUNK)),
        'conv_dw_w': nrm(ks[14], (L, C_KERNEL, C_WIDTH), C_KERNEL ** -0.5),
        'conv_dw_b': nrm(ks[15], (L, C_WIDTH), 0.01),
        'conv_ln_g': gain(ks[16], (L, C_WIDTH)),
        'conv_ln_b': nrm(ks[17], (L, C_WIDTH), 0.01),
        'conv_pw_w': nrm(ks[18], (L, C_WIDTH, C_WIDTH), C_WIDTH ** -0.5),
        'conv_pw_b': nrm(ks[19], (L, C_WIDTH), 0.01),
        'w_out': nrm(ks[20], (L, MIX_WIDTH, D), MIX_WIDTH ** -0.5),
        'norm2_g': gain(ks[21], (L, D)),
        'ffn_w13': nrm(ks[22], (L, D, 2 * D_FF), D ** -0.5),
        'ffn_w2': nrm(ks[23], (L, D_FF, D), D_FF ** -0.5),
        'final_norm_g': gain(ks[24], (D,)),
    }


def reference(x, c, ctx, c_ctx, ada_w, ada_b, norm1_g, w_in, hgrn_lb_logits, hgrn_onorm_g,
              gmlp_ln_g, gmlp_ln_b, gmlp_w_s, gmlp_b_s, conv_dw_w, conv_dw_b, conv_ln_g,
              conv_ln_b, conv_pw_w, conv_pw_b, w_out, norm2_g, ffn_w13, ffn_w2, final_norm_g):
    rows = x.shape[1] // GRID_W
    lb_p = jax.nn.softmax(hgrn_lb_logits.astype(jnp.float32), axis=0)
    lbs = jnp.cumsum(lb_p, axis=0) - lb_p[0:1]
    for l in range(DEPTH):
        lw = {
            'ada_w': ada_w[l], 'ada_b': ada_b[l], 'norm1_g': norm1_g[l], 'w_in': w_in[l],
            'hgrn_onorm_g': hgrn_onorm_g[l], 'gmlp_ln_g': gmlp_ln_g[l], 'gmlp_ln_b': gmlp_ln_b[l],
            'gmlp_w_s': gmlp_w_s[l], 'gmlp_b_s': gmlp_b_s[l], 'conv_dw_w': conv_dw_w[l],
            'conv_dw_b': conv_dw_b[l], 'conv_ln_g': conv_ln_g[l], 'conv_ln_b': conv_ln_b[l],
            'conv_pw_w': conv_pw_w[l], 'conv_pw_b': conv_pw_b[l], 'w_out': w_out[l],
            'norm2_g': norm2_g[l], 'ffn_w13': ffn_w13[l], 'ffn_w2': ffn_w2[l],
        }
        x, ctx = _layer(x, ctx, c, c_ctx, lw, lbs[l], rows, l == DEPTH - 1)
    return _rms_norm(x, final_norm_g)
```

```python
import numpy as np
from contextlib import ExitStack
import concourse.bass as bass
import concourse.mybir as mybir
from concourse.bass_utils import run_bass_kernel_spmd

F32 = mybir.dt.float32
BF16 = mybir.dt.bfloat16
I32 = mybir.dt.int32
AF = mybir.ActivationFunctionType
ALU = mybir.AluOpType

L = 2
D = 1024
T = 2304
NB = 9
BS = 256
DFF = 2816
EPS = 1e-6


class Buf:
    def __init__(self, name, excl=False):
        self.name = name
        self.w = None
        self.r = {}
        self.excl = excl


class Sched:
    ENG = ("pe", "act", "dve", "pool", "sp")

    def __init__(self, nc):
        self.nc = nc
        self.prog = {e: [] for e in self.ENG}
        self.NR = 8
        self.units = list(self.ENG) + [f"d_{q}{r}" for q in ("sp", "pool") for r in range(self.NR)] + ["cc"]
        self.dcount = {"sp": 0, "pool": 0}
        self.cnt = {u: 0 for u in self.units}
        self.mult = {u: (16 if u.startswith("d_") else 1) for u in self.units}
        self.seen = {e: {} for e in self.ENG}
        self.sem = {}

    def _deps(self, e, reads, writes):
        deps = {}
        for b in reads:
            if b.w is not None:
                f, i = b.w
                deps[f] = max(deps.get(f, 0), i)
        for b in writes:
            if b.w is not None:
                f, i = b.w
                deps[f] = max(deps.get(f, 0), i)
            for f, i in b.r.items():
                deps[f] = max(deps.get(f, 0), i)
        for f, i in deps.items():
            if f == e and e == "pe":
                continue
            if self.seen[e].get(f, 0) >= i:
                continue
            self.seen[e][f] = i
            self.prog[e].append(("wait", f, i * self.mult[f]))

    def op(self, e, fn, reads=(), writes=(), inc=True):
        ex = [r for r in reads if r.excl]
        if ex:
            writes = list(writes) + [r for r in ex if r not in writes]
            reads = [r for r in reads if not r.excl]
        self._deps(e, reads, writes)
        idx = self.cnt[e] + 1
        if inc:
            self.cnt[e] = idx
        self.prog[e].append(("inst", fn, e if inc else None, 1))
        for b in reads:
            b.r[e] = max(b.r.get(e, 0), idx)
        for b in writes:
            b.w = (e, idx)
            b.r = {}

    def dma(self, q, fn, reads=(), writes=(), unit=None):
        if unit is None:
            d = f"d_{q}{self.dcount[q] % self.NR}"
            self.dcount[q] += 1
            if self.cnt[d] > 0 and self.seen[q].get(d, 0) < self.cnt[d]:
                self.seen[q][d] = self.cnt[d]
                self.prog[q].append(("wait", d, self.cnt[d] * self.mult[d]))
        else:
            d = unit
        self._deps(q, reads, writes)
        self.cnt[d] += 1
        idx = self.cnt[d]
        self.prog[q].append(("inst", fn, d, self.mult[d]))
        for b in reads:
            b.r[d] = max(b.r.get(d, 0), idx)
        for b in writes:
            b.w = (d, idx)
            b.r = {}

    def wait_all(self, e, bufs):
        self._deps(e, bufs, ())

    def emit(self, block):
        S = self

        def run(e, h):
            for it in S.prog[e]:
                if it[0] == "wait":
                    h.wait_ge(S.sem[it[1]], it[2])
                else:
                    ins = it[1](h)
                    if it[2] is not None:
                        ins.then_inc(S.sem[it[2]], it[3])

        @block.tensor
        def _(h):
            run("pe", h)

        @block.scalar
        def _(h):
            run("act", h)

        @block.vector
        def _(h):
            run("dve", h)

        @block.gpsimd
        def _(h):
            run("pool", h)

        @block.sync
        def _(h):
            run("sp", h)


class _Stop(Exception):
    pass


def build(layers=(0, 1), dbg=None, stop=None, dumps=None):
    nc = bass.Bass("TRN2", target_bir_lowering=False)

    def din(name, shape, dt=F32):
        return nc.dram_tensor(name, list(shape), dt, kind="ExternalInput").ap()

    x_d = din("x", [T, D])
    cc_d = din("cc", [128, 8, 2])
    adaw_d = din("ada_w", [L, D, 6 * D])
    adab_d = din("ada_b", [128, L, 48])
    n1g_d = din("n1g", [128, L, 8])
    n2g_d = din("n2g", [128, L, 8])
    fng_d = din("fng", [128, 8])
    win_d = din("w_in", [L, D, 3584])
    lbl_d = din("lbl", [128, L, 2, 4])
    ong_d = din("ong", [128, L, 4])
    gln_d = din("gln", [128, L, 2, 256])
    wsT_d = din("wsT", [128, L, 4, 128])
    bs_d = din("bs", [128, L, 2, 128])
    dww_d = din("dww", [128, L, 2, 31])
    cpar_d = din("cpar", [128, L, 4, 2])
    pww_d = din("pww", [L, 256, 256])
    wout_d = din("w_out", [L, D, D])
    w13_d = din("w13", [L, D, 2 * DFF])
    w2_d = din("w2", [L, DFF, D])
    sel_d = din("sel", [128, 2])
    y_d = nc.dram_tensor("y", [2048, D], F32, kind="ExternalOutput").ap()
    if dbg:
        dbg_d = nc.dram_tensor("dbg", [128, dbg], F32, kind="ExternalOutput").ap()
    cc_in = [nc.dram_tensor(f"cc_in{l}", [128, 512], BF16, kind="Internal").ap() for l in range(L)]
    cc_out = [nc.dram_tensor(f"cc_out{l}", [256, 512], BF16, kind="Internal", addr_space="Local").ap() for l in range(L)]

    bar_in = [nc.dram_tensor(f"bar_in{l}", [128, 16], BF16, kind="Internal").ap() for l in range(L)]
    bar_out = [nc.dram_tensor(f"bar_out{l}", [256, 16], BF16, kind="Internal", addr_space="Local").ap() for l in range(L)]

    es = ExitStack()
    with es:
        def sb(name, shape, dt):
            return es.enter_context(nc.sbuf_tensor("sb_" + name, list(shape), dt))

        def pst(name, shape, dt):
            return es.enter_context(nc.psum_tensor("ps_" + name, list(shape), dt))

        S = Sched(nc)
        for u in S.units:
            S.sem[u] = es.enter_context(nc.semaphore("s_" + u))

        XT = sb("XT", [128, 8, T], F32)
        mix = sb("mix", [128, 8, T], BF16)
        bufA = sb("bufA", [128, 8, 1536], BF16)
        bufB = sb("bufB", [128, 8, 1536], BF16)
        hTb = sb("hTb", [128, 8, BS], BF16)
        gt = sb("gt", [128, 14, BS], F32)
        nt = gt[:, 0:8, :]
        scn = sb("scn", [128, 9216], BF16)
        qtT = scn[:, 0:1024].rearrange("p (h t) -> p h t", h=4)
        qhT = scn[:, 1024:2048].rearrange("p (h t) -> p h t", h=4)
        ktT = scn[:, 2048:3072].rearrange("p (h t) -> p h t", h=4)
        kt_tok = scn[:, 3072:5120].rearrange("p (a r n) -> p a r n", a=2, r=2)
        v_tok = scn[:, 5120:6144].rearrange("p (a n) -> p a n", a=2)
        dgD = scn[:, 6144:8192].rearrange("p (h c k) -> p h c k", h=4, c=4)
        yaT = scn[:, 8192:9216].rearrange("p (h t) -> p h t", h=4)
        Dw1 = scn[:, 0:3968].rearrange("p (t k) -> p t k", t=31)
        GE = sb("GE", [128, 4, 2, 4], F32)
        Sst = sb("Sst", [128, 4, 128], BF16)
        A_sb = sb("A_sb", [128, 2, 4, 64], BF16)
        ident_f = sb("ident_f", [128, 128], F32)
        ident_b = sb("ident_b", [128, 128], BF16)
        ones_f = sb("ones_f", [128, 128], F32)
        gones = sb("gones", [128, 128], F32)
        maskt = sb("maskt", [128, 2, 2, 64], I32)
        rmask = sb("rmask", [128, 256], F32)
        onesw = gt[:, 0, 0:64]
        cc_s = sb("cc_s", [128, 8, 2], F32)
        sil = sb("sil", [128, 8, 2], BF16)
        PAR = sb("PAR", [128, L, 2, 6, 8], F32)
        adab = sb("adab", [128, L, 48], F32)
        n1g = sb("n1g", [128, L, 8], F32)
        n2g = sb("n2g", [128, L, 8], F32)
        fng = sb("fng", [128, 8], F32)
        lbl = sb("lbl", [128, L, 2, 4], F32)
        LB = sb("LB", [128, L, 2, 4, 3], F32)
        ong = sb("ong", [128, L, 4], F32)
        gln = sb("gln", [128, 1, 2, 256], F32)
        wsT = sb("wsT", [128, 1, 4, 128], BF16)
        bs_s = sb("bs_s", [128, 1, 2, 128], F32)
        dww = sb("dww", [128, L, 2, 31], F32)
        cpar = sb("cpar", [128, L, 4, 2], F32)
        pww = sb("pww", [128, 1, 2, 256], BF16)
        selt = sb("selt", [128, 2], F32)
        gath = sb("gath", [128, 2, 512], BF16)
        stg = gt[:, 0:4, :].rearrange("p (a b) t -> p a (b t)", a=2)
        small = sb("small", [128, 64], F32)
        if dbg:
            dstg = sb("dstg", [128, 1024], F32)
        osb = mix[:, 0, 0:2048].bitcast(F32)

        pA = pst("pA", [128, 512], F32)
        pB = pst("pB", [128, 512], F32)
        pV = pst("pV", [128, 512], F32)
        pT = pst("pT", [128, 1024], BF16)
        pSW = pst("pSW", [128, 1024], F32)
        pO = pst("pO", [128, 1024], F32)
        pScs = [pSW[:, 0:256].rearrange("p (h t) -> p h t", h=4), pSW[:, 256:512].rearrange("p (h t) -> p h t", h=4)]
        pW = pSW[:, 512:1024].rearrange("p (h v) -> p h v", h=4)
        pOv = pO[:, :].rearrange("p (h t) -> p h t", h=4)

        block = es.enter_context(nc.Block())

        bXT = [Buf(f"XT{i}") for i in range(NB)]
        bmix = [Buf(f"mix{i}") for i in range(NB)]
        b = {n: Buf(n) for n in ("bufA", "bufB", "Dw", "hTb", "nt", "qtT", "qhT", "ktT", "kt_tok", "v_tok", "GE",
                                 "A0", "A1", "yaT", "const", "par", "PAR", "LB", "pA", "pB", "pV", "pT", "pS0", "pS1",
                                 "pO", "gath", "stg", "small", "osb", "y", "sil", "ccin", "ccout", "dbg", "Sfin", "Dw0", "lpar")}
        bgt = [Buf(f"gt{i}") for i in range(14)]
        bS = [Buf(f"S{h}") for h in range(4)]
        bpW = [Buf(f"pW{h}") for h in range(4)]
        bdg = [Buf(f"dg{h}") for h in range(4)]
        bosb = [Buf("osb0"), Buf("osb1")]
        NTB = bgt[0:8]
        STB = bgt[0:4]
        bA = [b["A0"], b["A1"]]
        for nm in ("pA", "pB", "pV", "pT", "pS0", "pO"):
            b[nm].excl = True
        b["pS1"] = b["pS0"]
        b["pSW1"] = Buf("pSW1", excl=True)
        bpS = [b["pS0"], b["pS0"]]
        Wreg = [(pA[:, 0:128], b["pA"]), (pB[:, 0:128], b["pB"]), (pV[:, 0:128], b["pV"]), (pSW[:, 512:640], b["pSW1"])]
        bpW = [w[1] for w in Wreg]

        def MM(out, lhsT, rhs, start, stop, reads, writes):
            S.op("pe", lambda h: h.matmul(out, lhsT, rhs, start=start, stop=stop), reads, writes, inc=stop)

        def TR(out, in_, ident, reads, writes):
            S.op("pe", lambda h: h.transpose(out, in_, ident), reads, writes, inc=True)

        def ACT(out, in_, func, reads, writes, bias=None, scale=None):
            kw = {}
            if bias is not None:
                kw["bias"] = bias
            if scale is not None:
                kw["scale"] = scale
            S.op("act", lambda h: h.activation(out=out, in_=in_, func=func, **kw), reads, writes)

        def TT(e, out, in0, in1, op, reads, writes):
            S.op(e, lambda h: h.tensor_tensor(out=out, in0=in0, in1=in1, op=op), reads, writes)

        def TS(e, out, in0, s1, s2, op0, op1, reads, writes):
            if op1 is None:
                S.op(e, lambda h: h.tensor_scalar(out=out, in0=in0, scalar1=s1, scalar2=None, op0=op0), reads, writes)
            else:
                S.op(e, lambda h: h.tensor_scalar(out=out, in0=in0, scalar1=s1, scalar2=s2, op0=op0, op1=op1), reads, writes)

        def STT(out, in0, scalar, in1, op0, op1, reads, writes):
            S.op("dve", lambda h: h.scalar_tensor_tensor(out=out, in0=in0, scalar=scalar, in1=in1, op0=op0, op1=op1), reads, writes)

        def CP(e, out, in_, reads, writes):
            if e == "act":
                S.op(e, lambda h: h.activation(out=out, in_=in_, func=AF.Identity), reads, writes)
            else:
                S.op(e, lambda h: h.tensor_copy(out=out, in_=in_), reads, writes)

        def DMA(q, out, in_, reads, writes):
            S.dma(q, lambda h: h.dma_start(out=out, in_=in_), reads, writes)

        def stage(k):
            if stop is not None and k > stop:
                raise _Stop()

        try:
            cst = [b["const"]]
            S.op("pool", lambda h: h.memset(ones_f[:], 1.0), (), cst)
            S.op("pool", lambda h: h.memset(onesw, 1.0), (), cst + [bgt[0]])
            S.op("pool", lambda h: h.memset(ident_f[:], 0.0), (), cst)
            S.op("pool", lambda h: h.affine_select(out=ident_f[:], in_=ident_f[:], pattern=[[1, 128]], compare_op=ALU.not_equal,
                                                    fill=1.0, base=0, channel_multiplier=-1), cst, cst)
            S.op("pool", lambda h: h.tensor_copy(out=ident_b[:], in_=ident_f[:]), cst, cst)
            S.op("pool", lambda h: h.memset(gones[:], 0.0), (), cst)
            S.op("pool", lambda h: h.memset(gones[0:64, 0:64], 1.0), cst, cst)
            S.op("pool", lambda h: h.memset(gones[64:128, 64:128], 1.0), cst, cst)
            S.op("pool", lambda h: h.memset(rmask[:], 1.0), (), cst)
            S.op("pool", lambda h: h.memset(rmask[:].rearrange("p (c t) -> p c t", t=64)[:, :, 0:1], 0.0), cst, cst)
            S.op("pool", lambda h: h.memset(A_sb[:], 0.0), (), bA)
            S.op("pool", lambda h: h.memset(maskt[:], 0.0), (), cst)
            for half in (0, 1):
                ps_ = slice(half * 64, half * 64 + 64)
                S.op("pool", lambda h, ps_=ps_, half=half: h.affine_select(out=maskt[ps_, 0, half, :], in_=onesw[ps_, :],
                                                                 pattern=[[1, 64]], compare_op=ALU.is_ge, fill=0.0, base=0,
                                                                 channel_multiplier=-1), cst + [bgt[0]], cst)
                S.op("pool", lambda h, ps_=ps_, half=half: h.affine_select(out=maskt[ps_, 1, half, :], in_=onesw[ps_, :],
                                                                 pattern=[[-1, 64]], compare_op=ALU.is_ge, fill=0.0, base=0,
                                                                 channel_multiplier=1), cst + [bgt[0]], cst)

            par = [b["par"], b["lpar"]]
            for dst, src in ((cc_s, cc_d), (adab, adab_d), (n1g, n1g_d), (n2g, n2g_d), (fng, fng_d), (lbl, lbl_d), (ong, ong_d),
                             (dww, dww_d), (cpar, cpar_d), (selt, sel_d)):
                DMA("sp", dst[:], src, (), par)
            S.op("dve", lambda h: h.memset(LB[:], 0.0), (), [b["LB"]])
            TT("dve", small[:, 0:8], lbl[:, 1].rearrange("p a b -> p (a b)"), lbl[:, 0].rearrange("p a b -> p (a b)"), ALU.subtract, par, [b["small"]])
            ACT(LB[:, 1, :, :, 0], small[:, 0:8].rearrange("p (a b) -> p a b", a=2), AF.Sigmoid, [b["small"]], [b["LB"]])
            for l in range(L):
                TS("dve", LB[:, l, :, :, 1], LB[:, l, :, :, 0], -1.0, 1.0, ALU.mult, ALU.add, [b["LB"]], [b["LB"]])
                TS("dve", LB[:, l, :, :, 2], LB[:, l, :, :, 0], 1.0, -1.0, ALU.mult, ALU.add, [b["LB"]], [b["LB"]])
            ACT(sil[:], cc_s[:], AF.Silu, par, [b["sil"]])

            pM = pA[:, 0:96].rearrange("p (n j) -> p n j", j=2)
            ada_slot = [bufA[:, :, 0:512], bufA[:, :, 512:1024], bufA[:, :, 1024:1536], bufB[:, :, 0:512], bufB[:, :, 512:1024], bufB[:, :, 1024:1536]]
            bslot = [Buf(f"adaslot{i}") for i in range(6)]
            si = 0
            for l in layers:
                for ng in range(12):
                    sl, bsl = ada_slot[si % 6], bslot[si % 6]
                    si += 1
                    DMA("pool", sl, adaw_d[l, :, ng * 512:(ng + 1) * 512].rearrange("(kc p) n -> p kc n", p=128), (), [bsl])
                    for nn in range(4):
                        for kc in range(8):
                            MM(pM[:, ng * 4 + nn, :], sl[:, kc, nn * 128:(nn + 1) * 128], sil[:, kc, :], kc == 0, kc == 7,
                               [bsl, b["sil"]], [b["pA"]])
                for j in range(2):
                    TT("dve", PAR[:, l, j].rearrange("p m k -> p (m k)"), pM[:, :, j], adab[:, l, :], ALU.add, [b["pA"], b["par"]], [b["PAR"]])
                    STT(PAR[:, l, j, 1, :], PAR[:, l, j, 1, :], 1.0, n1g[:, l, :], ALU.add, ALU.mult, [b["PAR"], b["par"]], [b["PAR"]])
                    STT(PAR[:, l, j, 4, :], PAR[:, l, j, 4, :], 1.0, n2g[:, l, :], ALU.add, ALU.mult, [b["PAR"], b["par"]], [b["PAR"]])
            relA = bslot[0:3]
            relB = bslot[3:6]

            stage(1)
            for tt in range(T // 128):
                st_ = gt[:, 4 * (tt % 2):4 * (tt % 2) + 4, :].rearrange("p a t -> p (a t)")
                STX = bgt[4 * (tt % 2):4 * (tt % 2) + 4]
                DMA("sp", st_, x_d[tt * 128:(tt + 1) * 128, :], (), STX)
                for half in range(2):
                    pb, bb = (pA, b["pA"]) if half == 0 else (pB, b["pB"])
                    for k4 in range(4):
                        kc = half * 4 + k4
                        TR(pb[:, k4 * 128:(k4 + 1) * 128], st_[:, kc * 128:(kc + 1) * 128], ident_f[:], STX + [b["const"]], [bb])
                    CP("act" if half == 0 else "dve", XT[:, half * 4:half * 4 + 4, tt * 128:(tt + 1) * 128],
                       pb[:, :].rearrange("p (k t) -> p k t", k=4), [bb], [bXT[tt // 2]])

            def rsqrt_psum(ps_ap, n, scale, reads_b, out_ap, out_b, tmp_ap, tmp_b):
                ACT(tmp_ap, ps_ap, AF.Ln, reads_b, tmp_b, bias=eps_col, scale=scale)
                ACT(out_ap, tmp_ap, AF.Exp, tmp_b, out_b, scale=-0.5)

            eps_t = small[:, 32:33]
            S.op("dve", lambda h: h.memset(eps_t, EPS), (), [b["const"]])
            eps_col = eps_t

            def norm_block(l, which, blk, out_ap, out_b, final=False):
                j = 1 if blk == 0 else 0
                t0 = blk * BS
                xs = XT[:, :, t0:t0 + BS]
                ACT(nt[:], xs, AF.Square, [bXT[blk]], NTB)
                for kc in range(8):
                    MM(pV[:, 0:BS], ones_f[:], nt[:, kc, :], kc == 0, kc == 7, NTB + [b["const"]], [b["pV"]])
                rsqrt_psum(pV[:, 0:BS], BS, 1.0 / D, [b["pV"], b["const"]], pV[:, 256:512], [b["pV"]], gt[:, 9, :], [bgt[9]])
                TT("dve", nt[:], xs, pV[:, 256:512].unsqueeze(1).broadcast_to([128, 8, BS]), ALU.mult, [bXT[blk], b["pV"]], NTB)
                for kc in range(8):
                    if final:
                        ACT(out_ap[:, kc, :], nt[:, kc, :], AF.Identity, NTB + [b["par"], b["lpar"]], out_b, scale=fng[:, kc:kc + 1])
                    else:
                        m = 0 if which == 1 else 3
                        ACT(out_ap[:, kc, :], nt[:, kc, :], AF.Identity, NTB + [b["PAR"]], out_b,
                            bias=PAR[:, l, j, m, kc:kc + 1], scale=PAR[:, l, j, m + 1, kc:kc + 1])

            def load_w(dst, l, c0, c1, src, wb):
                DMA("pool", dst, src[l, :, c0:c1].rearrange("(kc p) n -> p kc n", p=128), (), wb)

            def dump(ap, n, reads):
                DMA("sp", dbg_d[:, 0:n], ap, reads, [b["dbg"]])

            stage(2)
            for l in layers:
                last = (l == L - 1)
                WA = [b["bufA"]] + relA
                WB = [b["bufB"], b["Dw0"]] + relB
                load_w(bufA[:, :, 0:1024], l, 0, 1024, win_d, WA)
                load_w(bufA[:, :, 1024:1536], l, 1536, 2048, win_d, WA)
                relA = []
                lp = [b["lpar"]]
                DMA("sp", gln[:, 0], gln_d[:, l], (), lp)
                DMA("sp", bs_s[:, 0], bs_d[:, l], (), lp)
                DMA("pool", wsT[:, 0], wsT_d[:, l], (), lp)
                DMA("pool", pww[:, 0], pww_d[l].rearrange("(c p) n -> p c n", p=128), (), lp)
                load_w(bufB[:, :, 0:1024], l, 2560, 3584, win_d, WB)
                relB = []
                dwv = bufB[:, :, 1024:1536].rearrange("p k (j c) -> p k j c", j=4)
                TT("dve", dwv[:, 0:7], ident_b[:].unsqueeze(1).unsqueeze(1).broadcast_to([128, 7, 4, 128]),
                   dww[:, l, 0, 0:28].rearrange("p (k j) -> p k j", j=4).unsqueeze(3).broadcast_to([128, 7, 4, 128]), ALU.mult,
                   [b["par"], b["const"]], [b["Dw0"]])
                TT("dve", dwv[:, 7, 0:3], ident_b[:].unsqueeze(1).broadcast_to([128, 3, 128]),
                   dww[:, l, 0, 28:31].unsqueeze(2).broadcast_to([128, 3, 128]), ALU.mult, [b["par"], b["const"]], [b["Dw0"]])

                def hgrn_dir(dr, blocks, Wt, wb):
                    mid = 31 if dr == 0 else 32
                    eidx = 63 if dr == 0 else 0
                    S.op("pool", lambda hh: hh.memset(A_sb[:], 0.0), (), bA)
                    S.op("pool", lambda hh: hh.memset(kt_tok[:], 0.0), (), [b["kt_tok"]])
                    for blk in blocks:
                        j = 1 if blk == 0 else 0
                        t0 = blk * BS
                        first = (l == layers[0] and dr == 0 and blk == blocks[0])
                        if first:
                            stage(3.1)
                        norm_block(l, 1, blk, hTb, [b["hTb"]])
                        if first:
                            stage(3.2)
                        if True:
                            for tl in range(2):
                                for kc in range(8):
                                    MM(pV[:, :], hTb[:, kc, tl * 128:(tl + 1) * 128], Wt[:, kc, 1024:1536], kc == 0, kc == 7,
                                       [b["hTb"]] + wb, [b["pV"]])
                                CP("act", v_tok[:, tl, :], pV[:, :], [b["pV"]], [b["v_tok"]])
                        for hp in range(2):
                            hs2 = (2 * hp, 2 * hp + 1)
                            ctxs = []
                            for h in hs2:
                                pb, bb = (pA, b["pA"]) if h % 2 == 0 else (pB, b["pB"])
                                for kc in range(8):
                                    MM(pb[:, 0:256], Wt[:, kc, h * 128:(h + 1) * 128], hTb[:, kc, :], kc == 0, kc == 7, [b["hTb"]] + wb, [bb])
                                for kc in range(8):
                                    MM(pb[:, 256:512], Wt[:, kc, 512 + h * 128:512 + (h + 1) * 128], hTb[:, kc, :], kc == 0, kc == 7, [b["hTb"]] + wb, [bb])
                                g0 = 7 * (h % 2)
                                G = [bgt[g0 + i] for i in range(7)]
                                sl7 = [gt[:, g0 + i, :] for i in range(7)]
                                ctxs.append((h, pb, bb, G, sl7))
                            for (h, pb, bb, G, (sg, lf, bb_, dd, e1, e2, e3)) in ctxs:
                                ACT(sg, pb[:, 256:512], AF.Sigmoid, [bb], [G[0]])
                            for (h, pb, bb, G, (sg, lf, bb_, dd, e1, e2, e3)) in ctxs:
                                ACT(lf, sg, AF.Ln, [G[0], b["LB"]], [G[1]], bias=LB[:, l, dr, h, 0:1], scale=LB[:, l, dr, h, 1:2])
                            for (h, pb, bb, G, (sg, lf, bb_, dd, e1, e2, e3)) in ctxs:
                                S.op("dve", lambda hh, bb_=bb_, lf=lf: hh.tensor_tensor_scan(out=bb_, data0=rmask[:], data1=lf,
                                                                                          initial=0.0, op0=ALU.mult, op1=ALU.add),
                                     [G[1], b["const"]], [G[2]])
                            for (h, pb, bb, G, (sg, lf, bb_, dd, e1, e2, e3)) in ctxs:
                                b3 = bb_.rearrange("p (c t) -> p c t", t=64)
                                d3 = dd.rearrange("p (c t) -> p c t", t=64)
                                if dr == 0:
                                    TT("pool", d3, b3, b3[:, :, mid:mid + 1].broadcast_to([128, 4, 64]), ALU.subtract, [G[2]], [G[3]])
                                else:
                                    tmp = e1
                                    TT("pool", tmp, lf, bb_, ALU.subtract, [G[1], G[2]], [G[4]])
                                    TT("pool", d3, tmp.rearrange("p (c t) -> p c t", t=64), b3[:, :, 63:64].broadcast_to([128, 4, 64]), ALU.add,
                                       [G[4], G[2]], [G[3]])
                                    TT("pool", b3, d3, d3[:, :, mid:mid + 1].broadcast_to([128, 4, 64]), ALU.subtract, [G[3]], [G[2]])
                            for (h, pb, bb, G, (sg, lf, bb_, dd, e1, e2, e3)) in ctxs:
                                dsrc, dB, bsrc, bB = (dd, G[3], bb_, G[2]) if dr == 0 else (bb_, G[2], dd, G[3])
                                ACT(e1, dsrc, AF.Exp, [dB], [G[4]])
                                ACT(e2, dsrc, AF.Exp, [dB], [G[5]], scale=-1.0)
                                ACT(e3, bsrc, AF.Exp, [bB], [G[6]])
                            for (h, pb, bb, G, (sg, lf, bb_, dd, e1, e2, e3)) in ctxs:
                                kk = lf
                                TS("dve", kk, sg, LB[:, l, dr, h, 2:3], LB[:, l, dr, h, 1:2], ALU.mult, ALU.add, [G[0], b["LB"]], [G[1]])
                                TT("dve", ktT[:, h, :], kk, e2, ALU.mult, [G[1], G[5]], [b["ktT"]])
                                TT("dve", qtT[:, h, :], pb[:, 0:256], e1, ALU.mult, [bb, G[4]], [b["qtT"]])
                                TT("dve", qhT[:, h, :], pb[:, 0:256], e3, ALU.mult, [bb, G[6]], [b["qhT"]])
                                CP("dve", GE[:, h, 0, :], e3.rearrange("p (c t) -> p c t", t=64)[:, :, mid], [G[6]], [b["GE"]])
                                CP("dve", GE[:, h, 1, :], e1.rearrange("p (c t) -> p c t", t=64)[:, :, eidx], [G[4]], [b["GE"]])
                            for (h, pb, bb, G, sl7) in ctxs:
                                TT("dve", dgD[:, h, :, :], ident_b[:].unsqueeze(1).broadcast_to([128, 4, 128]),
                                   GE[:, h, 0, :].unsqueeze(2).broadcast_to([128, 4, 128]), ALU.mult, [b["GE"], b["const"]], [bdg[h]])
                        if first:
                            stage(3.3)
                        for tl in range(2):
                            for h in range(4):
                                TR(pT[:, h * 128:(h + 1) * 128], ktT[:, h, tl * 128:(tl + 1) * 128], ident_b[:], [b["ktT"], b["const"]], [b["pT"]])
                            CP("act", kt_tok[0:64, tl, 0, :], pT[0:64, 0:512], [b["pT"]], [b["kt_tok"]])
                            CP("act", kt_tok[64:128, tl, 1, :], pT[64:128, 0:512], [b["pT"]], [b["kt_tok"]])
                        if first:
                            stage(3.4)
                        corder = range(4) if dr == 0 else range(3, -1, -1)
                        for c in corder:
                            tl, par = c // 2, c % 2
                            cs = slice(c * 64, c * 64 + 64)
                            pSc = pScs[par]
                            for h in range(4):
                                MM(pSc[:, h, :], ktT[:, h, tl * 128:(tl + 1) * 128], qtT[:, h, cs], True, True, [b["ktT"], b["qtT"]], [bpS[par]])
                            S.op("dve", lambda hh, par=par, pSc=pSc: hh.copy_predicated(out=A_sb[:, par, :, :],
                                                                                   mask=maskt[:, dr, par, :].unsqueeze(1).broadcast_to([128, 4, 64]),
                                                                                   data=pSc[:, :, :]),
                                 [bpS[par], b["const"]], [bA[par]])
                            if first and c == 0:
                                stage(3.5)
                            for h in range(4):
                                hs = slice(h * 128, h * 128 + 128)
                                MM(pOv[:, h, cs], v_tok[:, tl, hs], A_sb[:, par, h, :], True, False, [b["v_tok"], bA[par]], [b["pO"]])
                                MM(pOv[:, h, cs], Sst[:, h, :], qhT[:, h, cs], False, True, [bS[h], b["qhT"]], [b["pO"]])
                                MM(Wreg[h][0], dgD[:, h, c, :], Sst[:, h, :], True, False, [bdg[h], bS[h]], [bpW[h]])
                                MM(Wreg[h][0], kt_tok[:, tl, par, hs], v_tok[:, tl, hs], False, True, [b["kt_tok"], b["v_tok"]], [bpW[h]])
                                ACT(Sst[:, h, :], Wreg[h][0], AF.Identity, [bpW[h], b["GE"]], [bS[h]], scale=GE[:, h, 1, c:c + 1])
                        if first:
                            stage(3.6)
                        if dr == 0:
                            CP("act", mix[:, 0:4, t0:t0 + BS], pOv, [b["pO"]], [bmix[blk]])
                            if first:
                                stage(3.7)
                        else:
                            osum = gt[:, 0:4, :]
                            TT("dve", osum, pOv, mix[:, 0:4, t0:t0 + BS], ALU.add, [b["pO"], bmix[blk]], bgt[0:4])
                            osq = gt[:, 4:8, :]
                            ACT(osq, osum, AF.Square, bgt[0:4], bgt[4:8])
                            for hp in range(2):
                                MM(pO[:, hp * 512:(hp + 1) * 512], ones_f[:], osq[:, 2 * hp:2 * hp + 2, :].rearrange("p a t -> p (a t)"), True, True,
                                   bgt[4:8] + [b["const"]], [b["pO"]])
                            ACT(osq.rearrange("p a t -> p (a t)"), pO[:, :], AF.Ln, [b["pO"], b["const"]], bgt[4:8], bias=eps_col, scale=1.0 / 128)
                            ACT(osq.rearrange("p a t -> p (a t)"), osq.rearrange("p a t -> p (a t)"), AF.Exp, bgt[4:8], bgt[4:8], scale=-0.5)
                            TT("dve", osum, osum, osq, ALU.mult, bgt[0:4] + bgt[4:8], bgt[0:4])
                            for h in range(4):
                                pb, bb = (pA, b["pA"]) if h % 2 == 0 else (pB, b["pB"])
                                for kc in range(8):
                                    MM(pb[:, 0:256], bufB[:, kc, 1024 + h * 128:1024 + (h + 1) * 128], hTb[:, kc, :], kc == 0, kc == 7, [b["hTb"]] + WBcur, [bb])
                                ACT(gt[:, 8, :], pb[:, 0:256], AF.Silu, [bb], [bgt[8]])
                                STT(yaT[:, h, :], osum[:, h, :], ong[:, l, h:h + 1], gt[:, 8, :], ALU.mult, ALU.mult, bgt[0:4] + [bgt[8], b["par"]], [b["yaT"]])
                            for ncn in range(8):
                                pb, bb = (pA, b["pA"]) if ncn % 2 == 0 else (pB, b["pB"])
                                for kk_ in range(8):
                                    rhs = yaT[:, kk_, :] if kk_ < 4 else mix[:, kk_, t0:t0 + BS]
                                    MM(pb[:, 0:256], bufB[:, kk_, ncn * 128:(ncn + 1) * 128], rhs, kk_ == 0, kk_ == 7,
                                       [b["yaT"], bmix[blk]] + WBcur, [bb])
                                STT(XT[:, ncn, t0:t0 + BS], pb[:, 0:256], PAR[:, l, j, 2, ncn:ncn + 1], XT[:, ncn, t0:t0 + BS], ALU.mult, ALU.add,
                                    [bb, b["PAR"], bXT[blk]], [bXT[blk]])

                stage(3 + 10 * l)
                for h in range(4):
                    S.op("pool", lambda hh, h=h: hh.memset(Sst[:, h, :], 0.0), (), [bS[h]])
                hgrn_dir(0, list(range(NB)), bufA, [b["bufA"]])
                stage(4 + 10 * l)
                DMA("sp", cc_in[l], Sst[:].rearrange("p h v -> p (h v)"), bS, [b["ccin"]])
                S.dma("pool", lambda hh, l=l: hh.collective_compute("AllGather", ALU.bypass, replica_groups=[[0, 1], [2, 3], [4, 5], [6, 7]],
                                                                    ins=[cc_in[l]], outs=[cc_out[l]]), [b["ccin"]], [b["ccout"]], unit="cc")
                S.dma("pool", lambda hh, l=l: hh.collective_compute("AllGather", ALU.bypass, replica_groups=[[0, 1], [2, 3], [4, 5], [6, 7]],
                                                                    ins=[bar_in[l]], outs=[bar_out[l]]), [b["ccout"]], [b["ccout"]], unit="cc")

                stage(5 + 10 * l)
                load_w(bufA[:, :, 512:1024], l, 1024, 1536, win_d, [b["bufA"]])
                DW1B = [b["qtT"], b["qhT"], b["ktT"], b["kt_tok"]]
                TT("dve", Dw1[:, :, :], ident_b[:].unsqueeze(1).broadcast_to([128, 31, 128]),
                   dww[:, l, 1, :].unsqueeze(2).broadcast_to([128, 31, 128]), ALU.mult, [b["par"], b["const"]], DW1B)
                bc_blocks = list(range(1 if last else 0, NB))
                zp = gt[:, 0:3, :].rearrange("p a t -> p (a t)").bitcast(BF16) if False else None
                for blk in bc_blocks:
                    j = 1 if blk == 0 else 0
                    t0 = blk * BS
                    norm_block(l, 1, blk, hTb, [b["hTb"]])
                    wbB = [b["bufB"]]
                    for ch in range(2):
                        for kc in range(8):
                            MM(pA[:, ch * 256:(ch + 1) * 256], bufB[:, kc, ch * 128:(ch + 1) * 128], hTb[:, kc, :], kc == 0, kc == 7, [b["hTb"]] + wbB, [b["pA"]])
                    gu = gt[:, 0:2, :]
                    ACT(gu, pA[:, :].rearrange("p (c t) -> p c t", c=2), AF.Gelu_apprx_tanh, [b["pA"]], bgt[0:2])
                    for tl in range(2):
                        for kc in range(8):
                            MM(pV[:, 0:256], hTb[:, kc, tl * 128:(tl + 1) * 128], bufB[:, kc, 256:512], kc == 0, kc == 7, [b["hTb"]] + wbB, [b["pV"]])
                        gv = gt[:, 2, :]
                        ACT(gv, pV[:, 0:256], AF.Gelu_apprx_tanh, [b["pV"]], [bgt[2]])
                        st6 = small[:, 0:24].rearrange("p (g s) -> p g s", g=4)
                        mv = small[:, 24:32].rearrange("p (g s) -> p g s", g=4)
                        for g in range(4):
                            S.op("dve", lambda hh, g=g: hh.bn_stats(out=st6[:, g, :], in_=gv[:, g * 64:(g + 1) * 64]), [bgt[2]], [b["small"]])
                            S.op("dve", lambda hh, g=g: hh.bn_aggr(out=mv[:, g, :], in_=st6[:, g, :]), [b["small"]], [b["small"]])
                        rs4 = small[:, 40:44]
                        ACT(small[:, 36:40], mv[:, :, 1], AF.Ln, [b["small"], b["const"]], [b["small"]], bias=eps_col, scale=1.0)
                        ACT(rs4, small[:, 36:40], AF.Exp, [b["small"]], [b["small"]], scale=-0.5)
                        vn = gt[:, 3, :]
                        for g in range(4):
                            TS("dve", vn[:, g * 64:(g + 1) * 64], gv[:, g * 64:(g + 1) * 64], mv[:, g, 0:1], rs4[:, g:g + 1], ALU.subtract, ALU.mult,
                               [bgt[2], b["small"]], [bgt[3]])
                        TT("dve", vn, vn, gln[:, 0, 0, :], ALU.mult, [bgt[3], b["par"]], [bgt[3]])
                        vln = gt[:, 4, :].bitcast(BF16)[:, 0:256]
                        TT("dve", vln, vn, gln[:, 0, 1, :], ALU.add, [bgt[3], b["par"]], [bgt[4]])
                        for g in range(4):
                            po = (g % 2) * 64
                            MM(pB[po:po + 64, (g // 2) * 256 + tl * 128:(g // 2) * 256 + (tl + 1) * 128], vln[:, g * 64:(g + 1) * 64], wsT[:, 0, g, :],
                               True, True, [bgt[4], b["par"]], [b["pB"]])
                    pB3 = pB[:, :].rearrange("p (c t) -> p c t", c=2)
                    mx = gt[:, 5:7, :]
                    for tl in range(2):
                        TT("dve", mx[:, :, tl * 128:(tl + 1) * 128], pB3[:, :, tl * 128:(tl + 1) * 128], bs_s[:, 0, :, :], ALU.add, [b["pB"], b["par"]], bgt[5:7])
                    TT("dve", mix[:, 4:6, t0:t0 + BS], mx, gu, ALU.mult, bgt[5:7] + bgt[0:2], [bmix[blk]])
                    nseg = 1 if blk == 0 else 4
                    sl_ = BS // nseg
                    zp = gt[:, 7:9, :].rearrange("p a t -> p (a t)").bitcast(BF16)
                    for ch in range(2):
                        for kc in range(8):
                            MM(pA[:, 0:256], bufB[:, kc, 512 + ch * 128:512 + (ch + 1) * 128], hTb[:, kc, :], kc == 0, kc == 7, [b["hTb"]] + wbB, [b["pA"]])
                        for kc in range(8):
                            MM(pA[:, 256:512], bufB[:, kc, 768 + ch * 128:768 + (ch + 1) * 128], hTb[:, kc, :], kc == 0, kc == 7, [b["hTb"]] + wbB, [b["pA"]])
                        sgm = gt[:, 0, :]
                        ACT(sgm, pA[:, 256:512], AF.Sigmoid, [b["pA"]], [bgt[0]])
                        zpv = zp[:, 0:nseg * (sl_ + 30)].rearrange("p (s t) -> p s t", s=nseg)
                        S.op("pool", lambda hh: hh.memset(zp[:, 0:512], 0.0), (), bgt[7:9])
                        TT("dve", zpv[:, :, 15:15 + sl_], pA[:, 0:256].rearrange("p (s t) -> p s t", s=nseg), sgm.rearrange("p (s t) -> p s t", s=nseg),
                           ALU.mult, [b["pA"], bgt[0]], bgt[7:9])
                        for tp in range(31):
                            dwt = bufB[:, tp // 4, 1024 + (tp % 4) * 128:1024 + (tp % 4 + 1) * 128] if ch == 0 else Dw1[:, tp, :]
                            MM(pV[:, 0:256].rearrange("p (s t) -> p s t", s=nseg), dwt, zpv[:, :, tp:tp + sl_], tp == 0, tp == 30,
                               bgt[7:9] + ([b["Dw0"]] if ch == 0 else DW1B), [b["pV"]])
                        cz = gt[:, 1, :]
                        ACT(cz, pV[:, 0:256], AF.Identity, [b["pV"], b["par"]], [bgt[1]], bias=cpar[:, l, 0, ch:ch + 1])
                        csq = gt[:, 2, :]
                        ACT(csq, cz, AF.Square, [bgt[1]], [bgt[2]])
                        MM(pV[:, 0:256], gones[:], cz, True, True, [bgt[1], b["const"]], [b["pV"]])
                        MM(pV[:, 256:512], gones[:], csq, True, True, [bgt[2], b["const"]], [b["pV"]])
                        mean = gt[:, 3, :]
                        ACT(mean, pV[:, 0:256], AF.Identity, [b["pV"]], [bgt[3]], scale=1.0 / 64)
                        msq = gt[:, 4, :]
                        ACT(msq, mean, AF.Square, [bgt[3]], [bgt[4]])
                        var = gt[:, 2, :]
                        STT(var, pV[:, 256:512], 1.0 / 64, msq, ALU.mult, ALU.subtract, [b["pV"], bgt[4]], [bgt[2]])
                        ACT(var, var, AF.Ln, [bgt[2], b["const"]], [bgt[2]], bias=eps_col, scale=1.0)
                        ACT(var, var, AF.Exp, [bgt[2]], [bgt[2]], scale=-0.5)
                        TT("dve", cz, cz, mean, ALU.subtract, [bgt[1], bgt[3]], [bgt[1]])
                        TT("dve", cz, cz, var, ALU.mult, [bgt[1], bgt[2]], [bgt[1]])
                        czs = gt[:, 5 + ch, :].bitcast(BF16)[:, 0:256]
                        ACT(czs, cz, AF.Silu, [bgt[1], b["par"]], [bgt[5 + ch]], bias=cpar[:, l, 2, ch:ch + 1], scale=cpar[:, l, 1, ch:ch + 1])
                    for co in range(2):
                        for ci in range(2):
                            MM(pB[:, co * 256:(co + 1) * 256], pww[:, 0, ci, co * 128:(co + 1) * 128], gt[:, 5 + ci, :].bitcast(BF16)[:, 0:256],
                               ci == 0, ci == 1, bgt[5:7] + [b["par"], b["lpar"]], [b["pB"]])
                        ACT(mix[:, 6 + co, t0:t0 + BS], pB[:, co * 256:(co + 1) * 256], AF.Identity, [b["pB"], b["par"]], [bmix[blk]],
                            bias=cpar[:, l, 3, co:co + 1])

                stage(6 + 10 * l)
                WBcur = [b["bufB"], b["Dw0"]]
                DMA("pool", bufB[:, :, 0:1024], wout_d[l].rearrange("(kc p) n -> p kc n", p=128), (), WBcur)
                load_w(bufB[:, :, 1024:1536], l, 2048, 2560, win_d, WBcur)
                DMA("sp", gath[:], cc_out[l].rearrange("(r p) n -> p r n", p=128), [b["ccout"], bmix[NB - 1], b["pB"]], [b["gath"]])
                TS("dve", stg[:, 0, :], gath[:, 0, :], selt[:, 0:1], None, ALU.mult, None, [b["gath"], b["par"]], STB)
                for h in range(4):
                    STT(Sst[:, h, :], gath[:, 1, h * 128:(h + 1) * 128], selt[:, 1:2], stg[:, 0, h * 128:(h + 1) * 128], ALU.mult, ALU.add,
                        [b["gath"], b["par"]] + STB, [bS[h]])
                hgrn_dir(1, list(range(NB - 1, 0, -1)), bufA, [b["bufA"]])
                if not last:
                    for h in range(4):
                        S.op("pool", lambda hh, h=h: hh.memset(Sst[:, h, :], 0.0), (), [bS[h]])
                    hgrn_dir(1, [0], bufA, [b["bufA"]])

                stage(7 + 10 * l)
                ffn_blocks = list(range(1 if last else 0, NB))
                def ffn_norm(blks):
                    for blk in blks:
                        norm_block(l, 2, blk, mix[:, :, blk * BS:(blk + 1) * BS], [bmix[blk]])
                tb_list = ([] if last else [(0, 256, [0])]) + [(256 + i * 512, 512, [1 + 2 * i, 2 + 2 * i]) for i in range(4)]
                slots = [(bufA, [b["bufA"]]), (bufB, [b["bufB"], b["Dw0"]])]
                pbanks = [(pA, b["pA"]), (pB, b["pB"]), (pV, b["pV"])]
                pyb = [(pSW[:, 0:512], [b["pS0"]]), (pSW[:, 512:1024], [b["pSW1"]]), (pO[:, 0:512], [b["pO"]]), (pT[:, :].bitcast(F32), [b["pT"]])]
                gT = gt[:, 0:4, :].rearrange("p a t -> p (a t)").bitcast(BF16).rearrange("p (g t) -> p g t", g=4)
                gTb = bgt[0:4]
                sa = gt[:, 4:6, :].rearrange("p a t -> p (a t)")
                pi = 0
                yi = 0
                for gi, j0 in enumerate(range(0, 22, 4)):
                    G = min(4, 22 - j0)
                    wt, wtb = slots[gi % 2]
                    wflat = wt[:].rearrange("p k n -> p (k n)")
                    w1 = wflat[:, 0:8192].rearrange("p (k a n) -> p k a n", k=8, a=2)
                    w2v = wflat[:, 8192:12288].rearrange("p (g n) -> p g n", g=4)
                    DMA("pool", w1[:, :, 0, 0:G * 128], w13_d[l, :, j0 * 128:(j0 + G) * 128].rearrange("(kc p) n -> p kc n", p=128), (), wtb)
                    DMA("pool", w1[:, :, 1, 0:G * 128], w13_d[l, :, DFF + j0 * 128:DFF + (j0 + G) * 128].rearrange("(kc p) n -> p kc n", p=128), (), wtb)
                    DMA("pool", w2v[:, 0:G, :], w2_d[l, j0 * 128:(j0 + G) * 128, :].rearrange("(g p) n -> p g n", p=128), (), wtb)
                    for ti, (t0, n, blks) in enumerate(tb_list):
                        if gi == 0:
                            if ti == 0:
                                ffn_norm(blks)
                            if ti + 1 < len(tb_list):
                                ffn_norm(tb_list[ti + 1][2])
                        xb = [bXT[i] for i in blks]
                        mb = [bmix[i] for i in blks]
                        for jj in range(G):
                            pa, pab = pbanks[pi % 3]
                            pb2, pbb = pbanks[(pi + 1) % 3]
                            pi += 2
                            for kc in range(8):
                                MM(pa[:, 0:n], w1[:, kc, 0, jj * 128:(jj + 1) * 128], mix[:, kc, t0:t0 + n], kc == 0, kc == 7, mb + wtb, [pab])
                            for kc in range(8):
                                MM(pb2[:, 0:n], w1[:, kc, 1, jj * 128:(jj + 1) * 128], mix[:, kc, t0:t0 + n], kc == 0, kc == 7, mb + wtb, [pbb])
                            ACT(sa[:, 0:n], pa[:, 0:n], AF.Silu, [pab], bgt[4:6])
                            TT("dve", gT[:, jj, 0:n], sa[:, 0:n], pb2[:, 0:n], ALU.mult, bgt[4:6] + [pbb], gTb)
                        for ncn in range(8):
                            py, pyb_ = pyb[yi % 4]
                            yi += 1
                            for jj in range(G):
                                MM(py[:, 0:n], w2v[:, jj, ncn * 128:(ncn + 1) * 128], gT[:, jj, 0:n], jj == 0, jj == G - 1, gTb + wtb, pyb_)
                            jm = 1 if blks == [0] else 0
                            STT(XT[:, ncn, t0:t0 + n], py[:, 0:n], PAR[:, l, jm, 5, ncn:ncn + 1], XT[:, ncn, t0:t0 + n], ALU.mult, ALU.add,
                                pyb_ + [b["PAR"]] + xb, xb)
                relA, relB = [], []

            stage(30)
            outb = []
            for blk in range(1, NB):
                t0 = blk * BS
                xs = XT[:, :, t0:t0 + BS]
                ACT(nt[:], xs, AF.Square, [bXT[blk]], NTB)
                for kc in range(8):
                    MM(pV[:, 0:BS], ones_f[:], nt[:, kc, :], kc == 0, kc == 7, NTB + [b["const"]], [b["pV"]])
                rsqrt_psum(pV[:, 0:BS], BS, 1.0 / D, [b["pV"], b["const"]], pV[:, 256:512], [b["pV"]], gt[:, 9, :], [bgt[9]])
                TT("dve", nt[:], xs, pV[:, 256:512].unsqueeze(1).broadcast_to([128, 8, BS]), ALU.mult, [bXT[blk], b["pV"]], NTB)
                for kc in range(8):
                    ACT(nt[:, kc, :], nt[:, kc, :], AF.Identity, NTB + [b["par"], b["lpar"]], NTB, scale=fng[:, kc:kc + 1])
                for tl in range(2):
                    for half in range(2):
                        pb, bb = (pA, b["pA"]) if half == 0 else (pB, b["pB"])
                        for k4 in range(4):
                            kc = half * 4 + k4
                            TR(pb[:, k4 * 128:(k4 + 1) * 128], nt[:, kc, tl * 128:(tl + 1) * 128], ident_f[:], NTB + [b["const"]], [bb])
                        ob = mix[:, tl, 0:2048].bitcast(F32)
                        CP("act" if half == 0 else "dve", ob[:, half * 512:(half + 1) * 512], pb[:, :], [bb], bmix + [bosb[tl]])
                    r0 = (blk - 1) * BS + tl * 128
                    DMA("sp", y_d[r0:r0 + 128, :], ob[:], [bosb[tl]], [b["y"]])

        except _Stop:
            pass
        if dumps:
            env = dict(locals())
            ALLB = list(b.values()) + bXT + bmix + bgt + bS + bpW + bdg
            bdst = Buf("dstg")
            off = 0
            for (ap, n) in dumps(env):
                dst = dbg_d[:, off:off + n]
                stv = dstg[:, 0:n]
                if len(ap.shape) == 3:
                    dst = dst.rearrange("p (a t) -> p a t", a=ap.shape[1])
                    stv = stv.rearrange("p (a t) -> p a t", a=ap.shape[1])
                S.op("dve", lambda h, ap=ap, stv=stv: h.tensor_copy(out=stv, in_=ap), ALLB + [bdst], [bdst])
                S.dma("sp", lambda h, stv=stv, dst=dst: h.dma_start(out=dst, in_=stv), [bdst], [b["dbg"], bdst])
                off += n
        S.wait_all("sp", [b["y"]])
        if dbg:
            S.wait_all("sp", [b["dbg"]])
        S.emit(block)
    return nc


def _col(v):
    v = np.asarray(v)
    lead = v.shape[:-1]
    n = v.shape[-1] // 128
    r = v.reshape(lead + (n, 128))
    return np.ascontiguousarray(np.moveaxis(r, -1, 0))


def prep_inputs(inp):
    f = lambda a: np.ascontiguousarray(np.asarray(a, dtype=np.float32))
    x, c, ctx, c_ctx = f(inp["x"]), f(inp["c"]), f(inp["ctx"]), f(inp["c_ctx"])
    w_in = f(inp["w_in"])
    w_in_odd = w_in.copy()
    w_in_odd[:, :, 512:1024] = w_in[:, :, 1024:1536]
    w_in_odd[:, :, 1024:1536] = w_in[:, :, 512:1024]
    lbl = f(inp["hgrn_lb_logits"])
    ws = f(inp["gmlp_w_s"])
    bsv = f(inp["gmlp_b_s"])
    dww = f(inp["conv_dw_w"])
    common = {
        "ada_w": f(inp["ada_w"]),
        "ada_b": _col(f(inp["ada_b"])),
        "n1g": _col(f(inp["norm1_g"])),
        "n2g": _col(f(inp["norm2_g"])),
        "fng": _col(f(inp["final_norm_g"])),
        "ong": _col(f(inp["hgrn_onorm_g"])),
        "gln": np.ascontiguousarray(np.broadcast_to(np.stack([f(inp["gmlp_ln_g"]), f(inp["gmlp_ln_b"])], axis=1)[None], (128, L, 2, 256))),
        "cpar": np.ascontiguousarray(np.stack([_col(f(inp["conv_dw_b"])), _col(f(inp["conv_ln_g"])), _col(f(inp["conv_ln_b"])),
                                               _col(f(inp["conv_pw_b"]))], axis=2)),
        "pww": f(inp["conv_pw_w"]),
        "w_out": f(inp["w_out"]),
        "w13": f(inp["ffn_w13"]),
        "w2": f(inp["ffn_w2"]),
    }
    per_par = []
    for par in range(2):
        m = par == 1
        ws_p = ws[:, :, ::-1, ::-1] if m else ws
        bs_p = bsv[:, :, ::-1] if m else bsv
        dw_p = dww[:, ::-1, :] if m else dww
        lb_p = lbl[:, ::-1, :] if m else lbl
        d = dict(common)
        d["w_in"] = w_in_odd if m else w_in
        d["lbl"] = _col(lb_p)
        d["wsT"] = np.ascontiguousarray(np.transpose(ws_p, (3, 0, 1, 2)))
        bsl = np.zeros((128, L, 2, 128), np.float32)
        for ch in range(2):
            for hh in range(2):
                bsl[hh * 64:(hh + 1) * 64, :, ch, :] = bs_p[:, ch * 2 + hh, :][None]
        d["bs"] = bsl
        d["dww"] = np.ascontiguousarray(np.transpose(dw_p.reshape(L, 31, 2, 128), (3, 0, 2, 1)))
        per_par.append(d)
    ins = []
    for core in range(8):
        bi, par = core // 2, core % 2
        d = dict(per_par[par])
        if par == 0:
            xl = np.concatenate([ctx[bi], x[bi, 0:2048]], axis=0)
        else:
            xl = np.concatenate([ctx[bi, ::-1], x[bi, 4095:2047:-1]], axis=0)
        d["x"] = np.ascontiguousarray(xl)
        d["cc"] = np.ascontiguousarray(np.stack([_col(c[bi]), _col(c_ctx)], axis=-1))
        sel = np.zeros((128, 2), np.float32)
        sel[:, 1 - par] = 1.0
        d["sel"] = sel
        ins.append(d)
    return ins


_NC = None


def kernel(**inputs):
    global _NC
    ins = prep_inputs(inputs)
    if _NC is None:
        _NC = build()
    res = run_bass_kernel_spmd(_NC, ins, core_ids=list(range(8)))
    out = np.zeros((4, 4096, D), np.float32)
    for core in range(8):
        bi, par = core // 2, core % 2
        y = res.results[core]["y"]
        if par == 0:
            out[bi, 0:2048] = y
        else:
            out[bi, 4095:2047:-1] = y
    return out
```

```python
import numpy as np
from contextlib import ExitStack
import concourse.bass as bass
import concourse.mybir as mybir
from concourse.bass_utils import run_bass_kernel_spmd

F32 = mybir.dt.float32
BF16 = mybir.dt.bfloat16
I32 = mybir.dt.int32
AF = mybir.ActivationFunctionType
ALU = mybir.AluOpType

L = 2
D = 1024
T = 2304
NB = 9
BS = 256
DFF = 2816
EPS = 1e-6


class Buf:
    def __init__(self, name, excl=False):
        self.name = name
        self.w = None
        self.r = {}
        self.excl = excl


class Sched:
    ENG = ("pe", "act", "dve", "pool", "sp")

    def __init__(self, nc):
        self.nc = nc
        self.prog = {e: [] for e in self.ENG}
        self.NR = 8
        self.units = list(self.ENG) + [f"d_{q}{r}" for q in ("sp", "pool") for r in range(self.NR)] + ["cc"]
        self.dcount = {"sp": 0, "pool": 0}
        self.cnt = {u: 0 for u in self.units}
        self.mult = {u: (16 if u.startswith("d_") else 1) for u in self.units}
        self.seen = {e: {} for e in self.ENG}
        self.sem = {}

    def _deps(self, e, reads, writes):
        deps = {}
        for b in reads:
            if b.w is not None:
                f, i = b.w
                deps[f] = max(deps.get(f, 0), i)
        for b in writes:
            if b.w is not None:
                f, i = b.w
                deps[f] = max(deps.get(f, 0), i)
            for f, i in b.r.items():
                deps[f] = max(deps.get(f, 0), i)
        for f, i in deps.items():
            if f == e and e == "pe":
                continue
            if self.seen[e].get(f, 0) >= i:
                continue
            self.seen[e][f] = i
            self.prog[e].append(("wait", f, i * self.mult[f]))

    def op(self, e, fn, reads=(), writes=(), inc=True):
        ex = [r for r in reads if r.excl]
        if ex:
            writes = list(writes) + [r for r in ex if r not in writes]
            reads = [r for r in reads if not r.excl]
        self._deps(e, reads, writes)
        idx = self.cnt[e] + 1
        if inc:
            self.cnt[e] = idx
        self.prog[e].append(("inst", fn, e if inc else None, 1))
        for b in reads:
            b.r[e] = max(b.r.get(e, 0), idx)
        for b in writes:
            b.w = (e, idx)
            b.r = {}

    def dma(self, q, fn, reads=(), writes=(), unit=None):
        if unit is None:
            d = f"d_{q}{self.dcount[q] % self.NR}"
            self.dcount[q] += 1
            if self.cnt[d] > 0 and self.seen[q].get(d, 0) < self.cnt[d]:
                self.seen[q][d] = self.cnt[d]
                self.prog[q].append(("wait", d, self.cnt[d] * self.mult[d]))
        else:
            d = unit
        self._deps(q, reads, writes)
        self.cnt[d] += 1
        idx = self.cnt[d]
        self.prog[q].append(("inst", fn, d, self.mult[d]))
        for b in reads:
            b.r[d] = max(b.r.get(d, 0), idx)
        for b in writes:
            b.w = (d, idx)
            b.r = {}

    def wait_all(self, e, bufs):
        self._deps(e, bufs, ())

    def emit(self, block):
        S = self

        def run(e, h):
            for it in S.prog[e]:
                if it[0] == "wait":
                    h.wait_ge(S.sem[it[1]], it[2])
                else:
                    ins = it[1](h)
                    if it[2] is not None:
                        ins.then_inc(S.sem[it[2]], it[3])

        @block.tensor
        def _(h):
            run("pe", h)

        @block.scalar
        def _(h):
            run("act", h)

        @block.vector
        def _(h):
            run("dve", h)

        @block.gpsimd
        def _(h):
            run("pool", h)

        @block.sync
        def _(h):
            run("sp", h)


class _Stop(Exception):
    pass


def build(layers=(0, 1), dbg=None, stop=None, dumps=None):
    nc = bass.Bass("TRN2", target_bir_lowering=False)

    def din(name, shape, dt=F32):
        return nc.dram_tensor(name, list(shape), dt, kind="ExternalInput").ap()

    x_d = din("x", [T, D])
    cc_d = din("cc", [128, 8, 2])
    adaw_d = din("ada_w", [L, D, 6 * D])
    adab_d = din("ada_b", [128, L, 48])
    n1g_d = din("n1g", [128, L, 8])
    n2g_d = din("n2g", [128, L, 8])
    fng_d = din("fng", [128, 8])
    win_d = din("w_in", [L, D, 3584])
    lbl_d = din("lbl", [128, L, 2, 4])
    ong_d = din("ong", [128, L, 4])
    gln_d = din("gln", [128, L, 2, 256])
    wsT_d = din("wsT", [128, L, 4, 128])
    bs_d = din("bs", [128, L, 2, 128])
    dww_d = din("dww", [128, L, 2, 31])
    cpar_d = din("cpar", [128, L, 4, 2])
    pww_d = din("pww", [L, 256, 256])
    wout_d = din("w_out", [L, D, D])
    w13_d = din("w13", [L, D, 2 * DFF])
    w2_d = din("w2", [L, DFF, D])
    sel_d = din("sel", [128, 2])
    y_d = nc.dram_tensor("y", [2048, D], F32, kind="ExternalOutput").ap()
    if dbg:
        dbg_d = nc.dram_tensor("dbg", [128, dbg], F32, kind="ExternalOutput").ap()
    cc_in = [nc.dram_tensor(f"cc_in{l}", [128, 512], BF16, kind="Internal").ap() for l in range(L)]
    cc_out = [nc.dram_tensor(f"cc_out{l}", [256, 512], BF16, kind="Internal", addr_space="Local").ap() for l in range(L)]

    bar_in = [nc.dram_tensor(f"bar_in{l}", [128, 16], BF16, kind="Internal").ap() for l in range(L)]
    bar_out = [nc.dram_tensor(f"bar_out{l}", [256, 16], BF16, kind="Internal", addr_space="Local").ap() for l in range(L)]

    es = ExitStack()
    with es:
        def sb(name, shape, dt):
            return es.enter_context(nc.sbuf_tensor("sb_" + name, list(shape), dt))

        def pst(name, shape, dt):
            return es.enter_context(nc.psum_tensor("ps_" + name, list(shape), dt))

        S = Sched(nc)
        for u in S.units:
            S.sem[u] = es.enter_context(nc.semaphore("s_" + u))

        XT = sb("XT", [128, 8, T], F32)
        mix = sb("mix", [128, 8, T], BF16)
        bufA = sb("bufA", [128, 8, 1536], BF16)
        bufB = sb("bufB", [128, 8, 1536], BF16)
        hTb = sb("hTb", [128, 8, BS], BF16)
        gt = sb("gt", [128, 14, BS], F32)
        nt = gt[:, 0:8, :]
        scn = sb("scn", [128, 9216], BF16)
        qtT = scn[:, 0:1024].rearrange("p (h t) -> p h t", h=4)
        qhT = scn[:, 1024:2048].rearrange("p (h t) -> p h t", h=4)
        ktT = scn[:, 2048:3072].rearrange("p (h t) -> p h t", h=4)
        kt_tok = scn[:, 3072:5120].rearrange("p (a r n) -> p a r n", a=2, r=2)
        v_tok = scn[:, 5120:6144].rearrange("p (a n) -> p a n", a=2)
        dgD = scn[:, 6144:8192].rearrange("p (h c k) -> p h c k", h=4, c=4)
        yaT = scn[:, 8192:9216].rearrange("p (h t) -> p h t", h=4)
        Dw1 = scn[:, 0:3968].rearrange("p (t k) -> p t k", t=31)
        GE = sb("GE", [128, 4, 2, 4], F32)
        Sst = sb("Sst", [128, 4, 128], BF16)
        A_sb = sb("A_sb", [128, 2, 4, 64], BF16)
        ident_f = sb("ident_f", [128, 128], F32)
        ident_b = sb("ident_b", [128, 128], BF16)
        ones_f = sb("ones_f", [128, 128], F32)
        gones = sb("gones", [128, 128], F32)
        maskt = sb("maskt", [128, 2, 2, 64], I32)
        rmask = sb("rmask", [128, 256], F32)
        onesw = gt[:, 0, 0:64]
        cc_s = sb("cc_s", [128, 8, 2], F32)
        sil = sb("sil", [128, 8, 2], BF16)
        PAR = sb("PAR", [128, L, 2, 6, 8], F32)
        adab = sb("adab", [128, L, 48], F32)
        n1g = sb("n1g", [128, L, 8], F32)
        n2g = sb("n2g", [128, L, 8], F32)
        fng = sb("fng", [128, 8], F32)
        lbl = sb("lbl", [128, L, 2, 4], F32)
        LB = sb("LB", [128, L, 2, 4, 3], F32)
        ong = sb("ong", [128, L, 4], F32)
        gln = sb("gln", [128, 1, 2, 256], F32)
        wsT = sb("wsT", [128, 1, 4, 128], BF16)
        bs_s = sb("bs_s", [128, 1, 2, 128], F32)
        dww = sb("dww", [128, L, 2, 31], F32)
        cpar = sb("cpar", [128, L, 4, 2], F32)
        pww = sb("pww", [128, 1, 2, 256], BF16)
        selt = sb("selt", [128, 2], F32)
        gath = sb("gath", [128, 2, 512], BF16)
        stg = gt[:, 0:4, :].rearrange("p (a b) t -> p a (b t)", a=2)
        small = sb("small", [128, 64], F32)
        if dbg:
            dstg = sb("dstg", [128, 1024], F32)
        osb = mix[:, 0, 0:2048].bitcast(F32)

        pA = pst("pA", [128, 512], F32)
        pB = pst("pB", [128, 512], F32)
        pV = pst("pV", [128, 512], F32)
        pT = pst("pT", [128, 1024], BF16)
        pSW = pst("pSW", [128, 1024], F32)
        pO = pst("pO", [128, 1024], F32)
        pScs = [pSW[:, 0:256].rearrange("p (h t) -> p h t", h=4), pSW[:, 256:512].rearrange("p (h t) -> p h t", h=4)]
        pW = pSW[:, 512:1024].rearrange("p (h v) -> p h v", h=4)
        pOv = pO[:, :].rearrange("p (h t) -> p h t", h=4)

        block = es.enter_context(nc.Block())

        bXT = [Buf(f"XT{i}") for i in range(NB)]
        bmix = [Buf(f"mix{i}") for i in range(NB)]
        b = {n: Buf(n) for n in ("bufA", "bufB", "Dw", "hTb", "nt", "qtT", "qhT", "ktT", "kt_tok", "v_tok", "GE",
                                 "A0", "A1", "yaT", "const", "par", "PAR", "LB", "pA", "pB", "pV", "pT", "pS0", "pS1",
                                 "pO", "gath", "stg", "small", "osb", "y", "sil", "ccin", "ccout", "dbg", "Sfin", "Dw0", "lpar")}
        bgt = [Buf(f"gt{i}") for i in range(14)]
        bS = [Buf(f"S{h}") for h in range(4)]
        bpW = [Buf(f"pW{h}") for h in range(4)]
        bdg = [Buf(f"dg{h}") for h in range(4)]
        bosb = [Buf("osb0"), Buf("osb1")]
        NTB = bgt[0:8]
        STB = bgt[0:4]
        bA = [b["A0"], b["A1"]]
        for nm in ("pA", "pB", "pV", "pT", "pS0", "pO"):
            b[nm].excl = True
        b["pS1"] = b["pS0"]
        b["pSW1"] = Buf("pSW1", excl=True)
        bpS = [b["pS0"], b["pS0"]]
        Wreg = [(pA[:, 0:128], b["pA"]), (pB[:, 0:128], b["pB"]), (pV[:, 0:128], b["pV"]), (pSW[:, 512:640], b["pSW1"])]
        bpW = [w[1] for w in Wreg]

        def MM(out, lhsT, rhs, start, stop, reads, writes):
            S.op("pe", lambda h: h.matmul(out, lhsT, rhs, start=start, stop=stop), reads, writes, inc=stop)

        def TR(out, in_, ident, reads, writes):
            S.op("pe", lambda h: h.transpose(out, in_, ident), reads, writes, inc=True)

        def ACT(out, in_, func, reads, writes, bias=None, scale=None):
            kw = {}
            if bias is not None:
                kw["bias"] = bias
            if scale is not None:
                kw["scale"] = scale
            S.op("act", lambda h: h.activation(out=out, in_=in_, func=func, **kw), reads, writes)

        def TT(e, out, in0, in1, op, reads, writes):
            S.op(e, lambda h: h.tensor_tensor(out=out, in0=in0, in1=in1, op=op), reads, writes)

        def TS(e, out, in0, s1, s2, op0, op1, reads, writes):
            if op1 is None:
                S.op(e, lambda h: h.tensor_scalar(out=out, in0=in0, scalar1=s1, scalar2=None, op0=op0), reads, writes)
            else:
                S.op(e, lambda h: h.tensor_scalar(out=out, in0=in0, scalar1=s1, scalar2=s2, op0=op0, op1=op1), reads, writes)

        def STT(out, in0, scalar, in1, op0, op1, reads, writes):
            S.op("dve", lambda h: h.scalar_tensor_tensor(out=out, in0=in0, scalar=scalar, in1=in1, op0=op0, op1=op1), reads, writes)

        def CP(e, out, in_, reads, writes):
            if e == "act":
                S.op(e, lambda h: h.activation(out=out, in_=in_, func=AF.Identity), reads, writes)
            else:
                S.op(e, lambda h: h.tensor_copy(out=out, in_=in_), reads, writes)

        def DMA(q, out, in_, reads, writes):
            S.dma(q, lambda h: h.dma_start(out=out, in_=in_), reads, writes)

        def stage(k):
            if stop is not None and k > stop:
                raise _Stop()

        try:
            cst = [b["const"]]
            S.op("pool", lambda h: h.memset(ones_f[:], 1.0), (), cst)
            S.op("pool", lambda h: h.memset(onesw, 1.0), (), cst + [bgt[0]])
            S.op("pool", lambda h: h.memset(ident_f[:], 0.0), (), cst)
            S.op("pool", lambda h: h.affine_select(out=ident_f[:], in_=ident_f[:], pattern=[[1, 128]], compare_op=ALU.not_equal,
                                                    fill=1.0, base=0, channel_multiplier=-1), cst, cst)
            S.op("pool", lambda h: h.tensor_copy(out=ident_b[:], in_=ident_f[:]), cst, cst)
            S.op("pool", lambda h: h.memset(gones[:], 0.0), (), cst)
            S.op("pool", lambda h: h.memset(gones[0:64, 0:64], 1.0), cst, cst)
            S.op("pool", lambda h: h.memset(gones[64:128, 64:128], 1.0), cst, cst)
            S.op("pool", lambda h: h.memset(rmask[:], 1.0), (), cst)
            S.op("pool", lambda h: h.memset(rmask[:].rearrange("p (c t) -> p c t", t=64)[:, :, 0:1], 0.0), cst, cst)
            S.op("pool", lambda h: h.memset(A_sb[:], 0.0), (), bA)
            S.op("pool", lambda h: h.memset(maskt[:], 0.0), (), cst)
            for half in (0, 1):
                ps_ = slice(half * 64, half * 64 + 64)
                S.op("pool", lambda h, ps_=ps_, half=half: h.affine_select(out=maskt[ps_, 0, half, :], in_=onesw[ps_, :],
                                                                 pattern=[[1, 64]], compare_op=ALU.is_ge, fill=0.0, base=0,
                                                                 channel_multiplier=-1), cst + [bgt[0]], cst)
                S.op("pool", lambda h, ps_=ps_, half=half: h.affine_select(out=maskt[ps_, 1, half, :], in_=onesw[ps_, :],
                                                                 pattern=[[-1, 64]], compare_op=ALU.is_ge, fill=0.0, base=0,
                                                                 channel_multiplier=1), cst + [bgt[0]], cst)

            par = [b["par"], b["lpar"]]
            for dst, src in ((cc_s, cc_d), (adab, adab_d), (n1g, n1g_d), (n2g, n2g_d), (fng, fng_d), (lbl, lbl_d), (ong, ong_d),
                             (dww, dww_d), (cpar, cpar_d), (selt, sel_d)):
                DMA("sp", dst[:], src, (), par)
            S.op("dve", lambda h: h.memset(LB[:], 0.0), (), [b["LB"]])
            TT("dve", small[:, 0:8], lbl[:, 1].rearrange("p a b -> p (a b)"), lbl[:, 0].rearrange("p a b -> p (a b)"), ALU.subtract, par, [b["small"]])
            ACT(LB[:, 1, :, :, 0], small[:, 0:8].rearrange("p (a b) -> p a b", a=2), AF.Sigmoid, [b["small"]], [b["LB"]])
            for l in range(L):
                TS("dve", LB[:, l, :, :, 1], LB[:, l, :, :, 0], -1.0, 1.0, ALU.mult, ALU.add, [b["LB"]], [b["LB"]])
                TS("dve", LB[:, l, :, :, 2], LB[:, l, :, :, 0], 1.0, -1.0, ALU.mult, ALU.add, [b["LB"]], [b["LB"]])
            ACT(sil[:], cc_s[:], AF.Silu, par, [b["sil"]])

            pM = pA[:, 0:96].rearrange("p (n j) -> p n j", j=2)
            ada_slot = [bufA[:, :, 0:512], bufA[:, :, 512:1024], bufA[:, :, 1024:1536], bufB[:, :, 0:512], bufB[:, :, 512:1024], bufB[:, :, 1024:1536]]
            bslot = [Buf(f"adaslot{i}") for i in range(6)]
            si = 0
            for l in layers:
                for ng in range(12):
                    sl, bsl = ada_slot[si % 6], bslot[si % 6]
                    si += 1
                    DMA("pool", sl, adaw_d[l, :, ng * 512:(ng + 1) * 512].rearrange("(kc p) n -> p kc n", p=128), (), [bsl])
                    for nn in range(4):
                        for kc in range(8):
                            MM(pM[:, ng * 4 + nn, :], sl[:, kc, nn * 128:(nn + 1) * 128], sil[:, kc, :], kc == 0, kc == 7,
                               [bsl, b["sil"]], [b["pA"]])
                for j in range(2):
                    TT("dve", PAR[:, l, j].rearrange("p m k -> p (m k)"), pM[:, :, j], adab[:, l, :], ALU.add, [b["pA"], b["par"]], [b["PAR"]])
                    STT(PAR[:, l, j, 1, :], PAR[:, l, j, 1, :], 1.0, n1g[:, l, :], ALU.add, ALU.mult, [b["PAR"], b["par"]], [b["PAR"]])
                    STT(PAR[:, l, j, 4, :], PAR[:, l, j, 4, :], 1.0, n2g[:, l, :], ALU.add, ALU.mult, [b["PAR"], b["par"]], [b["PAR"]])
            relA = bslot[0:3]
            relB = bslot[3:6]

            stage(1)
            for tt in range(T // 128):
                st_ = gt[:, 4 * (tt % 2):4 * (tt % 2) + 4, :].rearrange("p a t -> p (a t)")
                STX = bgt[4 * (tt % 2):4 * (tt % 2) + 4]
                DMA("sp", st_, x_d[tt * 128:(tt + 1) * 128, :], (), STX)
                for half in range(2):
                    pb, bb = (pA, b["pA"]) if half == 0 else (pB, b["pB"])
                    for k4 in range(4):
                        kc = half * 4 + k4
                        TR(pb[:, k4 * 128:(k4 + 1) * 128], st_[:, kc * 128:(kc + 1) * 128], ident_f[:], STX + [b["const"]], [bb])
                    CP("act" if half == 0 else "dve", XT[:, half * 4:half * 4 + 4, tt * 128:(tt + 1) * 128],
                       pb[:, :].rearrange("p (k t) -> p k t", k=4), [bb], [bXT[tt // 2]])

            def rsqrt_psum(ps_ap, n, scale, reads_b, out_ap, out_b, tmp_ap, tmp_b):
                ACT(tmp_ap, ps_ap, AF.Ln, reads_b, tmp_b, bias=eps_col, scale=scale)
                ACT(out_ap, tmp_ap, AF.Exp, tmp_b, out_b, scale=-0.5)

            eps_t = small[:, 32:33]
            S.op("dve", lambda h: h.memset(eps_t, EPS), (), [b["const"]])
            eps_col = eps_t

            def norm_block(l, which, blk, out_ap, out_b, final=False):
                j = 1 if blk == 0 else 0
                t0 = blk * BS
                xs = XT[:, :, t0:t0 + BS]
                ACT(nt[:], xs, AF.Square, [bXT[blk]], NTB)
                for kc in range(8):
                    MM(pV[:, 0:BS], ones_f[:], nt[:, kc, :], kc == 0, kc == 7, NTB + [b["const"]], [b["pV"]])
                rsqrt_psum(pV[:, 0:BS], BS, 1.0 / D, [b["pV"], b["const"]], pV[:, 256:512], [b["pV"]], gt[:, 9, :], [bgt[9]])
                TT("dve", nt[:], xs, pV[:, 256:512].unsqueeze(1).broadcast_to([128, 8, BS]), ALU.mult, [bXT[blk], b["pV"]], NTB)
                for kc in range(8):
                    if final:
                        ACT(out_ap[:, kc, :], nt[:, kc, :], AF.Identity, NTB + [b["par"], b["lpar"]], out_b, scale=fng[:, kc:kc + 1])
                    else:
                        m = 0 if which == 1 else 3
                        ACT(out_ap[:, kc, :], nt[:, kc, :], AF.Identity, NTB + [b["PAR"]], out_b,
                            bias=PAR[:, l, j, m, kc:kc + 1], scale=PAR[:, l, j, m + 1, kc:kc + 1])

            def load_w(dst, l, c0, c1, src, wb):
                DMA("pool", dst, src[l, :, c0:c1].rearrange("(kc p) n -> p kc n", p=128), (), wb)

            def dump(ap, n, reads):
                DMA("sp", dbg_d[:, 0:n], ap, reads, [b["dbg"]])

            stage(2)
            for l in layers:
                last = (l == L - 1)
                WA = [b["bufA"]] + relA
                WB = [b["bufB"], b["Dw0"]] + relB
                load_w(bufA[:, :, 0:1024], l, 0, 1024, win_d, WA)
                load_w(bufA[:, :, 1024:1536], l, 1536, 2048, win_d, WA)
                relA = []
                lp = [b["lpar"]]
                DMA("sp", gln[:, 0], gln_d[:, l], (), lp)
                DMA("sp", bs_s[:, 0], bs_d[:, l], (), lp)
                DMA("pool", wsT[:, 0], wsT_d[:, l], (), lp)
                DMA("pool", pww[:, 0], pww_d[l].rearrange("(c p) n -> p c n", p=128), (), lp)
                load_w(bufB[:, :, 0:1024], l, 2560, 3584, win_d, WB)
                relB = []
                dwv = bufB[:, :, 1024:1536].rearrange("p k (j c) -> p k j c", j=4)
                TT("dve", dwv[:, 0:7], ident_b[:].unsqueeze(1).unsqueeze(1).broadcast_to([128, 7, 4, 128]),
                   dww[:, l, 0, 0:28].rearrange("p (k j) -> p k j", j=4).unsqueeze(3).broadcast_to([128, 7, 4, 128]), ALU.mult,
                   [b["par"], b["const"]], [b["Dw0"]])
                TT("dve", dwv[:, 7, 0:3], ident_b[:].unsqueeze(1).broadcast_to([128, 3, 128]),
                   dww[:, l, 0, 28:31].unsqueeze(2).broadcast_to([128, 3, 128]), ALU.mult, [b["par"], b["const"]], [b["Dw0"]])

                def hgrn_dir(dr, blocks, Wt, wb):
                    mid = 31 if dr == 0 else 32
                    eidx = 63 if dr == 0 else 0
                    S.op("pool", lambda hh: hh.memset(A_sb[:], 0.0), (), bA)
                    S.op("pool", lambda hh: hh.memset(kt_tok[:], 0.0), (), [b["kt_tok"]])
                    for blk in blocks:
                        j = 1 if blk == 0 else 0
                        t0 = blk * BS
                        first = (l == layers[0] and dr == 0 and blk == blocks[0])
                        if first:
                            stage(3.1)
                        norm_block(l, 1, blk, hTb, [b["hTb"]])
                        if first:
                            stage(3.2)
                        if True:
                            for tl in range(2):
                                for kc in range(8):
                                    MM(pV[:, :], hTb[:, kc, tl * 128:(tl + 1) * 128], Wt[:, kc, 1024:1536], kc == 0, kc == 7,
                                       [b["hTb"]] + wb, [b["pV"]])
                                CP("act", v_tok[:, tl, :], pV[:, :], [b["pV"]], [b["v_tok"]])
                        for hp in range(2):
                            hs2 = (2 * hp, 2 * hp + 1)
                            ctxs = []
                            for h in hs2:
                                pb, bb = (pA, b["pA"]) if h % 2 == 0 else (pB, b["pB"])
                                for kc in range(8):
                                    MM(pb[:, 0:256], Wt[:, kc, h * 128:(h + 1) * 128], hTb[:, kc, :], kc == 0, kc == 7, [b["hTb"]] + wb, [bb])
                                for kc in range(8):
                                    MM(pb[:, 256:512], Wt[:, kc, 512 + h * 128:512 + (h + 1) * 128], hTb[:, kc, :], kc == 0, kc == 7, [b["hTb"]] + wb, [bb])
                                g0 = 7 * (h % 2)
                                G = [bgt[g0 + i] for i in range(7)]
                                sl7 = [gt[:, g0 + i, :] for i in range(7)]
                                ctxs.append((h, pb, bb, G, sl7))
                            for (h, pb, bb, G, (sg, lf, bb_, dd, e1, e2, e3)) in ctxs:
                                ACT(sg, pb[:, 256:512], AF.Sigmoid, [bb], [G[0]])
                            for (h, pb, bb, G, (sg, lf, bb_, dd, e1, e2, e3)) in ctxs:
                                ACT(lf, sg, AF.Ln, [G[0], b["LB"]], [G[1]], bias=LB[:, l, dr, h, 0:1], scale=LB[:, l, dr, h, 1:2])
                            for (h, pb, bb, G, (sg, lf, bb_, dd, e1, e2, e3)) in ctxs:
                                S.op("dve", lambda hh, bb_=bb_, lf=lf: hh.tensor_tensor_scan(out=bb_, data0=rmask[:], data1=lf,
                                                                                          initial=0.0, op0=ALU.mult, op1=ALU.add),
                                     [G[1], b["const"]], [G[2]])
                            for (h, pb, bb, G, (sg, lf, bb_, dd, e1, e2, e3)) in ctxs:
                                b3 = bb_.rearrange("p (c t) -> p c t", t=64)
                                d3 = dd.rearrange("p (c t) -> p c t", t=64)
                                if dr == 0:
                                    TT("pool", d3, b3, b3[:, :, mid:mid + 1].broadcast_to([128, 4, 64]), ALU.subtract, [G[2]], [G[3]])
                                else:
                                    tmp = e1
                                    TT("pool", tmp, lf, bb_, ALU.subtract, [G[1], G[2]], [G[4]])
                                    TT("pool", d3, tmp.rearrange("p (c t) -> p c t", t=64), b3[:, :, 63:64].broadcast_to([128, 4, 64]), ALU.add,
                                       [G[4], G[2]], [G[3]])
                                    TT("pool", b3, d3, d3[:, :, mid:mid + 1].broadcast_to([128, 4, 64]), ALU.subtract, [G[3]], [G[2]])
                            for (h, pb, bb, G, (sg, lf, bb_, dd, e1, e2, e3)) in ctxs:
                                dsrc, dB, bsrc, bB = (dd, G[3], bb_, G[2]) if dr == 0 else (bb_, G[2], dd, G[3])
                                ACT(e1, dsrc, AF.Exp, [dB], [G[4]])
                                ACT(e2, dsrc, AF.Exp, [dB], [G[5]], scale=-1.0)
                                ACT(e3, bsrc, AF.Exp, [bB], [G[6]])
                            for (h, pb, bb, G, (sg, lf, bb_, dd, e1, e2, e3)) in ctxs:
                                kk = lf
                                TS("dve", kk, sg, LB[:, l, dr, h, 2:3], LB[:, l, dr, h, 1:2], ALU.mult, ALU.add, [G[0], b["LB"]], [G[1]])
                                TT("dve", ktT[:, h, :], kk, e2, ALU.mult, [G[1], G[5]], [b["ktT"]])
                                TT("dve", qtT[:, h, :], pb[:, 0:256], e1, ALU.mult, [bb, G[4]], [b["qtT"]])
                                TT("dve", qhT[:, h, :], pb[:, 0:256], e3, ALU.mult, [bb, G[6]], [b["qhT"]])
                                CP("dve", GE[:, h, 0, :], e3.rearrange("p (c t) -> p c t", t=64)[:, :, mid], [G[6]], [b["GE"]])
                                CP("dve", GE[:, h, 1, :], e1.rearrange("p (c t) -> p c t", t=64)[:, :, eidx], [G[4]], [b["GE"]])
                            for (h, pb, bb, G, sl7) in ctxs:
                                TT("dve", dgD[:, h, :, :], ident_b[:].unsqueeze(1).broadcast_to([128, 4, 128]),
                                   GE[:, h, 0, :].unsqueeze(2).broadcast_to([128, 4, 128]), ALU.mult, [b["GE"], b["const"]], [bdg[h]])
                        if first:
                            stage(3.3)
                        for tl in range(2):
                            for h in range(4):
                                TR(pT[:, h * 128:(h + 1) * 128], ktT[:, h, tl * 128:(tl + 1) * 128], ident_b[:], [b["ktT"], b["const"]], [b["pT"]])
                            CP("act", kt_tok[0:64, tl, 0, :], pT[0:64, 0:512], [b["pT"]], [b["kt_tok"]])
                            CP("act", kt_tok[64:128, tl, 1, :], pT[64:128, 0:512], [b["pT"]], [b["kt_tok"]])
                        if first:
                            stage(3.4)
                        corder = range(4) if dr == 0 else range(3, -1, -1)
                        for c in corder:
                            tl, par = c // 2, c % 2
                            cs = slice(c * 64, c * 64 + 64)
                            pSc = pScs[par]
                            for h in range(4):
                                MM(pSc[:, h, :], ktT[:, h, tl * 128:(tl + 1) * 128], qtT[:, h, cs], True, True, [b["ktT"], b["qtT"]], [bpS[par]])
                            S.op("dve", lambda hh, par=par, pSc=pSc: hh.copy_predicated(out=A_sb[:, par, :, :],
                                                                                   mask=maskt[:, dr, par, :].unsqueeze(1).broadcast_to([128, 4, 64]),
                                                                                   data=pSc[:, :, :]),
                                 [bpS[par], b["const"]], [bA[par]])
                            if first and c == 0:
                                stage(3.5)
                            for h in range(4):
                                hs = slice(h * 128, h * 128 + 128)
                                MM(pOv[:, h, cs], v_tok[:, tl, hs], A_sb[:, par, h, :], True, False, [b["v_tok"], bA[par]], [b["pO"]])
                                MM(pOv[:, h, cs], Sst[:, h, :], qhT[:, h, cs], False, True, [bS[h], b["qhT"]], [b["pO"]])
                                MM(Wreg[h][0], dgD[:, h, c, :], Sst[:, h, :], True, False, [bdg[h], bS[h]], [bpW[h]])
                                MM(Wreg[h][0], kt_tok[:, tl, par, hs], v_tok[:, tl, hs], False, True, [b["kt_tok"], b["v_tok"]], [bpW[h]])
                                ACT(Sst[:, h, :], Wreg[h][0], AF.Identity, [bpW[h], b["GE"]], [bS[h]], scale=GE[:, h, 1, c:c + 1])
                        if first:
                            stage(3.6)
                        if dr == 0:
                            CP("act", mix[:, 0:4, t0:t0 + BS], pOv, [b["pO"]], [bmix[blk]])
                            if first:
                                stage(3.7)
                        else:
                            osum = gt[:, 0:4, :]
                            TT("dve", osum, pOv, mix[:, 0:4, t0:t0 + BS], ALU.add, [b["pO"], bmix[blk]], bgt[0:4])
                            osq = gt[:, 4:8, :]
                            ACT(osq, osum, AF.Square, bgt[0:4], bgt[4:8])
                            for hp in range(2):
                                MM(pO[:, hp * 512:(hp + 1) * 512], ones_f[:], osq[:, 2 * hp:2 * hp + 2, :].rearrange("p a t -> p (a t)"), True, True,
                                   bgt[4:8] + [b["const"]], [b["pO"]])
                            ACT(osq.rearrange("p a t -> p (a t)"), pO[:, :], AF.Ln, [b["pO"], b["const"]], bgt[4:8], bias=eps_col, scale=1.0 / 128)
                            ACT(osq.rearrange("p a t -> p (a t)"), osq.rearrange("p a t -> p (a t)"), AF.Exp, bgt[4:8], bgt[4:8], scale=-0.5)
                            TT("dve", osum, osum, osq, ALU.mult, bgt[0:4] + bgt[4:8], bgt[0:4])
                            for h in range(4):
                                pb, bb = (pA, b["pA"]) if h % 2 == 0 else (pB, b["pB"])
                                for kc in range(8):
                                    MM(pb[:, 0:256], bufB[:, kc, 1024 + h * 128:1024 + (h + 1) * 128], hTb[:, kc, :], kc == 0, kc == 7, [b["hTb"]] + WBcur, [bb])
                                ACT(gt[:, 8, :], pb[:, 0:256], AF.Silu, [bb], [bgt[8]])
                                STT(yaT[:, h, :], osum[:, h, :], ong[:, l, h:h + 1], gt[:, 8, :], ALU.mult, ALU.mult, bgt[0:4] + [bgt[8], b["par"]], [b["yaT"]])
                            for ncn in range(8):
                                pb, bb = (pA, b["pA"]) if ncn % 2 == 0 else (pB, b["pB"])
                                for kk_ in range(8):
                                    rhs = yaT[:, kk_, :] if kk_ < 4 else mix[:, kk_, t0:t0 + BS]
                                    MM(pb[:, 0:256], bufB[:, kk_, ncn * 128:(ncn + 1) * 128], rhs, kk_ == 0, kk_ == 7,
                                       [b["yaT"], bmix[blk]] + WBcur, [bb])
                                STT(XT[:, ncn, t0:t0 + BS], pb[:, 0:256], PAR[:, l, j, 2, ncn:ncn + 1], XT[:, ncn, t0:t0 + BS], ALU.mult, ALU.add,
                                    [bb, b["PAR"], bXT[blk]], [bXT[blk]])

                stage(3 + 10 * l)
                for h in range(4):
                    S.op("pool", lambda hh, h=h: hh.memset(Sst[:, h, :], 0.0), (), [bS[h]])
                hgrn_dir(0, list(range(NB)), bufA, [b["bufA"]])
                stage(4 + 10 * l)
                DMA("sp", cc_in[l], Sst[:].rearrange("p h v -> p (h v)"), bS, [b["ccin"]])
                S.dma("pool", lambda hh, l=l: hh.collective_compute("AllGather", ALU.bypass, replica_groups=[[0, 1], [2, 3], [4, 5], [6, 7]],
                                                                    ins=[cc_in[l]], outs=[cc_out[l]]), [b["ccin"]], [b["ccout"]], unit="cc")
                S.dma("pool", lambda hh, l=l: hh.collective_compute("AllGather", ALU.bypass, replica_groups=[[0, 1], [2, 3], [4, 5], [6, 7]],
                                                                    ins=[bar_in[l]], outs=[bar_out[l]]), [b["ccout"]], [b["ccout"]], unit="cc")

                stage(5 + 10 * l)
                load_w(bufA[:, :, 512:1024], l, 1024, 1536, win_d, [b["bufA"]])
                DW1B = [b["qtT"], b["qhT"], b["ktT"], b["kt_tok"]]
                TT("dve", Dw1[:, :, :], ident_b[:].unsqueeze(1).broadcast_to([128, 31, 128]),
                   dww[:, l, 1, :].unsqueeze(2).broadcast_to([128, 31, 128]), ALU.mult, [b["par"], b["const"]], DW1B)
                bc_blocks = list(range(1 if last else 0, NB))
                zp = gt[:, 0:3, :].rearrange("p a t -> p (a t)").bitcast(BF16) if False else None
                for blk in bc_blocks:
                    j = 1 if blk == 0 else 0
                    t0 = blk * BS
                    norm_block(l, 1, blk, hTb, [b["hTb"]])
                    wbB = [b["bufB"]]
                    for ch in range(2):
                        for kc in range(8):
                            MM(pA[:, ch * 256:(ch + 1) * 256], bufB[:, kc, ch * 128:(ch + 1) * 128], hTb[:, kc, :], kc == 0, kc == 7, [b["hTb"]] + wbB, [b["pA"]])
                    gu = gt[:, 0:2, :]
                    ACT(gu, pA[:, :].rearrange("p (c t) -> p c t", c=2), AF.Gelu_apprx_tanh, [b["pA"]], bgt[0:2])
                    for tl in range(2):
                        for kc in range(8):
                            MM(pV[:, 0:256], hTb[:, kc, tl * 128:(tl + 1) * 128], bufB[:, kc, 256:512], kc == 0, kc == 7, [b["hTb"]] + wbB, [b["pV"]])
                        gv = gt[:, 2, :]
                        ACT(gv, pV[:, 0:256], AF.Gelu_apprx_tanh, [b["pV"]], [bgt[2]])
                        st6 = small[:, 0:24].rearrange("p (g s) -> p g s", g=4)
                        mv = small[:, 24:32].rearrange("p (g s) -> p g s", g=4)
                        for g in range(4):
                            S.op("dve", lambda hh, g=g: hh.bn_stats(out=st6[:, g, :], in_=gv[:, g * 64:(g + 1) * 64]), [bgt[2]], [b["small"]])
                            S.op("dve", lambda hh, g=g: hh.bn_aggr(out=mv[:, g, :], in_=st6[:, g, :]), [b["small"]], [b["small"]])
                        rs4 = small[:, 40:44]
                        ACT(small[:, 36:40], mv[:, :, 1], AF.Ln, [b["small"], b["const"]], [b["small"]], bias=eps_col, scale=1.0)
                        ACT(rs4, small[:, 36:40], AF.Exp, [b["small"]], [b["small"]], scale=-0.5)
                        vn = gt[:, 3, :]
                        for g in range(4):
                            TS("dve", vn[:, g * 64:(g + 1) * 64], gv[:, g * 64:(g + 1) * 64], mv[:, g, 0:1], rs4[:, g:g + 1], ALU.subtract, ALU.mult,
                               [bgt[2], b["small"]], [bgt[3]])
                        TT("dve", vn, vn, gln[:, 0, 0, :], ALU.mult, [bgt[3], b["par"]], [bgt[3]])
                        vln = gt[:, 4, :].bitcast(BF16)[:, 0:256]
                        TT("dve", vln, vn, gln[:, 0, 1, :], ALU.add, [bgt[3], b["par"]], [bgt[4]])
                        for g in range(4):
                            po = (g % 2) * 64
                            MM(pB[po:po + 64, (g // 2) * 256 + tl * 128:(g // 2) * 256 + (tl + 1) * 128], vln[:, g * 64:(g + 1) * 64], wsT[:, 0, g, :],
                               True, True, [bgt[4], b["par"]], [b["pB"]])
                    pB3 = pB[:, :].rearrange("p (c t) -> p c t", c=2)
                    mx = gt[:, 5:7, :]
                    for tl in range(2):
                        TT("dve", mx[:, :, tl * 128:(tl + 1) * 128], pB3[:, :, tl * 128:(tl + 1) * 128], bs_s[:, 0, :, :], ALU.add, [b["pB"], b["par"]], bgt[5:7])
                    TT("dve", mix[:, 4:6, t0:t0 + BS], mx, gu, ALU.mult, bgt[5:7] + bgt[0:2], [bmix[blk]])
                    nseg = 1 if blk == 0 else 4
                    sl_ = BS // nseg
                    pTf = pT[:, :].bitcast(F32)
                    CH = []
                    for ch in range(2):
                        sl = [0, 1, 2, 3, 4, 5] if ch == 0 else [9, 10, 11, 12, 13, 6]
                        zpc = gt[:, 7 + ch, :].bitcast(BF16)
                        CH.append(dict(ch=ch, pin=(pA, b["pA"]) if ch == 0 else (pB, b["pB"]), pcv=(pV, b["pV"]) if ch == 0 else (pTf, b["pT"]),
                                       sgm=gt[:, sl[0], :], cz=gt[:, sl[1], :], var=gt[:, sl[2], :], mean=gt[:, sl[3], :], msq=gt[:, sl[4], :],
                                       czs=gt[:, sl[5], :].bitcast(BF16)[:, 0:256], B=[bgt[i] for i in sl], zp=zpc, zb=bgt[7 + ch],
                                       zpv=zpc[:, 0:nseg * (sl_ + 30)].rearrange("p (s t) -> p s t", s=nseg)))
                    for c_ in CH:
                        ch = c_["ch"]; pin, pinb = c_["pin"]
                        for kc in range(8):
                            MM(pin[:, 0:256], bufB[:, kc, 512 + ch * 128:512 + (ch + 1) * 128], hTb[:, kc, :], kc == 0, kc == 7, [b["hTb"]] + wbB, [pinb])
                        for kc in range(8):
                            MM(pin[:, 256:512], bufB[:, kc, 768 + ch * 128:768 + (ch + 1) * 128], hTb[:, kc, :], kc == 0, kc == 7, [b["hTb"]] + wbB, [pinb])
                    for c_ in CH:
                        ACT(c_["sgm"], c_["pin"][0][:, 256:512], AF.Sigmoid, [c_["pin"][1]], [c_["B"][0]])
                    for c_ in CH:
                        S.op("pool", lambda hh, zp=c_["zp"]: hh.memset(zp[:, 0:512], 0.0), (), [c_["zb"]])
                    for c_ in CH:
                        TT("dve", c_["zpv"][:, :, 15:15 + sl_], c_["pin"][0][:, 0:256].rearrange("p (s t) -> p s t", s=nseg),
                           c_["sgm"].rearrange("p (s t) -> p s t", s=nseg), ALU.mult, [c_["pin"][1], c_["B"][0]], [c_["zb"]])
                    for c_ in CH:
                        ch = c_["ch"]; pcv, pcvb = c_["pcv"]
                        for tp in range(31):
                            dwt = bufB[:, tp // 4, 1024 + (tp % 4) * 128:1024 + (tp % 4 + 1) * 128] if ch == 0 else Dw1[:, tp, :]
                            MM(pcv[:, 0:256].rearrange("p (s t) -> p s t", s=nseg), dwt, c_["zpv"][:, :, tp:tp + sl_], tp == 0, tp == 30,
                               [c_["zb"]] + ([b["Dw0"]] if ch == 0 else DW1B), [pcvb])
                    for c_ in CH:
                        ACT(c_["cz"], c_["pcv"][0][:, 0:256], AF.Identity, [c_["pcv"][1], b["par"]], [c_["B"][1]], bias=cpar[:, l, 0, c_["ch"]:c_["ch"] + 1])
                    for c_ in CH:
                        ACT(c_["var"], c_["cz"], AF.Square, [c_["B"][1]], [c_["B"][2]])
                    for c_ in CH:
                        pcv, pcvb = c_["pcv"]
                        MM(pcv[:, 0:256], gones[:], c_["cz"], True, True, [c_["B"][1], b["const"]], [pcvb])
                        MM(pcv[:, 256:512], gones[:], c_["var"], True, True, [c_["B"][2], b["const"]], [pcvb])
                    for c_ in CH:
                        ACT(c_["mean"], c_["pcv"][0][:, 0:256], AF.Identity, [c_["pcv"][1]], [c_["B"][3]], scale=1.0 / 64)
                    for c_ in CH:
                        ACT(c_["msq"], c_["mean"], AF.Square, [c_["B"][3]], [c_["B"][4]])
                    for c_ in CH:
                        STT(c_["var"], c_["pcv"][0][:, 256:512], 1.0 / 64, c_["msq"], ALU.mult, ALU.subtract, [c_["pcv"][1], c_["B"][4]], [c_["B"][2]])
                    for c_ in CH:
                        ACT(c_["var"], c_["var"], AF.Ln, [c_["B"][2], b["const"]], [c_["B"][2]], bias=eps_col, scale=1.0)
                    for c_ in CH:
                        ACT(c_["var"], c_["var"], AF.Exp, [c_["B"][2]], [c_["B"][2]], scale=-0.5)
                    for c_ in CH:
                        TT("dve", c_["cz"], c_["cz"], c_["mean"], ALU.subtract, [c_["B"][1], c_["B"][3]], [c_["B"][1]])
                        TT("dve", c_["cz"], c_["cz"], c_["var"], ALU.mult, [c_["B"][1], c_["B"][2]], [c_["B"][1]])
                    for c_ in CH:
                        ch = c_["ch"]
                        ACT(c_["czs"], c_["cz"], AF.Silu, [c_["B"][1], b["par"]], [c_["B"][5]], bias=cpar[:, l, 2, ch:ch + 1], scale=cpar[:, l, 1, ch:ch + 1])
                    for co in range(2):
                        for ci in range(2):
                            MM(pB[:, co * 256:(co + 1) * 256], pww[:, 0, ci, co * 128:(co + 1) * 128], CH[ci]["czs"],
                               ci == 0, ci == 1, [CH[0]["B"][5], CH[1]["B"][5], b["par"], b["lpar"]], [b["pB"]])
                        ACT(mix[:, 6 + co, t0:t0 + BS], pB[:, co * 256:(co + 1) * 256], AF.Identity, [b["pB"], b["par"]], [bmix[blk]],
                            bias=cpar[:, l, 3, co:co + 1])

                stage(6 + 10 * l)
                WBcur = [b["bufB"], b["Dw0"]]
                DMA("pool", bufB[:, :, 0:1024], wout_d[l].rearrange("(kc p) n -> p kc n", p=128), (), WBcur)
                load_w(bufB[:, :, 1024:1536], l, 2048, 2560, win_d, WBcur)
                DMA("sp", gath[:], cc_out[l].rearrange("(r p) n -> p r n", p=128), [b["ccout"], bmix[NB - 1], b["pB"]], [b["gath"]])
                TS("dve", stg[:, 0, :], gath[:, 0, :], selt[:, 0:1], None, ALU.mult, None, [b["gath"], b["par"]], STB)
                for h in range(4):
                    STT(Sst[:, h, :], gath[:, 1, h * 128:(h + 1) * 128], selt[:, 1:2], stg[:, 0, h * 128:(h + 1) * 128], ALU.mult, ALU.add,
                        [b["gath"], b["par"]] + STB, [bS[h]])
                hgrn_dir(1, list(range(NB - 1, 0, -1)), bufA, [b["bufA"]])
                if not last:
                    for h in range(4):
                        S.op("pool", lambda hh, h=h: hh.memset(Sst[:, h, :], 0.0), (), [bS[h]])
                    hgrn_dir(1, [0], bufA, [b["bufA"]])

                stage(7 + 10 * l)
                ffn_blocks = list(range(1 if last else 0, NB))
                for blk in ffn_blocks:
                    norm_block(l, 2, blk, mix[:, :, blk * BS:(blk + 1) * BS], [bmix[blk]])
                tb_list = ([] if last else [(0, 256, [0])]) + [(256 + i * 512, 512, [1 + 2 * i, 2 + 2 * i]) for i in range(4)]
                slots = [(bufA, [b["bufA"]]), (bufB, [b["bufB"], b["Dw0"]])]
                pbanks = [(pA, b["pA"]), (pB, b["pB"]), (pV, b["pV"])]
                pyb = [(pSW[:, 0:512], [b["pS0"]]), (pSW[:, 512:1024], [b["pSW1"]]), (pO[:, 0:512], [b["pO"]]), (pT[:, :].bitcast(F32), [b["pT"]])]
                gT = gt[:, 0:4, :].rearrange("p a t -> p (a t)").bitcast(BF16).rearrange("p (g t) -> p g t", g=4)
                gTb = bgt[0:4]
                sa = gt[:, 4:6, :].rearrange("p a t -> p (a t)")
                pi = 0
                yi = 0
                for gi, j0 in enumerate(range(0, 22, 4)):
                    G = min(4, 22 - j0)
                    wt, wtb = slots[gi % 2]
                    wflat = wt[:].rearrange("p k n -> p (k n)")
                    w1 = wflat[:, 0:8192].rearrange("p (k a n) -> p k a n", k=8, a=2)
                    w2v = wflat[:, 8192:12288].rearrange("p (g n) -> p g n", g=4)
                    DMA("pool", w1[:, :, 0, 0:G * 128], w13_d[l, :, j0 * 128:(j0 + G) * 128].rearrange("(kc p) n -> p kc n", p=128), (), wtb)
                    DMA("pool", w1[:, :, 1, 0:G * 128], w13_d[l, :, DFF + j0 * 128:DFF + (j0 + G) * 128].rearrange("(kc p) n -> p kc n", p=128), (), wtb)
                    DMA("pool", w2v[:, 0:G, :], w2_d[l, j0 * 128:(j0 + G) * 128, :].rearrange("(g p) n -> p g n", p=128), (), wtb)
                    for (t0, n, blks) in tb_list:
                        xb = [bXT[i] for i in blks]
                        mb = [bmix[i] for i in blks]
                        for jj in range(G):
                            pa, pab = pbanks[pi % 3]
                            pb2, pbb = pbanks[(pi + 1) % 3]
                            pi += 2
                            for kc in range(8):
                                MM(pa[:, 0:n], w1[:, kc, 0, jj * 128:(jj + 1) * 128], mix[:, kc, t0:t0 + n], kc == 0, kc == 7, mb + wtb, [pab])
                            for kc in range(8):
                                MM(pb2[:, 0:n], w1[:, kc, 1, jj * 128:(jj + 1) * 128], mix[:, kc, t0:t0 + n], kc == 0, kc == 7, mb + wtb, [pbb])
                            ACT(sa[:, 0:n], pa[:, 0:n], AF.Silu, [pab], bgt[4:6])
                            TT("dve", gT[:, jj, 0:n], sa[:, 0:n], pb2[:, 0:n], ALU.mult, bgt[4:6] + [pbb], gTb)
                        for ncn in range(8):
                            py, pyb_ = pyb[yi % 4]
                            yi += 1
                            for jj in range(G):
                                MM(py[:, 0:n], w2v[:, jj, ncn * 128:(ncn + 1) * 128], gT[:, jj, 0:n], jj == 0, jj == G - 1, gTb + wtb, pyb_)
                            jm = 1 if blks == [0] else 0
                            STT(XT[:, ncn, t0:t0 + n], py[:, 0:n], PAR[:, l, jm, 5, ncn:ncn + 1], XT[:, ncn, t0:t0 + n], ALU.mult, ALU.add,
                                pyb_ + [b["PAR"]] + xb, xb)
                relA, relB = [], []

            stage(30)
            outb = []
            for blk in range(1, NB):
                t0 = blk * BS
                xs = XT[:, :, t0:t0 + BS]
                ACT(nt[:], xs, AF.Square, [bXT[blk]], NTB)
                for kc in range(8):
                    MM(pV[:, 0:BS], ones_f[:], nt[:, kc, :], kc == 0, kc == 7, NTB + [b["const"]], [b["pV"]])
                rsqrt_psum(pV[:, 0:BS], BS, 1.0 / D, [b["pV"], b["const"]], pV[:, 256:512], [b["pV"]], gt[:, 9, :], [bgt[9]])
                TT("dve", nt[:], xs, pV[:, 256:512].unsqueeze(1).broadcast_to([128, 8, BS]), ALU.mult, [bXT[blk], b["pV"]], NTB)
                for kc in range(8):
                    ACT(nt[:, kc, :], nt[:, kc, :], AF.Identity, NTB + [b["par"], b["lpar"]], NTB, scale=fng[:, kc:kc + 1])
                for tl in range(2):
                    for half in range(2):
                        pb, bb = (pA, b["pA"]) if half == 0 else (pB, b["pB"])
                        for k4 in range(4):
                            kc = half * 4 + k4
                            TR(pb[:, k4 * 128:(k4 + 1) * 128], nt[:, kc, tl * 128:(tl + 1) * 128], ident_f[:], NTB + [b["const"]], [bb])
                        ob = mix[:, tl, 0:2048].bitcast(F32)
                        CP("act" if half == 0 else "dve", ob[:, half * 512:(half + 1) * 512], pb[:, :], [bb], bmix + [bosb[tl]])
                    r0 = (blk - 1) * BS + tl * 128
                    DMA("sp", y_d[r0:r0 + 128, :], ob[:], [bosb[tl]], [b["y"]])

        except _Stop:
            pass
        if dumps:
            env = dict(locals())
            ALLB = list(b.values()) + bXT + bmix + bgt + bS + bpW + bdg
            bdst = Buf("dstg")
            off = 0
            for (ap, n) in dumps(env):
                dst = dbg_d[:, off:off + n]
                stv = dstg[:, 0:n]
                if len(ap.shape) == 3:
                    dst = dst.rearrange("p (a t) -> p a t", a=ap.shape[1])
                    stv = stv.rearrange("p (a t) -> p a t", a=ap.shape[1])
                S.op("dve", lambda h, ap=ap, stv=stv: h.tensor_copy(out=stv, in_=ap), ALLB + [bdst], [bdst])
                S.dma("sp", lambda h, stv=stv, dst=dst: h.dma_start(out=dst, in_=stv), [bdst], [b["dbg"], bdst])
                off += n
        S.wait_all("sp", [b["y"]])
        if dbg:
            S.wait_all("sp", [b["dbg"]])
        S.emit(block)
    return nc


def _col(v):
    v = np.asarray(v)
    lead = v.shape[:-1]
    n = v.shape[-1] // 128
    r = v.reshape(lead + (n, 128))
    return np.ascontiguousarray(np.moveaxis(r, -1, 0))


def prep_inputs(inp):
    f = lambda a: np.ascontiguousarray(np.asarray(a, dtype=np.float32))
    x, c, ctx, c_ctx = f(inp["x"]), f(inp["c"]), f(inp["ctx"]), f(inp["c_ctx"])
    w_in = f(inp["w_in"])
    w_in_odd = w_in.copy()
    w_in_odd[:, :, 512:1024] = w_in[:, :, 1024:1536]
    w_in_odd[:, :, 1024:1536] = w_in[:, :, 512:1024]
    lbl = f(inp["hgrn_lb_logits"])
    ws = f(inp["gmlp_w_s"])
    bsv = f(inp["gmlp_b_s"])
    dww = f(inp["conv_dw_w"])
    common = {
        "ada_w": f(inp["ada_w"]),
        "ada_b": _col(f(inp["ada_b"])),
        "n1g": _col(f(inp["norm1_g"])),
        "n2g": _col(f(inp["norm2_g"])),
        "fng": _col(f(inp["final_norm_g"])),
        "ong": _col(f(inp["hgrn_onorm_g"])),
        "gln": np.ascontiguousarray(np.broadcast_to(np.stack([f(inp["gmlp_ln_g"]), f(inp["gmlp_ln_b"])], axis=1)[None], (128, L, 2, 256))),
        "cpar": np.ascontiguousarray(np.stack([_col(f(inp["conv_dw_b"])), _col(f(inp["conv_ln_g"])), _col(f(inp["conv_ln_b"])),
                                               _col(f(inp["conv_pw_b"]))], axis=2)),
        "pww": f(inp["conv_pw_w"]),
        "w_out": f(inp["w_out"]),
        "w13": f(inp["ffn_w13"]),
        "w2": f(inp["ffn_w2"]),
    }
    per_par = []
    for par in range(2):
        m = par == 1
        ws_p = ws[:, :, ::-1, ::-1] if m else ws
        bs_p = bsv[:, :, ::-1] if m else bsv
        dw_p = dww[:, ::-1, :] if m else dww
        lb_p = lbl[:, ::-1, :] if m else lbl
        d = dict(common)
        d["w_in"] = w_in_odd if m else w_in
        d["lbl"] = _col(lb_p)
        d["wsT"] = np.ascontiguousarray(np.transpose(ws_p, (3, 0, 1, 2)))
        bsl = np.zeros((128, L, 2, 128), np.float32)
        for ch in range(2):
            for hh in range(2):
                bsl[hh * 64:(hh + 1) * 64, :, ch, :] = bs_p[:, ch * 2 + hh, :][None]
        d["bs"] = bsl
        d["dww"] = np.ascontiguousarray(np.transpose(dw_p.reshape(L, 31, 2, 128), (3, 0, 2, 1)))
        per_par.append(d)
    ins = []
    for core in range(8):
        bi, par = core // 2, core % 2
        d = dict(per_par[par])
        if par == 0:
            xl = np.concatenate([ctx[bi], x[bi, 0:2048]], axis=0)
        else:
            xl = np.concatenate([ctx[bi, ::-1], x[bi, 4095:2047:-1]], axis=0)
        d["x"] = np.ascontiguousarray(xl)
        d["cc"] = np.ascontiguousarray(np.stack([_col(c[bi]), _col(c_ctx)], axis=-1))
        sel = np.zeros((128, 2), np.float32)
        sel[:, 1 - par] = 1.0
        d["sel"] = sel
        ins.append(d)
    return ins


_NC = None


def kernel(**inputs):
    global _NC
    ins = prep_inputs(inputs)
    if _NC is None:
        _NC = build()
    res = run_bass_kernel_spmd(_NC, ins, core_ids=list(range(8)))
    out = np.zeros((4, 4096, D), np.float32)
    for core in range(8):
        bi, par = core // 2, core % 2
        y = res.results[core]["y"]
        if par == 0:
            out[bi, 0:2048] = y
        else:
            out[bi, 4095:2047:-1] = y
    return out
```

```python
import numpy as np
from contextlib import ExitStack
import concourse.bass as bass
import concourse.mybir as mybir
from concourse.bass_utils import run_bass_kernel_spmd

F32 = mybir.dt.float32
BF16 = mybir.dt.bfloat16
I32 = mybir.dt.int32
AF = mybir.ActivationFunctionType
ALU = mybir.AluOpType

L = 2
D = 1024
T = 2304
NB = 9
BS = 256
DFF = 2816
EPS = 1e-6


class Buf:
    def __init__(self, name, excl=False):
        self.name = name
        self.w = None
        self.r = {}
        self.excl = excl


class Sched:
    ENG = ("pe", "act", "dve", "pool", "sp")

    def __init__(self, nc):
        self.nc = nc
        self.prog = {e: [] for e in self.ENG}
        self.NR = 8
        self.units = list(self.ENG) + [f"d_{q}{r}" for q in ("sp", "pool") for r in range(self.NR)] + ["cc"]
        self.dcount = {"sp": 0, "pool": 0}
        self.cnt = {u: 0 for u in self.units}
        self.mult = {u: (16 if u.startswith("d_") else 1) for u in self.units}
        self.seen = {e: {} for e in self.ENG}
        self.sem = {}

    def _deps(self, e, reads, writes):
        deps = {}
        for b in reads:
            if b.w is not None:
                f, i = b.w
                deps[f] = max(deps.get(f, 0), i)
        for b in writes:
            if b.w is not None:
                f, i = b.w
                deps[f] = max(deps.get(f, 0), i)
            for f, i in b.r.items():
                deps[f] = max(deps.get(f, 0), i)
        for f, i in deps.items():
            if f == e and e == "pe":
                continue
            if self.seen[e].get(f, 0) >= i:
                continue
            self.seen[e][f] = i
            self.prog[e].append(("wait", f, i * self.mult[f]))

    def op(self, e, fn, reads=(), writes=(), inc=True):
        ex = [r for r in reads if r.excl]
        if ex:
            writes = list(writes) + [r for r in ex if r not in writes]
            reads = [r for r in reads if not r.excl]
        self._deps(e, reads, writes)
        idx = self.cnt[e] + 1
        if inc:
            self.cnt[e] = idx
        self.prog[e].append(("inst", fn, e if inc else None, 1))
        for b in reads:
            b.r[e] = max(b.r.get(e, 0), idx)
        for b in writes:
            b.w = (e, idx)
            b.r = {}

    def dma(self, q, fn, reads=(), writes=(), unit=None):
        if unit is None:
            d = f"d_{q}{self.dcount[q] % self.NR}"
            self.dcount[q] += 1
            if self.cnt[d] > 0 and self.seen[q].get(d, 0) < self.cnt[d]:
                self.seen[q][d] = self.cnt[d]
                self.prog[q].append(("wait", d, self.cnt[d] * self.mult[d]))
        else:
            d = unit
        self._deps(q, reads, writes)
        self.cnt[d] += 1
        idx = self.cnt[d]
        self.prog[q].append(("inst", fn, d, self.mult[d]))
        for b in reads:
            b.r[d] = max(b.r.get(d, 0), idx)
        for b in writes:
            b.w = (d, idx)
            b.r = {}

    def wait_all(self, e, bufs):
        self._deps(e, bufs, ())

    def emit(self, block):
        S = self

        def run(e, h):
            for it in S.prog[e]:
                if it[0] == "wait":
                    h.wait_ge(S.sem[it[1]], it[2])
                else:
                    ins = it[1](h)
                    if it[2] is not None:
                        ins.then_inc(S.sem[it[2]], it[3])

        @block.tensor
        def _(h):
            run("pe", h)

        @block.scalar
        def _(h):
            run("act", h)

        @block.vector
        def _(h):
            run("dve", h)

        @block.gpsimd
        def _(h):
            run("pool", h)

        @block.sync
        def _(h):
            run("sp", h)


class _Stop(Exception):
    pass


def build(layers=(0, 1), dbg=None, stop=None, dumps=None):
    nc = bass.Bass("TRN2", target_bir_lowering=False)

    def din(name, shape, dt=F32):
        return nc.dram_tensor(name, list(shape), dt, kind="ExternalInput").ap()

    x_d = din("x", [T, D])
    cc_d = din("cc", [128, 8, 2])
    adaw_d = din("ada_w", [L, D, 6 * D])
    adab_d = din("ada_b", [128, L, 48])
    n1g_d = din("n1g", [128, L, 8])
    n2g_d = din("n2g", [128, L, 8])
    fng_d = din("fng", [128, 8])
    win_d = din("w_in", [L, D, 3584])
    lbl_d = din("lbl", [128, L, 2, 4])
    ong_d = din("ong", [128, L, 4])
    gln_d = din("gln", [128, L, 2, 256])
    wsT_d = din("wsT", [128, L, 4, 128])
    bs_d = din("bs", [128, L, 2, 128])
    dww_d = din("dww", [128, L, 2, 31])
    cpar_d = din("cpar", [128, L, 4, 2])
    pww_d = din("pww", [L, 256, 256])
    wout_d = din("w_out", [L, D, D])
    w13_d = din("w13", [L, D, 2 * DFF])
    w2_d = din("w2", [L, DFF, D])
    sel_d = din("sel", [128, 2])
    y_d = nc.dram_tensor("y", [2048, D], F32, kind="ExternalOutput").ap()
    if dbg:
        dbg_d = nc.dram_tensor("dbg", [128, dbg], F32, kind="ExternalOutput").ap()
    cc_in = [nc.dram_tensor(f"cc_in{l}", [128, 512], BF16, kind="Internal").ap() for l in range(L)]
    cc_out = [nc.dram_tensor(f"cc_out{l}", [256, 512], BF16, kind="Internal", addr_space="Local").ap() for l in range(L)]

    bar_in = [nc.dram_tensor(f"bar_in{l}", [128, 16], BF16, kind="Internal").ap() for l in range(L)]
    bar_out = [nc.dram_tensor(f"bar_out{l}", [256, 16], BF16, kind="Internal", addr_space="Local").ap() for l in range(L)]

    es = ExitStack()
    with es:
        def sb(name, shape, dt):
            return es.enter_context(nc.sbuf_tensor("sb_" + name, list(shape), dt))

        def pst(name, shape, dt):
            return es.enter_context(nc.psum_tensor("ps_" + name, list(shape), dt))

        S = Sched(nc)
        for u in S.units:
            S.sem[u] = es.enter_context(nc.semaphore("s_" + u))

        XT = sb("XT", [128, 8, T], F32)
        mix = sb("mix", [128, 8, T], BF16)
        bufA = sb("bufA", [128, 8, 1536], BF16)
        bufB = sb("bufB", [128, 8, 1536], BF16)
        hTb = sb("hTb", [128, 8, BS], BF16)
        gt = sb("gt", [128, 14, BS], F32)
        nt = gt[:, 0:8, :]
        scn = sb("scn", [128, 9216], BF16)
        qtT = scn[:, 0:1024].rearrange("p (h t) -> p h t", h=4)
        qhT = scn[:, 1024:2048].rearrange("p (h t) -> p h t", h=4)
        ktT = scn[:, 2048:3072].rearrange("p (h t) -> p h t", h=4)
        kt_tok = scn[:, 3072:5120].rearrange("p (a r n) -> p a r n", a=2, r=2)
        v_tok = scn[:, 5120:6144].rearrange("p (a n) -> p a n", a=2)
        dgD = scn[:, 6144:8192].rearrange("p (h c k) -> p h c k", h=4, c=4)
        yaT = scn[:, 8192:9216].rearrange("p (h t) -> p h t", h=4)
        Dw1 = scn[:, 0:3968].rearrange("p (t k) -> p t k", t=31)
        GE = sb("GE", [128, 4, 2, 4], F32)
        Sst = sb("Sst", [128, 4, 128], BF16)
        A_sb = sb("A_sb", [128, 2, 4, 64], BF16)
        ident_f = sb("ident_f", [128, 128], F32)
        ident_b = sb("ident_b", [128, 128], BF16)
        ones_f = sb("ones_f", [128, 128], F32)
        gones = sb("gones", [128, 128], F32)
        maskt = sb("maskt", [128, 2, 2, 64], I32)
        rmask = sb("rmask", [128, 256], F32)
        onesw = gt[:, 0, 0:64]
        cc_s = sb("cc_s", [128, 8, 2], F32)
        sil = sb("sil", [128, 8, 2], BF16)
        PAR = sb("PAR", [128, L, 2, 6, 8], F32)
        adab = sb("adab", [128, L, 48], F32)
        n1g = sb("n1g", [128, L, 8], F32)
        n2g = sb("n2g", [128, L, 8], F32)
        fng = sb("fng", [128, 8], F32)
        lbl = sb("lbl", [128, L, 2, 4], F32)
        LB = sb("LB", [128, L, 2, 4, 3], F32)
        ong = sb("ong", [128, L, 4], F32)
        gln = sb("gln", [128, 1, 2, 256], F32)
        wsT = sb("wsT", [128, 1, 4, 128], BF16)
        bs_s = sb("bs_s", [128, 1, 2, 128], F32)
        dww = sb("dww", [128, L, 2, 31], F32)
        cpar = sb("cpar", [128, L, 4, 2], F32)
        pww = sb("pww", [128, 1, 2, 256], BF16)
        selt = sb("selt", [128, 2], F32)
        gath = sb("gath", [128, 2, 512], BF16)
        stg = gt[:, 0:4, :].rearrange("p (a b) t -> p a (b t)", a=2)
        small = sb("small", [128, 64], F32)
        small2 = sb("small2", [128, 64], F32)
        if dbg:
            dstg = sb("dstg", [128, 1024], F32)
        osb = mix[:, 0, 0:2048].bitcast(F32)

        pA = pst("pA", [128, 512], F32)
        pB = pst("pB", [128, 512], F32)
        pV = pst("pV", [128, 512], F32)
        pT = pst("pT", [128, 1024], BF16)
        pSW = pst("pSW", [128, 1024], F32)
        pO = pst("pO", [128, 1024], F32)
        pScs = [pSW[:, 0:256].rearrange("p (h t) -> p h t", h=4), pSW[:, 256:512].rearrange("p (h t) -> p h t", h=4)]
        pW = pSW[:, 512:1024].rearrange("p (h v) -> p h v", h=4)
        pOv = pO[:, :].rearrange("p (h t) -> p h t", h=4)

        block = es.enter_context(nc.Block())

        bXT = [Buf(f"XT{i}") for i in range(NB)]
        bmix = [Buf(f"mix{i}") for i in range(NB)]
        b = {n: Buf(n) for n in ("bufA", "bufB", "Dw", "hTb", "nt", "qtT", "qhT", "ktT", "kt_tok", "v_tok", "GE",
                                 "A0", "A1", "yaT", "const", "par", "PAR", "LB", "pA", "pB", "pV", "pT", "pS0", "pS1",
                                 "pO", "gath", "stg", "small", "osb", "y", "sil", "ccin", "ccout", "dbg", "Sfin", "Dw0", "lpar", "small2")}
        bgt = [Buf(f"gt{i}") for i in range(14)]
        bS = [Buf(f"S{h}") for h in range(4)]
        bpW = [Buf(f"pW{h}") for h in range(4)]
        bdg = [Buf(f"dg{h}") for h in range(4)]
        bosb = [Buf("osb0"), Buf("osb1")]
        NTB = bgt[0:8]
        STB = bgt[0:4]
        bA = [b["A0"], b["A1"]]
        for nm in ("pA", "pB", "pV", "pT", "pS0", "pO"):
            b[nm].excl = True
        b["pS1"] = b["pS0"]
        b["pSW1"] = Buf("pSW1", excl=True)
        bpS = [b["pS0"], b["pS0"]]
        Wreg = [(pA[:, 0:128], b["pA"]), (pB[:, 0:128], b["pB"]), (pV[:, 0:128], b["pV"]), (pSW[:, 512:640], b["pSW1"])]
        bpW = [w[1] for w in Wreg]

        def MM(out, lhsT, rhs, start, stop, reads, writes):
            S.op("pe", lambda h: h.matmul(out, lhsT, rhs, start=start, stop=stop), reads, writes, inc=stop)

        def TR(out, in_, ident, reads, writes):
            S.op("pe", lambda h: h.transpose(out, in_, ident), reads, writes, inc=True)

        def ACT(out, in_, func, reads, writes, bias=None, scale=None):
            kw = {}
            if bias is not None:
                kw["bias"] = bias
            if scale is not None:
                kw["scale"] = scale
            S.op("act", lambda h: h.activation(out=out, in_=in_, func=func, **kw), reads, writes)

        def TT(e, out, in0, in1, op, reads, writes):
            S.op(e, lambda h: h.tensor_tensor(out=out, in0=in0, in1=in1, op=op), reads, writes)

        def TS(e, out, in0, s1, s2, op0, op1, reads, writes):
            if op1 is None:
                S.op(e, lambda h: h.tensor_scalar(out=out, in0=in0, scalar1=s1, scalar2=None, op0=op0), reads, writes)
            else:
                S.op(e, lambda h: h.tensor_scalar(out=out, in0=in0, scalar1=s1, scalar2=s2, op0=op0, op1=op1), reads, writes)

        def STT(out, in0, scalar, in1, op0, op1, reads, writes):
            S.op("dve", lambda h: h.scalar_tensor_tensor(out=out, in0=in0, scalar=scalar, in1=in1, op0=op0, op1=op1), reads, writes)

        def CP(e, out, in_, reads, writes):
            if e == "act":
                S.op(e, lambda h: h.activation(out=out, in_=in_, func=AF.Identity), reads, writes)
            else:
                S.op(e, lambda h: h.tensor_copy(out=out, in_=in_), reads, writes)

        def DMA(q, out, in_, reads, writes):
            S.dma(q, lambda h: h.dma_start(out=out, in_=in_), reads, writes)

        def stage(k):
            if stop is not None and k > stop:
                raise _Stop()

        try:
            cst = [b["const"]]
            S.op("pool", lambda h: h.memset(ones_f[:], 1.0), (), cst)
            S.op("pool", lambda h: h.memset(onesw, 1.0), (), cst + [bgt[0]])
            S.op("pool", lambda h: h.memset(ident_f[:], 0.0), (), cst)
            S.op("pool", lambda h: h.affine_select(out=ident_f[:], in_=ident_f[:], pattern=[[1, 128]], compare_op=ALU.not_equal,
                                                    fill=1.0, base=0, channel_multiplier=-1), cst, cst)
            S.op("pool", lambda h: h.tensor_copy(out=ident_b[:], in_=ident_f[:]), cst, cst)
            S.op("pool", lambda h: h.memset(gones[:], 0.0), (), cst)
            S.op("pool", lambda h: h.memset(gones[0:64, 0:64], 1.0), cst, cst)
            S.op("pool", lambda h: h.memset(gones[64:128, 64:128], 1.0), cst, cst)
            S.op("pool", lambda h: h.memset(rmask[:], 1.0), (), cst)
            S.op("pool", lambda h: h.memset(rmask[:].rearrange("p (c t) -> p c t", t=64)[:, :, 0:1], 0.0), cst, cst)
            S.op("pool", lambda h: h.memset(A_sb[:], 0.0), (), bA)
            S.op("pool", lambda h: h.memset(maskt[:], 0.0), (), cst)
            for half in (0, 1):
                ps_ = slice(half * 64, half * 64 + 64)
                S.op("pool", lambda h, ps_=ps_, half=half: h.affine_select(out=maskt[ps_, 0, half, :], in_=onesw[ps_, :],
                                                                 pattern=[[1, 64]], compare_op=ALU.is_ge, fill=0.0, base=0,
                                                                 channel_multiplier=-1), cst + [bgt[0]], cst)
                S.op("pool", lambda h, ps_=ps_, half=half: h.affine_select(out=maskt[ps_, 1, half, :], in_=onesw[ps_, :],
                                                                 pattern=[[-1, 64]], compare_op=ALU.is_ge, fill=0.0, base=0,
                                                                 channel_multiplier=1), cst + [bgt[0]], cst)

            par = [b["par"], b["lpar"]]
            for dst, src in ((cc_s, cc_d), (adab, adab_d), (n1g, n1g_d), (n2g, n2g_d), (fng, fng_d), (lbl, lbl_d), (ong, ong_d),
                             (dww, dww_d), (cpar, cpar_d), (selt, sel_d)):
                DMA("sp", dst[:], src, (), par)
            S.op("dve", lambda h: h.memset(LB[:], 0.0), (), [b["LB"]])
            TT("dve", small[:, 0:8], lbl[:, 1].rearrange("p a b -> p (a b)"), lbl[:, 0].rearrange("p a b -> p (a b)"), ALU.subtract, par, [b["small"]])
            ACT(LB[:, 1, :, :, 0], small[:, 0:8].rearrange("p (a b) -> p a b", a=2), AF.Sigmoid, [b["small"]], [b["LB"]])
            for l in range(L):
                TS("dve", LB[:, l, :, :, 1], LB[:, l, :, :, 0], -1.0, 1.0, ALU.mult, ALU.add, [b["LB"]], [b["LB"]])
                TS("dve", LB[:, l, :, :, 2], LB[:, l, :, :, 0], 1.0, -1.0, ALU.mult, ALU.add, [b["LB"]], [b["LB"]])
            ACT(sil[:], cc_s[:], AF.Silu, par, [b["sil"]])

            pM = pA[:, 0:96].rearrange("p (n j) -> p n j", j=2)
            ada_slot = [bufA[:, :, 0:512], bufA[:, :, 512:1024], bufA[:, :, 1024:1536], bufB[:, :, 0:512], bufB[:, :, 512:1024], bufB[:, :, 1024:1536]]
            bslot = [Buf(f"adaslot{i}") for i in range(6)]
            si = 0
            for l in layers:
                for ng in range(12):
                    sl, bsl = ada_slot[si % 6], bslot[si % 6]
                    si += 1
                    DMA("pool", sl, adaw_d[l, :, ng * 512:(ng + 1) * 512].rearrange("(kc p) n -> p kc n", p=128), (), [bsl])
                    for nn in range(4):
                        for kc in range(8):
                            MM(pM[:, ng * 4 + nn, :], sl[:, kc, nn * 128:(nn + 1) * 128], sil[:, kc, :], kc == 0, kc == 7,
                               [bsl, b["sil"]], [b["pA"]])
                for j in range(2):
                    TT("dve", PAR[:, l, j].rearrange("p m k -> p (m k)"), pM[:, :, j], adab[:, l, :], ALU.add, [b["pA"], b["par"]], [b["PAR"]])
                    STT(PAR[:, l, j, 1, :], PAR[:, l, j, 1, :], 1.0, n1g[:, l, :], ALU.add, ALU.mult, [b["PAR"], b["par"]], [b["PAR"]])
                    STT(PAR[:, l, j, 4, :], PAR[:, l, j, 4, :], 1.0, n2g[:, l, :], ALU.add, ALU.mult, [b["PAR"], b["par"]], [b["PAR"]])
            relA = bslot[0:3]
            relB = bslot[3:6]

            stage(1)
            for tt in range(T // 128):
                st_ = gt[:, 4 * (tt % 2):4 * (tt % 2) + 4, :].rearrange("p a t -> p (a t)")
                STX = bgt[4 * (tt % 2):4 * (tt % 2) + 4]
                DMA("sp", st_, x_d[tt * 128:(tt + 1) * 128, :], (), STX)
                for half in range(2):
                    pb, bb = (pA, b["pA"]) if half == 0 else (pB, b["pB"])
                    for k4 in range(4):
                        kc = half * 4 + k4
                        TR(pb[:, k4 * 128:(k4 + 1) * 128], st_[:, kc * 128:(kc + 1) * 128], ident_f[:], STX + [b["const"]], [bb])
                    CP("act" if half == 0 else "dve", XT[:, half * 4:half * 4 + 4, tt * 128:(tt + 1) * 128],
                       pb[:, :].rearrange("p (k t) -> p k t", k=4), [bb], [bXT[tt // 2]])

            def rsqrt_psum(ps_ap, n, scale, reads_b, out_ap, out_b, tmp_ap, tmp_b):
                ACT(tmp_ap, ps_ap, AF.Ln, reads_b, tmp_b, bias=eps_col, scale=scale)
                ACT(out_ap, tmp_ap, AF.Exp, tmp_b, out_b, scale=-0.5)

            eps_t = small[:, 32:33]
            S.op("dve", lambda h: h.memset(eps_t, EPS), (), [b["const"]])
            eps_col = eps_t

            def norm_block(l, which, blk, out_ap, out_b, final=False):
                j = 1 if blk == 0 else 0
                t0 = blk * BS
                xs = XT[:, :, t0:t0 + BS]
                ACT(nt[:], xs, AF.Square, [bXT[blk]], NTB)
                for kc in range(8):
                    MM(pV[:, 0:BS], ones_f[:], nt[:, kc, :], kc == 0, kc == 7, NTB + [b["const"]], [b["pV"]])
                rsqrt_psum(pV[:, 0:BS], BS, 1.0 / D, [b["pV"], b["const"]], pV[:, 256:512], [b["pV"]], gt[:, 9, :], [bgt[9]])
                TT("dve", nt[:], xs, pV[:, 256:512].unsqueeze(1).broadcast_to([128, 8, BS]), ALU.mult, [bXT[blk], b["pV"]], NTB)
                for kc in range(8):
                    if final:
                        ACT(out_ap[:, kc, :], nt[:, kc, :], AF.Identity, NTB + [b["par"], b["lpar"]], out_b, scale=fng[:, kc:kc + 1])
                    else:
                        m = 0 if which == 1 else 3
                        ACT(out_ap[:, kc, :], nt[:, kc, :], AF.Identity, NTB + [b["PAR"]], out_b,
                            bias=PAR[:, l, j, m, kc:kc + 1], scale=PAR[:, l, j, m + 1, kc:kc + 1])

            def load_w(dst, l, c0, c1, src, wb):
                DMA("pool", dst, src[l, :, c0:c1].rearrange("(kc p) n -> p kc n", p=128), (), wb)

            def dump(ap, n, reads):
                DMA("sp", dbg_d[:, 0:n], ap, reads, [b["dbg"]])

            stage(2)
            for l in layers:
                last = (l == L - 1)
                WA = [b["bufA"]] + relA
                WB = [b["bufB"], b["Dw0"]] + relB
                load_w(bufA[:, :, 0:1024], l, 0, 1024, win_d, WA)
                load_w(bufA[:, :, 1024:1536], l, 1536, 2048, win_d, WA)
                relA = []
                lp = [b["lpar"]]
                DMA("sp", gln[:, 0], gln_d[:, l], (), lp)
                DMA("sp", bs_s[:, 0], bs_d[:, l], (), lp)
                DMA("pool", wsT[:, 0], wsT_d[:, l], (), lp)
                DMA("pool", pww[:, 0], pww_d[l].rearrange("(c p) n -> p c n", p=128), (), lp)
                load_w(bufB[:, :, 0:1024], l, 2560, 3584, win_d, WB)
                relB = []
                dwv = bufB[:, :, 1024:1536].rearrange("p k (j c) -> p k j c", j=4)
                TT("dve", dwv[:, 0:7], ident_b[:].unsqueeze(1).unsqueeze(1).broadcast_to([128, 7, 4, 128]),
                   dww[:, l, 0, 0:28].rearrange("p (k j) -> p k j", j=4).unsqueeze(3).broadcast_to([128, 7, 4, 128]), ALU.mult,
                   [b["par"], b["const"]], [b["Dw0"]])
                TT("dve", dwv[:, 7, 0:3], ident_b[:].unsqueeze(1).broadcast_to([128, 3, 128]),
                   dww[:, l, 0, 28:31].unsqueeze(2).broadcast_to([128, 3, 128]), ALU.mult, [b["par"], b["const"]], [b["Dw0"]])

                def hgrn_dir(dr, blocks, Wt, wb):
                    mid = 31 if dr == 0 else 32
                    eidx = 63 if dr == 0 else 0
                    S.op("pool", lambda hh: hh.memset(A_sb[:], 0.0), (), bA)
                    S.op("pool", lambda hh: hh.memset(kt_tok[:], 0.0), (), [b["kt_tok"]])
                    for blk in blocks:
                        j = 1 if blk == 0 else 0
                        t0 = blk * BS
                        first = (l == layers[0] and dr == 0 and blk == blocks[0])
                        if first:
                            stage(3.1)
                        norm_block(l, 1, blk, hTb, [b["hTb"]])
                        if first:
                            stage(3.2)
                        if True:
                            for tl in range(2):
                                for kc in range(8):
                                    MM(pV[:, :], hTb[:, kc, tl * 128:(tl + 1) * 128], Wt[:, kc, 1024:1536], kc == 0, kc == 7,
                                       [b["hTb"]] + wb, [b["pV"]])
                                CP("act", v_tok[:, tl, :], pV[:, :], [b["pV"]], [b["v_tok"]])
                        for hp in range(2):
                            hs2 = (2 * hp, 2 * hp + 1)
                            ctxs = []
                            for h in hs2:
                                pb, bb = (pA, b["pA"]) if h % 2 == 0 else (pB, b["pB"])
                                for kc in range(8):
                                    MM(pb[:, 0:256], Wt[:, kc, h * 128:(h + 1) * 128], hTb[:, kc, :], kc == 0, kc == 7, [b["hTb"]] + wb, [bb])
                                for kc in range(8):
                                    MM(pb[:, 256:512], Wt[:, kc, 512 + h * 128:512 + (h + 1) * 128], hTb[:, kc, :], kc == 0, kc == 7, [b["hTb"]] + wb, [bb])
                                g0 = 7 * (h % 2)
                                G = [bgt[g0 + i] for i in range(7)]
                                sl7 = [gt[:, g0 + i, :] for i in range(7)]
                                ctxs.append((h, pb, bb, G, sl7))
                            for (h, pb, bb, G, (sg, lf, bb_, dd, e1, e2, e3)) in ctxs:
                                ACT(sg, pb[:, 256:512], AF.Sigmoid, [bb], [G[0]])
                            for (h, pb, bb, G, (sg, lf, bb_, dd, e1, e2, e3)) in ctxs:
                                ACT(lf, sg, AF.Ln, [G[0], b["LB"]], [G[1]], bias=LB[:, l, dr, h, 0:1], scale=LB[:, l, dr, h, 1:2])
                            for (h, pb, bb, G, (sg, lf, bb_, dd, e1, e2, e3)) in ctxs:
                                S.op("dve", lambda hh, bb_=bb_, lf=lf: hh.tensor_tensor_scan(out=bb_, data0=rmask[:], data1=lf,
                                                                                          initial=0.0, op0=ALU.mult, op1=ALU.add),
                                     [G[1], b["const"]], [G[2]])
                            for (h, pb, bb, G, (sg, lf, bb_, dd, e1, e2, e3)) in ctxs:
                                b3 = bb_.rearrange("p (c t) -> p c t", t=64)
                                d3 = dd.rearrange("p (c t) -> p c t", t=64)
                                if dr == 0:
                                    TT("pool", d3, b3, b3[:, :, mid:mid + 1].broadcast_to([128, 4, 64]), ALU.subtract, [G[2]], [G[3]])
                                else:
                                    tmp = e1
                                    TT("pool", tmp, lf, bb_, ALU.subtract, [G[1], G[2]], [G[4]])
                                    TT("pool", d3, tmp.rearrange("p (c t) -> p c t", t=64), b3[:, :, 63:64].broadcast_to([128, 4, 64]), ALU.add,
                                       [G[4], G[2]], [G[3]])
                                    TT("pool", b3, d3, d3[:, :, mid:mid + 1].broadcast_to([128, 4, 64]), ALU.subtract, [G[3]], [G[2]])
                            for (h, pb, bb, G, (sg, lf, bb_, dd, e1, e2, e3)) in ctxs:
                                dsrc, dB, bsrc, bB = (dd, G[3], bb_, G[2]) if dr == 0 else (bb_, G[2], dd, G[3])
                                ACT(e1, dsrc, AF.Exp, [dB], [G[4]])
                                ACT(e2, dsrc, AF.Exp, [dB], [G[5]], scale=-1.0)
                                ACT(e3, bsrc, AF.Exp, [bB], [G[6]])
                            for (h, pb, bb, G, (sg, lf, bb_, dd, e1, e2, e3)) in ctxs:
                                kk = lf
                                TS("dve", kk, sg, LB[:, l, dr, h, 2:3], LB[:, l, dr, h, 1:2], ALU.mult, ALU.add, [G[0], b["LB"]], [G[1]])
                                TT("dve", ktT[:, h, :], kk, e2, ALU.mult, [G[1], G[5]], [b["ktT"]])
                                TT("dve", qtT[:, h, :], pb[:, 0:256], e1, ALU.mult, [bb, G[4]], [b["qtT"]])
                                TT("dve", qhT[:, h, :], pb[:, 0:256], e3, ALU.mult, [bb, G[6]], [b["qhT"]])
                                CP("dve", GE[:, h, 0, :], e3.rearrange("p (c t) -> p c t", t=64)[:, :, mid], [G[6]], [b["GE"]])
                                CP("dve", GE[:, h, 1, :], e1.rearrange("p (c t) -> p c t", t=64)[:, :, eidx], [G[4]], [b["GE"]])
                            for (h, pb, bb, G, sl7) in ctxs:
                                TT("dve", dgD[:, h, :, :], ident_b[:].unsqueeze(1).broadcast_to([128, 4, 128]),
                                   GE[:, h, 0, :].unsqueeze(2).broadcast_to([128, 4, 128]), ALU.mult, [b["GE"], b["const"]], [bdg[h]])
                        if first:
                            stage(3.3)
                        for tl in range(2):
                            for h in range(4):
                                TR(pT[:, h * 128:(h + 1) * 128], ktT[:, h, tl * 128:(tl + 1) * 128], ident_b[:], [b["ktT"], b["const"]], [b["pT"]])
                            CP("act", kt_tok[0:64, tl, 0, :], pT[0:64, 0:512], [b["pT"]], [b["kt_tok"]])
                            CP("act", kt_tok[64:128, tl, 1, :], pT[64:128, 0:512], [b["pT"]], [b["kt_tok"]])
                        if first:
                            stage(3.4)
                        corder = range(4) if dr == 0 else range(3, -1, -1)
                        for c in corder:
                            tl, par = c // 2, c % 2
                            cs = slice(c * 64, c * 64 + 64)
                            pSc = pScs[par]
                            for h in range(4):
                                MM(pSc[:, h, :], ktT[:, h, tl * 128:(tl + 1) * 128], qtT[:, h, cs], True, True, [b["ktT"], b["qtT"]], [bpS[par]])
                            S.op("dve", lambda hh, par=par, pSc=pSc: hh.copy_predicated(out=A_sb[:, par, :, :],
                                                                                   mask=maskt[:, dr, par, :].unsqueeze(1).broadcast_to([128, 4, 64]),
                                                                                   data=pSc[:, :, :]),
                                 [bpS[par], b["const"]], [bA[par]])
                            if first and c == 0:
                                stage(3.5)
                            for h in range(4):
                                hs = slice(h * 128, h * 128 + 128)
                                MM(pOv[:, h, cs], v_tok[:, tl, hs], A_sb[:, par, h, :], True, False, [b["v_tok"], bA[par]], [b["pO"]])
                                MM(pOv[:, h, cs], Sst[:, h, :], qhT[:, h, cs], False, True, [bS[h], b["qhT"]], [b["pO"]])
                                MM(Wreg[h][0], dgD[:, h, c, :], Sst[:, h, :], True, False, [bdg[h], bS[h]], [bpW[h]])
                                MM(Wreg[h][0], kt_tok[:, tl, par, hs], v_tok[:, tl, hs], False, True, [b["kt_tok"], b["v_tok"]], [bpW[h]])
                                ACT(Sst[:, h, :], Wreg[h][0], AF.Identity, [bpW[h], b["GE"]], [bS[h]], scale=GE[:, h, 1, c:c + 1])
                        if first:
                            stage(3.6)
                        if dr == 0:
                            CP("act", mix[:, 0:4, t0:t0 + BS], pOv, [b["pO"]], [bmix[blk]])
                            if first:
                                stage(3.7)
                        else:
                            osum = gt[:, 0:4, :]
                            TT("dve", osum, pOv, mix[:, 0:4, t0:t0 + BS], ALU.add, [b["pO"], bmix[blk]], bgt[0:4])
                            osq = gt[:, 4:8, :]
                            ACT(osq, osum, AF.Square, bgt[0:4], bgt[4:8])
                            for hp in range(2):
                                MM(pO[:, hp * 512:(hp + 1) * 512], ones_f[:], osq[:, 2 * hp:2 * hp + 2, :].rearrange("p a t -> p (a t)"), True, True,
                                   bgt[4:8] + [b["const"]], [b["pO"]])
                            ACT(osq.rearrange("p a t -> p (a t)"), pO[:, :], AF.Ln, [b["pO"], b["const"]], bgt[4:8], bias=eps_col, scale=1.0 / 128)
                            ACT(osq.rearrange("p a t -> p (a t)"), osq.rearrange("p a t -> p (a t)"), AF.Exp, bgt[4:8], bgt[4:8], scale=-0.5)
                            TT("dve", osum, osum, osq, ALU.mult, bgt[0:4] + bgt[4:8], bgt[0:4])
                            for h in range(4):
                                pb, bb = (pA, b["pA"]) if h % 2 == 0 else (pB, b["pB"])
                                for kc in range(8):
                                    MM(pb[:, 0:256], bufB[:, kc, 1024 + h * 128:1024 + (h + 1) * 128], hTb[:, kc, :], kc == 0, kc == 7, [b["hTb"]] + WBcur, [bb])
                                ACT(gt[:, 8, :], pb[:, 0:256], AF.Silu, [bb], [bgt[8]])
                                STT(yaT[:, h, :], osum[:, h, :], ong[:, l, h:h + 1], gt[:, 8, :], ALU.mult, ALU.mult, bgt[0:4] + [bgt[8], b["par"]], [b["yaT"]])
                            for ncn in range(8):
                                pb, bb = (pA, b["pA"]) if ncn % 2 == 0 else (pB, b["pB"])
                                for kk_ in range(8):
                                    rhs = yaT[:, kk_, :] if kk_ < 4 else mix[:, kk_, t0:t0 + BS]
                                    MM(pb[:, 0:256], bufB[:, kk_, ncn * 128:(ncn + 1) * 128], rhs, kk_ == 0, kk_ == 7,
                                       [b["yaT"], bmix[blk]] + WBcur, [bb])
                                STT(XT[:, ncn, t0:t0 + BS], pb[:, 0:256], PAR[:, l, j, 2, ncn:ncn + 1], XT[:, ncn, t0:t0 + BS], ALU.mult, ALU.add,
                                    [bb, b["PAR"], bXT[blk]], [bXT[blk]])

                stage(3 + 10 * l)
                for h in range(4):
                    S.op("pool", lambda hh, h=h: hh.memset(Sst[:, h, :], 0.0), (), [bS[h]])
                hgrn_dir(0, list(range(NB)), bufA, [b["bufA"]])
                stage(4 + 10 * l)
                DMA("sp", cc_in[l], Sst[:].rearrange("p h v -> p (h v)"), bS, [b["ccin"]])
                S.dma("pool", lambda hh, l=l: hh.collective_compute("AllGather", ALU.bypass, replica_groups=[[0, 1], [2, 3], [4, 5], [6, 7]],
                                                                    ins=[cc_in[l]], outs=[cc_out[l]]), [b["ccin"]], [b["ccout"]], unit="cc")
                S.dma("pool", lambda hh, l=l: hh.collective_compute("AllGather", ALU.bypass, replica_groups=[[0, 1], [2, 3], [4, 5], [6, 7]],
                                                                    ins=[bar_in[l]], outs=[bar_out[l]]), [b["ccout"]], [b["ccout"]], unit="cc")

                stage(5 + 10 * l)
                load_w(bufA[:, :, 512:1024], l, 1024, 1536, win_d, [b["bufA"]])
                DW1B = [b["qtT"], b["qhT"], b["ktT"], b["kt_tok"]]
                TT("dve", Dw1[:, :, :], ident_b[:].unsqueeze(1).broadcast_to([128, 31, 128]),
                   dww[:, l, 1, :].unsqueeze(2).broadcast_to([128, 31, 128]), ALU.mult, [b["par"], b["const"]], DW1B)
                bc_blocks = list(range(1 if last else 0, NB))
                zp = gt[:, 0:3, :].rearrange("p a t -> p (a t)").bitcast(BF16) if False else None
                for blk in bc_blocks:
                    j = 1 if blk == 0 else 0
                    t0 = blk * BS
                    norm_block(l, 1, blk, hTb, [b["hTb"]])
                    wbB = [b["bufB"]]
                    for ch in range(2):
                        for kc in range(8):
                            MM(pA[:, ch * 256:(ch + 1) * 256], bufB[:, kc, ch * 128:(ch + 1) * 128], hTb[:, kc, :], kc == 0, kc == 7, [b["hTb"]] + wbB, [b["pA"]])
                    gu = gt[:, 0:2, :]
                    ACT(gu, pA[:, :].rearrange("p (c t) -> p c t", c=2), AF.Gelu_apprx_tanh, [b["pA"]], bgt[0:2])
                    TL = []
                    for tl in range(2):
                        sm = small if tl == 0 else small2
                        TL.append(dict(tl=tl, pv=pV[:, tl * 256:(tl + 1) * 256], gv=gt[:, 2 if tl == 0 else 9, :], vn=gt[:, 3 if tl == 0 else 10, :],
                                       vln=gt[:, 4 if tl == 0 else 11, :].bitcast(BF16)[:, 0:256],
                                       bg=bgt[2 if tl == 0 else 9], bn=bgt[3 if tl == 0 else 10], bl=bgt[4 if tl == 0 else 11],
                                       st6=sm[:, 0:24].rearrange("p (g s) -> p g s", g=4), mv=sm[:, 24:32].rearrange("p (g s) -> p g s", g=4),
                                       lnv=sm[:, 36:40], rs4=sm[:, 40:44], bs=b["small"] if tl == 0 else b["small2"]))
                    for t_ in TL:
                        tl = t_["tl"]
                        for kc in range(8):
                            MM(t_["pv"], hTb[:, kc, tl * 128:(tl + 1) * 128], bufB[:, kc, 256:512], kc == 0, kc == 7, [b["hTb"]] + wbB, [b["pV"]])
                    for t_ in TL:
                        ACT(t_["gv"], t_["pv"], AF.Gelu_apprx_tanh, [b["pV"]], [t_["bg"]])
                    for t_ in TL:
                        for g in range(4):
                            S.op("dve", lambda hh, g=g, t_=t_: hh.bn_stats(out=t_["st6"][:, g, :], in_=t_["gv"][:, g * 64:(g + 1) * 64]), [t_["bg"]], [t_["bs"]])
                            S.op("dve", lambda hh, g=g, t_=t_: hh.bn_aggr(out=t_["mv"][:, g, :], in_=t_["st6"][:, g, :]), [t_["bs"]], [t_["bs"]])
                    for t_ in TL:
                        ACT(t_["lnv"], t_["mv"][:, :, 1], AF.Ln, [t_["bs"], b["const"]], [t_["bs"]], bias=eps_col, scale=1.0)
                    for t_ in TL:
                        ACT(t_["rs4"], t_["lnv"], AF.Exp, [t_["bs"]], [t_["bs"]], scale=-0.5)
                    for t_ in TL:
                        for g in range(4):
                            TS("dve", t_["vn"][:, g * 64:(g + 1) * 64], t_["gv"][:, g * 64:(g + 1) * 64], t_["mv"][:, g, 0:1], t_["rs4"][:, g:g + 1],
                               ALU.subtract, ALU.mult, [t_["bg"], t_["bs"]], [t_["bn"]])
                        TT("dve", t_["vn"], t_["vn"], gln[:, 0, 0, :], ALU.mult, [t_["bn"], b["par"], b["lpar"]], [t_["bn"]])
                        TT("dve", t_["vln"], t_["vn"], gln[:, 0, 1, :], ALU.add, [t_["bn"], b["par"], b["lpar"]], [t_["bl"]])
                    for t_ in TL:
                        tl = t_["tl"]
                        for g in range(4):
                            po = (g % 2) * 64
                            MM(pB[po:po + 64, (g // 2) * 256 + tl * 128:(g // 2) * 256 + (tl + 1) * 128], t_["vln"][:, g * 64:(g + 1) * 64], wsT[:, 0, g, :],
                               True, True, [t_["bl"], b["par"], b["lpar"]], [b["pB"]])
                    pB3 = pB[:, :].rearrange("p (c t) -> p c t", c=2)
                    mx = gt[:, 5:7, :]
                    for tl in range(2):
                        TT("dve", mx[:, :, tl * 128:(tl + 1) * 128], pB3[:, :, tl * 128:(tl + 1) * 128], bs_s[:, 0, :, :], ALU.add, [b["pB"], b["par"]], bgt[5:7])
                    TT("dve", mix[:, 4:6, t0:t0 + BS], mx, gu, ALU.mult, bgt[5:7] + bgt[0:2], [bmix[blk]])
                    nseg = 1 if blk == 0 else 4
                    sl_ = BS // nseg
                    pTf = pT[:, :].bitcast(F32)
                    CH = []
                    for ch in range(2):
                        sl = [0, 1, 2, 3, 4, 5] if ch == 0 else [9, 10, 11, 12, 13, 6]
                        zpc = gt[:, 7 + ch, :].bitcast(BF16)
                        CH.append(dict(ch=ch, pin=(pA, b["pA"]) if ch == 0 else (pB, b["pB"]), pcv=(pV, b["pV"]) if ch == 0 else (pTf, b["pT"]),
                                       sgm=gt[:, sl[0], :], cz=gt[:, sl[1], :], var=gt[:, sl[2], :], mean=gt[:, sl[3], :], msq=gt[:, sl[4], :],
                                       czs=gt[:, sl[5], :].bitcast(BF16)[:, 0:256], B=[bgt[i] for i in sl], zp=zpc, zb=bgt[7 + ch],
                                       zpv=zpc[:, 0:nseg * (sl_ + 30)].rearrange("p (s t) -> p s t", s=nseg)))
                    for c_ in CH:
                        ch = c_["ch"]; pin, pinb = c_["pin"]
                        for kc in range(8):
                            MM(pin[:, 0:256], bufB[:, kc, 512 + ch * 128:512 + (ch + 1) * 128], hTb[:, kc, :], kc == 0, kc == 7, [b["hTb"]] + wbB, [pinb])
                        for kc in range(8):
                            MM(pin[:, 256:512], bufB[:, kc, 768 + ch * 128:768 + (ch + 1) * 128], hTb[:, kc, :], kc == 0, kc == 7, [b["hTb"]] + wbB, [pinb])
                    for c_ in CH:
                        ACT(c_["sgm"], c_["pin"][0][:, 256:512], AF.Sigmoid, [c_["pin"][1]], [c_["B"][0]])
                    for c_ in CH:
                        S.op("pool", lambda hh, zp=c_["zp"]: hh.memset(zp[:, 0:512], 0.0), (), [c_["zb"]])
                    for c_ in CH:
                        TT("dve", c_["zpv"][:, :, 15:15 + sl_], c_["pin"][0][:, 0:256].rearrange("p (s t) -> p s t", s=nseg),
                           c_["sgm"].rearrange("p (s t) -> p s t", s=nseg), ALU.mult, [c_["pin"][1], c_["B"][0]], [c_["zb"]])
                    for c_ in CH:
                        ch = c_["ch"]; pcv, pcvb = c_["pcv"]
                        for tp in range(31):
                            dwt = bufB[:, tp // 4, 1024 + (tp % 4) * 128:1024 + (tp % 4 + 1) * 128] if ch == 0 else Dw1[:, tp, :]
                            MM(pcv[:, 0:256].rearrange("p (s t) -> p s t", s=nseg), dwt, c_["zpv"][:, :, tp:tp + sl_], tp == 0, tp == 30,
                               [c_["zb"]] + ([b["Dw0"]] if ch == 0 else DW1B), [pcvb])
                    for c_ in CH:
                        ACT(c_["cz"], c_["pcv"][0][:, 0:256], AF.Identity, [c_["pcv"][1], b["par"]], [c_["B"][1]], bias=cpar[:, l, 0, c_["ch"]:c_["ch"] + 1])
                    for c_ in CH:
                        ACT(c_["var"], c_["cz"], AF.Square, [c_["B"][1]], [c_["B"][2]])
                    for c_ in CH:
                        pcv, pcvb = c_["pcv"]
                        MM(pcv[:, 0:256], gones[:], c_["cz"], True, True, [c_["B"][1], b["const"]], [pcvb])
                        MM(pcv[:, 256:512], gones[:], c_["var"], True, True, [c_["B"][2], b["const"]], [pcvb])
                    for c_ in CH:
                        ACT(c_["mean"], c_["pcv"][0][:, 0:256], AF.Identity, [c_["pcv"][1]], [c_["B"][3]], scale=1.0 / 64)
                    for c_ in CH:
                        ACT(c_["msq"], c_["mean"], AF.Square, [c_["B"][3]], [c_["B"][4]])
                    for c_ in CH:
                        STT(c_["var"], c_["pcv"][0][:, 256:512], 1.0 / 64, c_["msq"], ALU.mult, ALU.subtract, [c_["pcv"][1], c_["B"][4]], [c_["B"][2]])
                    for c_ in CH:
                        ACT(c_["var"], c_["var"], AF.Ln, [c_["B"][2], b["const"]], [c_["B"][2]], bias=eps_col, scale=1.0)
                    for c_ in CH:
                        ACT(c_["var"], c_["var"], AF.Exp, [c_["B"][2]], [c_["B"][2]], scale=-0.5)
                    for c_ in CH:
                        TT("dve", c_["cz"], c_["cz"], c_["mean"], ALU.subtract, [c_["B"][1], c_["B"][3]], [c_["B"][1]])
                        TT("dve", c_["cz"], c_["cz"], c_["var"], ALU.mult, [c_["B"][1], c_["B"][2]], [c_["B"][1]])
                    for c_ in CH:
                        ch = c_["ch"]
                        ACT(c_["czs"], c_["cz"], AF.Silu, [c_["B"][1], b["par"]], [c_["B"][5]], bias=cpar[:, l, 2, ch:ch + 1], scale=cpar[:, l, 1, ch:ch + 1])
                    for co in range(2):
                        for ci in range(2):
                            MM(pB[:, co * 256:(co + 1) * 256], pww[:, 0, ci, co * 128:(co + 1) * 128], CH[ci]["czs"],
                               ci == 0, ci == 1, [CH[0]["B"][5], CH[1]["B"][5], b["par"], b["lpar"]], [b["pB"]])
                        ACT(mix[:, 6 + co, t0:t0 + BS], pB[:, co * 256:(co + 1) * 256], AF.Identity, [b["pB"], b["par"]], [bmix[blk]],
                            bias=cpar[:, l, 3, co:co + 1])

                stage(6 + 10 * l)
                WBcur = [b["bufB"], b["Dw0"]]
                DMA("pool", bufB[:, :, 0:1024], wout_d[l].rearrange("(kc p) n -> p kc n", p=128), (), WBcur)
                load_w(bufB[:, :, 1024:1536], l, 2048, 2560, win_d, WBcur)
                DMA("sp", gath[:], cc_out[l].rearrange("(r p) n -> p r n", p=128), [b["ccout"], bmix[NB - 1], b["pB"]], [b["gath"]])
                TS("dve", stg[:, 0, :], gath[:, 0, :], selt[:, 0:1], None, ALU.mult, None, [b["gath"], b["par"]], STB)
                for h in range(4):
                    STT(Sst[:, h, :], gath[:, 1, h * 128:(h + 1) * 128], selt[:, 1:2], stg[:, 0, h * 128:(h + 1) * 128], ALU.mult, ALU.add,
                        [b["gath"], b["par"]] + STB, [bS[h]])
                hgrn_dir(1, list(range(NB - 1, 0, -1)), bufA, [b["bufA"]])
                if not last:
                    for h in range(4):
                        S.op("pool", lambda hh, h=h: hh.memset(Sst[:, h, :], 0.0), (), [bS[h]])
                    hgrn_dir(1, [0], bufA, [b["bufA"]])

                stage(7 + 10 * l)
                ffn_blocks = list(range(1 if last else 0, NB))
                for blk in ffn_blocks:
                    norm_block(l, 2, blk, mix[:, :, blk * BS:(blk + 1) * BS], [bmix[blk]])
                tb_list = ([] if last else [(0, 256, [0])]) + [(256 + i * 512, 512, [1 + 2 * i, 2 + 2 * i]) for i in range(4)]
                slots = [(bufA, [b["bufA"]]), (bufB, [b["bufB"], b["Dw0"]])]
                pbanks = [(pA, b["pA"]), (pB, b["pB"]), (pV, b["pV"])]
                pyb = [(pSW[:, 0:512], [b["pS0"]]), (pSW[:, 512:1024], [b["pSW1"]]), (pO[:, 0:512], [b["pO"]]), (pT[:, :].bitcast(F32), [b["pT"]])]
                gT = gt[:, 0:4, :].rearrange("p a t -> p (a t)").bitcast(BF16).rearrange("p (g t) -> p g t", g=4)
                gTb = bgt[0:4]
                sa = gt[:, 4:6, :].rearrange("p a t -> p (a t)")
                pi = 0
                yi = 0
                for gi, j0 in enumerate(range(0, 22, 4)):
                    G = min(4, 22 - j0)
                    wt, wtb = slots[gi % 2]
                    wflat = wt[:].rearrange("p k n -> p (k n)")
                    w1 = wflat[:, 0:8192].rearrange("p (k a n) -> p k a n", k=8, a=2)
                    w2v = wflat[:, 8192:12288].rearrange("p (g n) -> p g n", g=4)
                    DMA("pool", w1[:, :, 0, 0:G * 128], w13_d[l, :, j0 * 128:(j0 + G) * 128].rearrange("(kc p) n -> p kc n", p=128), (), wtb)
                    DMA("pool", w1[:, :, 1, 0:G * 128], w13_d[l, :, DFF + j0 * 128:DFF + (j0 + G) * 128].rearrange("(kc p) n -> p kc n", p=128), (), wtb)
                    DMA("pool", w2v[:, 0:G, :], w2_d[l, j0 * 128:(j0 + G) * 128, :].rearrange("(g p) n -> p g n", p=128), (), wtb)
                    for (t0, n, blks) in tb_list:
                        xb = [bXT[i] for i in blks]
                        mb = [bmix[i] for i in blks]
                        for jj in range(G):
                            pa, pab = pbanks[pi % 3]
                            pb2, pbb = pbanks[(pi + 1) % 3]
                            pi += 2
                            for kc in range(8):
                                MM(pa[:, 0:n], w1[:, kc, 0, jj * 128:(jj + 1) * 128], mix[:, kc, t0:t0 + n], kc == 0, kc == 7, mb + wtb, [pab])
                            for kc in range(8):
                                MM(pb2[:, 0:n], w1[:, kc, 1, jj * 128:(jj + 1) * 128], mix[:, kc, t0:t0 + n], kc == 0, kc == 7, mb + wtb, [pbb])
                            ACT(sa[:, 0:n], pa[:, 0:n], AF.Silu, [pab], bgt[4:6])
                            TT("dve", gT[:, jj, 0:n], sa[:, 0:n], pb2[:, 0:n], ALU.mult, bgt[4:6] + [pbb], gTb)
                        for ncn in range(8):
                            py, pyb_ = pyb[yi % 4]
                            yi += 1
                            for jj in range(G):
                                MM(py[:, 0:n], w2v[:, jj, ncn * 128:(ncn + 1) * 128], gT[:, jj, 0:n], jj == 0, jj == G - 1, gTb + wtb, pyb_)
                            jm = 1 if blks == [0] else 0
                            STT(XT[:, ncn, t0:t0 + n], py[:, 0:n], PAR[:, l, jm, 5, ncn:ncn + 1], XT[:, ncn, t0:t0 + n], ALU.mult, ALU.add,
                                pyb_ + [b["PAR"]] + xb, xb)
                relA, relB = [], []

            stage(30)
            outb = []
            for blk in range(1, NB):
                t0 = blk * BS
                xs = XT[:, :, t0:t0 + BS]
                ACT(nt[:], xs, AF.Square, [bXT[blk]], NTB)
                for kc in range(8):
                    MM(pV[:, 0:BS], ones_f[:], nt[:, kc, :], kc == 0, kc == 7, NTB + [b["const"]], [b["pV"]])
                rsqrt_psum(pV[:, 0:BS], BS, 1.0 / D, [b["pV"], b["const"]], pV[:, 256:512], [b["pV"]], gt[:, 9, :], [bgt[9]])
                TT("dve", nt[:], xs, pV[:, 256:512].unsqueeze(1).broadcast_to([128, 8, BS]), ALU.mult, [bXT[blk], b["pV"]], NTB)
                for kc in range(8):
                    ACT(nt[:, kc, :], nt[:, kc, :], AF.Identity, NTB + [b["par"], b["lpar"]], NTB, scale=fng[:, kc:kc + 1])
                for tl in range(2):
                    for half in range(2):
                        pb, bb = (pA, b["pA"]) if half == 0 else (pB, b["pB"])
                        for k4 in range(4):
                            kc = half * 4 + k4
                            TR(pb[:, k4 * 128:(k4 + 1) * 128], nt[:, kc, tl * 128:(tl + 1) * 128], ident_f[:], NTB + [b["const"]], [bb])
                        ob = mix[:, tl, 0:2048].bitcast(F32)
                        CP("act" if half == 0 else "dve", ob[:, half * 512:(half + 1) * 512], pb[:, :], [bb], bmix + [bosb[tl]])
                    r0 = (blk - 1) * BS + tl * 128
                    DMA("sp", y_d[r0:r0 + 128, :], ob[:], [bosb[tl]], [b["y"]])

        except _Stop:
            pass
        if dumps:
            env = dict(locals())
            ALLB = list(b.values()) + bXT + bmix + bgt + bS + bpW + bdg
            bdst = Buf("dstg")
            off = 0
            for (ap, n) in dumps(env):
                dst = dbg_d[:, off:off + n]
                stv = dstg[:, 0:n]
                if len(ap.shape) == 3:
                    dst = dst.rearrange("p (a t) -> p a t", a=ap.shape[1])
                    stv = stv.rearrange("p (a t) -> p a t", a=ap.shape[1])
                S.op("dve", lambda h, ap=ap, stv=stv: h.tensor_copy(out=stv, in_=ap), ALLB + [bdst], [bdst])
                S.dma("sp", lambda h, stv=stv, dst=dst: h.dma_start(out=dst, in_=stv), [bdst], [b["dbg"], bdst])
                off += n
        S.wait_all("sp", [b["y"]])
        if dbg:
            S.wait_all("sp", [b["dbg"]])
        S.emit(block)
    return nc


def _col(v):
    v = np.asarray(v)
    lead = v.shape[:-1]
    n = v.shape[-1] // 128
    r = v.reshape(lead + (n, 128))
    return np.ascontiguousarray(np.moveaxis(r, -1, 0))


def prep_inputs(inp):
    f = lambda a: np.ascontiguousarray(np.asarray(a, dtype=np.float32))
    x, c, ctx, c_ctx = f(inp["x"]), f(inp["c"]), f(inp["ctx"]), f(inp["c_ctx"])
    w_in = f(inp["w_in"])
    w_in_odd = w_in.copy()
    w_in_odd[:, :, 512:1024] = w_in[:, :, 1024:1536]
    w_in_odd[:, :, 1024:1536] = w_in[:, :, 512:1024]
    lbl = f(inp["hgrn_lb_logits"])
    ws = f(inp["gmlp_w_s"])
    bsv = f(inp["gmlp_b_s"])
    dww = f(inp["conv_dw_w"])
    common = {
        "ada_w": f(inp["ada_w"]),
        "ada_b": _col(f(inp["ada_b"])),
        "n1g": _col(f(inp["norm1_g"])),
        "n2g": _col(f(inp["norm2_g"])),
        "fng": _col(f(inp["final_norm_g"])),
        "ong": _col(f(inp["hgrn_onorm_g"])),
        "gln": np.ascontiguousarray(np.broadcast_to(np.stack([f(inp["gmlp_ln_g"]), f(inp["gmlp_ln_b"])], axis=1)[None], (128, L, 2, 256))),
        "cpar": np.ascontiguousarray(np.stack([_col(f(inp["conv_dw_b"])), _col(f(inp["conv_ln_g"])), _col(f(inp["conv_ln_b"])),
                                               _col(f(inp["conv_pw_b"]))], axis=2)),
        "pww": f(inp["conv_pw_w"]),
        "w_out": f(inp["w_out"]),
        "w13": f(inp["ffn_w13"]),
        "w2": f(inp["ffn_w2"]),
    }
    per_par = []
    for par in range(2):
        m = par == 1
        ws_p = ws[:, :, ::-1, ::-1] if m else ws
        bs_p = bsv[:, :, ::-1] if m else bsv
        dw_p = dww[:, ::-1, :] if m else dww
        lb_p = lbl[:, ::-1, :] if m else lbl
        d = dict(common)
        d["w_in"] = w_in_odd if m else w_in
        d["lbl"] = _col(lb_p)
        d["wsT"] = np.ascontiguousarray(np.transpose(ws_p, (3, 0, 1, 2)))
        bsl = np.zeros((128, L, 2, 128), np.float32)
        for ch in range(2):
            for hh in range(2):
                bsl[hh * 64:(hh + 1) * 64, :, ch, :] = bs_p[:, ch * 2 + hh, :][None]
        d["bs"] = bsl
        d["dww"] = np.ascontiguousarray(np.transpose(dw_p.reshape(L, 31, 2, 128), (3, 0, 2, 1)))
        per_par.append(d)
    ins = []
    for core in range(8):
        bi, par = core // 2, core % 2
        d = dict(per_par[par])
        if par == 0:
            xl = np.concatenate([ctx[bi], x[bi, 0:2048]], axis=0)
        else:
            xl = np.concatenate([ctx[bi, ::-1], x[bi, 4095:2047:-1]], axis=0)
        d["x"] = np.ascontiguousarray(xl)
        d["cc"] = np.ascontiguousarray(np.stack([_col(c[bi]), _col(c_ctx)], axis=-1))
        sel = np.zeros((128, 2), np.float32)
        sel[:, 1 - par] = 1.0
        d["sel"] = sel
        ins.append(d)
    return ins


_NC = None


def kernel(**inputs):
    global _NC
    ins = prep_inputs(inputs)
    if _NC is None:
        _NC = build()
    res = run_bass_kernel_spmd(_NC, ins, core_ids=list(range(8)))
    out = np.zeros((4, 4096, D), np.float32)
    for core in range(8):
        bi, par = core // 2, core % 2
        y = res.results[core]["y"]
        if par == 0:
            out[bi, 0:2048] = y
        else:
            out[bi, 4095:2047:-1] = y
    return out
```

```python
import numpy as np
from contextlib import ExitStack
import concourse.bass as bass
import concourse.mybir as mybir
from concourse.bass_utils import run_bass_kernel_spmd

F32 = mybir.dt.float32
BF16 = mybir.dt.bfloat16
I32 = mybir.dt.int32
AF = mybir.ActivationFunctionType
ALU = mybir.AluOpType

L = 2
D = 1024
T = 2304
NB = 9
BS = 256
DFF = 2816
EPS = 1e-6


class Buf:
    def __init__(self, name, excl=False):
        self.name = name
        self.w = None
        self.r = {}
        self.excl = excl


class Sched:
    ENG = ("pe", "act", "dve", "pool", "sp")

    def __init__(self, nc):
        self.nc = nc
        self.prog = {e: [] for e in self.ENG}
        self.NR = 8
        self.units = list(self.ENG) + [f"d_{q}{r}" for q in ("sp", "pool") for r in range(self.NR)] + ["cc"]
        self.dcount = {"sp": 0, "pool": 0}
        self.cnt = {u: 0 for u in self.units}
        self.mult = {u: (16 if u.startswith("d_") else 1) for u in self.units}
        self.seen = {e: {} for e in self.ENG}
        self.sem = {}

    def _deps(self, e, reads, writes):
        deps = {}
        for b in reads:
            if b.w is not None:
                f, i = b.w
                deps[f] = max(deps.get(f, 0), i)
        for b in writes:
            if b.w is not None:
                f, i = b.w
                deps[f] = max(deps.get(f, 0), i)
            for f, i in b.r.items():
                deps[f] = max(deps.get(f, 0), i)
        for f, i in deps.items():
            if f == e and e == "pe":
                continue
            if self.seen[e].get(f, 0) >= i:
                continue
            self.seen[e][f] = i
            self.prog[e].append(("wait", f, i * self.mult[f]))

    def op(self, e, fn, reads=(), writes=(), inc=True):
        ex = [r for r in reads if r.excl]
        if ex:
            writes = list(writes) + [r for r in ex if r not in writes]
            reads = [r for r in reads if not r.excl]
        self._deps(e, reads, writes)
        idx = self.cnt[e] + 1
        if inc:
            self.cnt[e] = idx
        self.prog[e].append(("inst", fn, e if inc else None, 1))
        for b in reads:
            b.r[e] = max(b.r.get(e, 0), idx)
        for b in writes:
            b.w = (e, idx)
            b.r = {}

    def dma(self, q, fn, reads=(), writes=(), unit=None):
        if unit is None:
            d = f"d_{q}{self.dcount[q] % self.NR}"
            self.dcount[q] += 1
            if self.cnt[d] > 0 and self.seen[q].get(d, 0) < self.cnt[d]:
                self.seen[q][d] = self.cnt[d]
                self.prog[q].append(("wait", d, self.cnt[d] * self.mult[d]))
        else:
            d = unit
        self._deps(q, reads, writes)
        self.cnt[d] += 1
        idx = self.cnt[d]
        self.prog[q].append(("inst", fn, d, self.mult[d]))
        for b in reads:
            b.r[d] = max(b.r.get(d, 0), idx)
        for b in writes:
            b.w = (d, idx)
            b.r = {}

    def wait_all(self, e, bufs):
        self._deps(e, bufs, ())

    def emit(self, block):
        S = self

        def run(e, h):
            for it in S.prog[e]:
                if it[0] == "wait":
                    h.wait_ge(S.sem[it[1]], it[2])
                else:
                    ins = it[1](h)
                    if it[2] is not None:
                        ins.then_inc(S.sem[it[2]], it[3])

        @block.tensor
        def _(h):
            run("pe", h)

        @block.scalar
        def _(h):
            run("act", h)

        @block.vector
        def _(h):
            run("dve", h)

        @block.gpsimd
        def _(h):
            run("pool", h)

        @block.sync
        def _(h):
            run("sp", h)


class _Stop(Exception):
    pass


def build(layers=(0, 1), dbg=None, stop=None, dumps=None):
    nc = bass.Bass("TRN2", target_bir_lowering=False)

    def din(name, shape, dt=F32):
        return nc.dram_tensor(name, list(shape), dt, kind="ExternalInput").ap()

    x_d = din("x", [T, D])
    cc_d = din("cc", [128, 8, 2])
    adaw_d = din("ada_w", [L, D, 6 * D])
    adab_d = din("ada_b", [128, L, 48])
    n1g_d = din("n1g", [128, L, 8])
    n2g_d = din("n2g", [128, L, 8])
    fng_d = din("fng", [128, 8])
    win_d = din("w_in", [L, D, 3584])
    lbl_d = din("lbl", [128, L, 2, 4])
    ong_d = din("ong", [128, L, 4])
    gln_d = din("gln", [128, L, 2, 256])
    wsT_d = din("wsT", [128, L, 4, 128])
    bs_d = din("bs", [128, L, 2, 128])
    dww_d = din("dww", [128, L, 2, 31])
    cpar_d = din("cpar", [128, L, 4, 2])
    pww_d = din("pww", [L, 256, 256])
    wout_d = din("w_out", [L, D, D])
    w13_d = din("w13", [L, D, 2 * DFF])
    w2_d = din("w2", [L, DFF, D])
    sel_d = din("sel", [128, 2])
    y_d = nc.dram_tensor("y", [2048, D], F32, kind="ExternalOutput").ap()
    if dbg:
        dbg_d = nc.dram_tensor("dbg", [128, dbg], F32, kind="ExternalOutput").ap()
    cc_in = [nc.dram_tensor(f"cc_in{l}", [128, 512], BF16, kind="Internal").ap() for l in range(L)]
    cc_out = [nc.dram_tensor(f"cc_out{l}", [256, 512], BF16, kind="Internal", addr_space="Local").ap() for l in range(L)]

    bar_in = [nc.dram_tensor(f"bar_in{l}", [128, 16], BF16, kind="Internal").ap() for l in range(L)]
    bar_out = [nc.dram_tensor(f"bar_out{l}", [256, 16], BF16, kind="Internal", addr_space="Local").ap() for l in range(L)]

    es = ExitStack()
    with es:
        def sb(name, shape, dt):
            return es.enter_context(nc.sbuf_tensor("sb_" + name, list(shape), dt))

        def pst(name, shape, dt):
            return es.enter_context(nc.psum_tensor("ps_" + name, list(shape), dt))

        S = Sched(nc)
        for u in S.units:
            S.sem[u] = es.enter_context(nc.semaphore("s_" + u))

        XT = sb("XT", [128, 8, T], F32)
        mix = sb("mix", [128, 8, T], BF16)
        bufA = sb("bufA", [128, 8, 1536], BF16)
        bufB = sb("bufB", [128, 8, 1536], BF16)
        hTb = sb("hTb", [128, 8, BS], BF16)
        gt = sb("gt", [128, 14, BS], F32)
        nt = gt[:, 0:8, :]
        scn = sb("scn", [128, 9216], BF16)
        qtT = scn[:, 0:1024].rearrange("p (h t) -> p h t", h=4)
        qhT = scn[:, 1024:2048].rearrange("p (h t) -> p h t", h=4)
        ktT = scn[:, 2048:3072].rearrange("p (h t) -> p h t", h=4)
        kt_tok = scn[:, 3072:5120].rearrange("p (a r n) -> p a r n", a=2, r=2)
        v_tok = scn[:, 5120:6144].rearrange("p (a n) -> p a n", a=2)
        dgD = scn[:, 6144:8192].rearrange("p (h c k) -> p h c k", h=4, c=4)
        yaT = scn[:, 8192:9216].rearrange("p (h t) -> p h t", h=4)
        Dw1 = scn[:, 0:3968].rearrange("p (t k) -> p t k", t=31)
        GE = sb("GE", [128, 4, 2, 4], F32)
        Sst = sb("Sst", [128, 4, 128], BF16)
        A_sb = sb("A_sb", [128, 2, 4, 64], BF16)
        ident_f = sb("ident_f", [128, 128], F32)
        ident_b = sb("ident_b", [128, 128], BF16)
        ones_f = sb("ones_f", [128, 128], F32)
        ones_b = sb("ones_b", [128, 128], BF16)
        gones = sb("gones", [128, 128], F32)
        maskt = sb("maskt", [128, 2, 2, 64], mybir.dt.int8)
        rmask = sb("rmask", [128, 256], F32)
        onesw = gt[:, 0, 0:64]
        cc_s = sb("cc_s", [128, 8, 2], F32)
        sil = sb("sil", [128, 8, 2], BF16)
        PAR = sb("PAR", [128, L, 2, 6, 8], F32)
        adab = sb("adab", [128, L, 48], F32)
        n1g = sb("n1g", [128, L, 8], F32)
        n2g = sb("n2g", [128, L, 8], F32)
        fng = sb("fng", [128, 8], F32)
        lbl = sb("lbl", [128, L, 2, 4], F32)
        LB = sb("LB", [128, L, 2, 4, 3], F32)
        ong = sb("ong", [128, L, 4], F32)
        gln = sb("gln", [128, 1, 2, 256], F32)
        wsT = sb("wsT", [128, 1, 4, 128], BF16)
        bs_s = sb("bs_s", [128, 1, 2, 128], F32)
        dww = sb("dww", [128, L, 2, 31], F32)
        cpar = sb("cpar", [128, L, 4, 2], F32)
        pww = sb("pww", [128, 1, 2, 256], BF16)
        selt = sb("selt", [128, 2], F32)
        gath = sb("gath", [128, 2, 512], BF16)
        stg = gt[:, 0:4, :].rearrange("p (a b) t -> p a (b t)", a=2)
        small = sb("small", [128, 64], F32)
        small2 = sb("small2", [128, 64], F32)
        if dbg:
            dstg = sb("dstg", [128, 1024], F32)
        osb = mix[:, 0, 0:2048].bitcast(F32)

        pA = pst("pA", [128, 512], F32)
        pB = pst("pB", [128, 512], F32)
        pV = pst("pV", [128, 512], F32)
        pT = pst("pT", [128, 1024], BF16)
        pSW = pst("pSW", [128, 1024], F32)
        pO = pst("pO", [128, 1024], F32)
        pScs = [pSW[:, 0:256].rearrange("p (h t) -> p h t", h=4), pSW[:, 256:512].rearrange("p (h t) -> p h t", h=4)]
        pW = pSW[:, 512:1024].rearrange("p (h v) -> p h v", h=4)
        pOv = pO[:, :].rearrange("p (h t) -> p h t", h=4)

        block = es.enter_context(nc.Block())

        bXT = [Buf(f"XT{i}") for i in range(NB)]
        bmix = [Buf(f"mix{i}") for i in range(NB)]
        b = {n: Buf(n) for n in ("bufA", "bufB", "Dw", "hTb", "nt", "qtT", "qhT", "ktT", "kt_tok", "v_tok", "GE",
                                 "A0", "A1", "yaT", "const", "par", "PAR", "LB", "pA", "pB", "pV", "pT", "pS0", "pS1",
                                 "pO", "gath", "stg", "small", "osb", "y", "sil", "ccin", "ccout", "dbg", "Sfin", "Dw0", "lpar", "small2")}
        bgt = [Buf(f"gt{i}") for i in range(14)]
        bS = [Buf(f"S{h}") for h in range(4)]
        bpW = [Buf(f"pW{h}") for h in range(4)]
        bdg = [Buf(f"dg{h}") for h in range(4)]
        bosb = [Buf("osb0"), Buf("osb1")]
        NTB = bgt[0:8]
        STB = bgt[0:4]
        bA = [b["A0"], b["A1"]]
        for nm in ("pA", "pB", "pV", "pT", "pS0", "pO"):
            b[nm].excl = True
        b["pS1"] = b["pS0"]
        b["pSW1"] = Buf("pSW1", excl=True)
        bpS = [b["pS0"], b["pS0"]]
        Wreg = [(pA[:, 0:128], b["pA"]), (pB[:, 0:128], b["pB"]), (pV[:, 0:128], b["pV"]), (pSW[:, 512:640], b["pSW1"])]
        bpW = [w[1] for w in Wreg]

        def MM(out, lhsT, rhs, start, stop, reads, writes):
            S.op("pe", lambda h: h.matmul(out, lhsT, rhs, start=start, stop=stop), reads, writes, inc=stop)

        def TR(out, in_, ident, reads, writes):
            S.op("pe", lambda h: h.transpose(out, in_, ident), reads, writes, inc=True)

        def ACT(out, in_, func, reads, writes, bias=None, scale=None):
            kw = {}
            if bias is not None:
                kw["bias"] = bias
            if scale is not None:
                kw["scale"] = scale
            S.op("act", lambda h: h.activation(out=out, in_=in_, func=func, **kw), reads, writes)

        def TT(e, out, in0, in1, op, reads, writes):
            S.op(e, lambda h: h.tensor_tensor(out=out, in0=in0, in1=in1, op=op), reads, writes)

        def TS(e, out, in0, s1, s2, op0, op1, reads, writes):
            if op1 is None:
                S.op(e, lambda h: h.tensor_scalar(out=out, in0=in0, scalar1=s1, scalar2=None, op0=op0), reads, writes)
            else:
                S.op(e, lambda h: h.tensor_scalar(out=out, in0=in0, scalar1=s1, scalar2=s2, op0=op0, op1=op1), reads, writes)

        def STT(out, in0, scalar, in1, op0, op1, reads, writes):
            S.op("dve", lambda h: h.scalar_tensor_tensor(out=out, in0=in0, scalar=scalar, in1=in1, op0=op0, op1=op1), reads, writes)

        def CP(e, out, in_, reads, writes):
            if e == "act":
                S.op(e, lambda h: h.activation(out=out, in_=in_, func=AF.Identity), reads, writes)
            else:
                S.op(e, lambda h: h.tensor_copy(out=out, in_=in_), reads, writes)

        def DMA(q, out, in_, reads, writes):
            S.dma(q, lambda h: h.dma_start(out=out, in_=in_), reads, writes)

        def stage(k):
            if stop is not None and k > stop:
                raise _Stop()

        try:
            cst = [b["const"]]
            S.op("pool", lambda h: h.memset(ones_f[:], 1.0), (), cst)
            S.op("pool", lambda h: h.memset(ones_b[:], 1.0), (), cst)
            S.op("pool", lambda h: h.memset(onesw, 1.0), (), cst + [bgt[0]])
            S.op("pool", lambda h: h.memset(ident_f[:], 0.0), (), cst)
            S.op("pool", lambda h: h.affine_select(out=ident_f[:], in_=ident_f[:], pattern=[[1, 128]], compare_op=ALU.not_equal,
                                                    fill=1.0, base=0, channel_multiplier=-1), cst, cst)
            S.op("pool", lambda h: h.tensor_copy(out=ident_b[:], in_=ident_f[:]), cst, cst)
            S.op("pool", lambda h: h.memset(gones[:], 0.0), (), cst)
            S.op("pool", lambda h: h.memset(gones[0:64, 0:64], 1.0), cst, cst)
            S.op("pool", lambda h: h.memset(gones[64:128, 64:128], 1.0), cst, cst)
            S.op("pool", lambda h: h.memset(rmask[:], 1.0), (), cst)
            S.op("pool", lambda h: h.memset(rmask[:].rearrange("p (c t) -> p c t", t=64)[:, :, 0:1], 0.0), cst, cst)
            S.op("pool", lambda h: h.memset(A_sb[:], 0.0), (), bA)
            S.op("pool", lambda h: h.memset(maskt[:], 0.0), (), cst)
            for half in (0, 1):
                ps_ = slice(half * 64, half * 64 + 64)
                S.op("pool", lambda h, ps_=ps_, half=half: h.affine_select(out=maskt[ps_, 0, half, :], in_=onesw[ps_, :],
                                                                 pattern=[[1, 64]], compare_op=ALU.is_ge, fill=0.0, base=0,
                                                                 channel_multiplier=-1), cst + [bgt[0]], cst)
                S.op("pool", lambda h, ps_=ps_, half=half: h.affine_select(out=maskt[ps_, 1, half, :], in_=onesw[ps_, :],
                                                                 pattern=[[-1, 64]], compare_op=ALU.is_ge, fill=0.0, base=0,
                                                                 channel_multiplier=1), cst + [bgt[0]], cst)

            par = [b["par"], b["lpar"]]
            for dst, src in ((cc_s, cc_d), (adab, adab_d), (n1g, n1g_d), (n2g, n2g_d), (fng, fng_d), (lbl, lbl_d), (ong, ong_d),
                             (dww, dww_d), (cpar, cpar_d), (selt, sel_d)):
                DMA("sp", dst[:], src, (), par)
            S.op("dve", lambda h: h.memset(LB[:], 0.0), (), [b["LB"]])
            TT("dve", small[:, 0:8], lbl[:, 1].rearrange("p a b -> p (a b)"), lbl[:, 0].rearrange("p a b -> p (a b)"), ALU.subtract, par, [b["small"]])
            ACT(LB[:, 1, :, :, 0], small[:, 0:8].rearrange("p (a b) -> p a b", a=2), AF.Sigmoid, [b["small"]], [b["LB"]])
            for l in range(L):
                TS("dve", LB[:, l, :, :, 1], LB[:, l, :, :, 0], -1.0, 1.0, ALU.mult, ALU.add, [b["LB"]], [b["LB"]])
                TS("dve", LB[:, l, :, :, 2], LB[:, l, :, :, 0], 1.0, -1.0, ALU.mult, ALU.add, [b["LB"]], [b["LB"]])
            ACT(sil[:], cc_s[:], AF.Silu, par, [b["sil"]])

            pM = pA[:, 0:96].rearrange("p (n j) -> p n j", j=2)
            ada_slot = [bufA[:, :, 0:512], bufA[:, :, 512:1024], bufA[:, :, 1024:1536], bufB[:, :, 0:512], bufB[:, :, 512:1024], bufB[:, :, 1024:1536]]
            bslot = [Buf(f"adaslot{i}") for i in range(6)]
            si = 0
            for l in layers:
                for ng in range(12):
                    sl, bsl = ada_slot[si % 6], bslot[si % 6]
                    si += 1
                    DMA("pool", sl, adaw_d[l, :, ng * 512:(ng + 1) * 512].rearrange("(kc p) n -> p kc n", p=128), (), [bsl])
                    for nn in range(4):
                        for kc in range(8):
                            MM(pM[:, ng * 4 + nn, :], sl[:, kc, nn * 128:(nn + 1) * 128], sil[:, kc, :], kc == 0, kc == 7,
                               [bsl, b["sil"]], [b["pA"]])
                for j in range(2):
                    TT("dve", PAR[:, l, j].rearrange("p m k -> p (m k)"), pM[:, :, j], adab[:, l, :], ALU.add, [b["pA"], b["par"]], [b["PAR"]])
                    STT(PAR[:, l, j, 1, :], PAR[:, l, j, 1, :], 1.0, n1g[:, l, :], ALU.add, ALU.mult, [b["PAR"], b["par"]], [b["PAR"]])
                    STT(PAR[:, l, j, 4, :], PAR[:, l, j, 4, :], 1.0, n2g[:, l, :], ALU.add, ALU.mult, [b["PAR"], b["par"]], [b["PAR"]])
            relA = bslot[0:3]
            relB = bslot[3:6]

            stage(1)
            for tt in range(T // 128):
                st_ = gt[:, 4 * (tt % 2):4 * (tt % 2) + 4, :].rearrange("p a t -> p (a t)")
                STX = bgt[4 * (tt % 2):4 * (tt % 2) + 4]
                DMA("sp", st_, x_d[tt * 128:(tt + 1) * 128, :], (), STX)
                for half in range(2):
                    pb, bb = (pA, b["pA"]) if half == 0 else (pB, b["pB"])
                    for k4 in range(4):
                        kc = half * 4 + k4
                        TR(pb[:, k4 * 128:(k4 + 1) * 128], st_[:, kc * 128:(kc + 1) * 128], ident_f[:], STX + [b["const"]], [bb])
                    CP("act" if half == 0 else "dve", XT[:, half * 4:half * 4 + 4, tt * 128:(tt + 1) * 128],
                       pb[:, :].rearrange("p (k t) -> p k t", k=4), [bb], [bXT[tt // 2]])

            def rsqrt_psum(ps_ap, n, scale, reads_b, out_ap, out_b, tmp_ap, tmp_b):
                ACT(tmp_ap, ps_ap, AF.Ln, reads_b, tmp_b, bias=eps_col, scale=scale)
                ACT(out_ap, tmp_ap, AF.Exp, tmp_b, out_b, scale=-0.5)

            eps_t = small[:, 32:33]
            S.op("dve", lambda h: h.memset(eps_t, EPS), (), [b["const"]])
            eps_col = eps_t

            def norm_block(l, which, blk, out_ap, out_b, final=False):
                j = 1 if blk == 0 else 0
                t0 = blk * BS
                xs = XT[:, :, t0:t0 + BS]
                ACT(out_ap[:, :, :], xs, AF.Square, [bXT[blk]], out_b)
                for kc in range(8):
                    MM(pV[:, 0:BS], ones_b[:], out_ap[:, kc, :], kc == 0, kc == 7, out_b + [b["const"]], [b["pV"]])
                rsqrt_psum(pV[:, 0:BS], BS, 1.0 / D, [b["pV"], b["const"]], pV[:, 256:512], [b["pV"]], gt[:, 9, :], [bgt[9]])
                TT("dve", nt[:], xs, pV[:, 256:512].unsqueeze(1).broadcast_to([128, 8, BS]), ALU.mult, [bXT[blk], b["pV"]], NTB)
                for kc in range(8):
                    if final:
                        ACT(out_ap[:, kc, :], nt[:, kc, :], AF.Identity, NTB + [b["par"], b["lpar"]], out_b, scale=fng[:, kc:kc + 1])
                    else:
                        m = 0 if which == 1 else 3
                        ACT(out_ap[:, kc, :], nt[:, kc, :], AF.Identity, NTB + [b["PAR"]], out_b,
                            bias=PAR[:, l, j, m, kc:kc + 1], scale=PAR[:, l, j, m + 1, kc:kc + 1])

            def load_w(dst, l, c0, c1, src, wb):
                DMA("pool", dst, src[l, :, c0:c1].rearrange("(kc p) n -> p kc n", p=128), (), wb)

            def dump(ap, n, reads):
                DMA("sp", dbg_d[:, 0:n], ap, reads, [b["dbg"]])

            stage(2)
            for l in layers:
                last = (l == L - 1)
                WA = [b["bufA"]] + relA
                WB = [b["bufB"], b["Dw0"]] + relB
                load_w(bufA[:, :, 0:1024], l, 0, 1024, win_d, WA)
                load_w(bufA[:, :, 1024:1536], l, 1536, 2048, win_d, WA)
                relA = []
                lp = [b["lpar"]]
                DMA("sp", gln[:, 0], gln_d[:, l], (), lp)
                DMA("sp", bs_s[:, 0], bs_d[:, l], (), lp)
                DMA("pool", wsT[:, 0], wsT_d[:, l], (), lp)
                DMA("pool", pww[:, 0], pww_d[l].rearrange("(c p) n -> p c n", p=128), (), lp)
                load_w(bufB[:, :, 0:1024], l, 2560, 3584, win_d, WB)
                relB = []
                dwv = bufB[:, :, 1024:1536].rearrange("p k (j c) -> p k j c", j=4)
                TT("dve", dwv[:, 0:7], ident_b[:].unsqueeze(1).unsqueeze(1).broadcast_to([128, 7, 4, 128]),
                   dww[:, l, 0, 0:28].rearrange("p (k j) -> p k j", j=4).unsqueeze(3).broadcast_to([128, 7, 4, 128]), ALU.mult,
                   [b["par"], b["const"]], [b["Dw0"]])
                TT("dve", dwv[:, 7, 0:3], ident_b[:].unsqueeze(1).broadcast_to([128, 3, 128]),
                   dww[:, l, 0, 28:31].unsqueeze(2).broadcast_to([128, 3, 128]), ALU.mult, [b["par"], b["const"]], [b["Dw0"]])

                def hgrn_dir(dr, blocks, Wt, wb):
                    mid = 31 if dr == 0 else 32
                    eidx = 63 if dr == 0 else 0
                    S.op("pool", lambda hh: hh.memset(A_sb[:], 0.0), (), bA)
                    S.op("pool", lambda hh: hh.memset(kt_tok[:], 0.0), (), [b["kt_tok"]])
                    for blk in blocks:
                        j = 1 if blk == 0 else 0
                        t0 = blk * BS
                        first = (l == layers[0] and dr == 0 and blk == blocks[0])
                        if first:
                            stage(3.1)
                        norm_block(l, 1, blk, hTb, [b["hTb"]])
                        if first:
                            stage(3.2)
                        if True:
                            for tl in range(2):
                                for kc in range(8):
                                    MM(pV[:, :], hTb[:, kc, tl * 128:(tl + 1) * 128], Wt[:, kc, 1024:1536], kc == 0, kc == 7,
                                       [b["hTb"]] + wb, [b["pV"]])
                                CP("act", v_tok[:, tl, :], pV[:, :], [b["pV"]], [b["v_tok"]])
                        for hp in range(2):
                            hs2 = (2 * hp, 2 * hp + 1)
                            ctxs = []
                            for h in hs2:
                                pb, bb = (pA, b["pA"]) if h % 2 == 0 else (pB, b["pB"])
                                for kc in range(8):
                                    MM(pb[:, 0:256], Wt[:, kc, h * 128:(h + 1) * 128], hTb[:, kc, :], kc == 0, kc == 7, [b["hTb"]] + wb, [bb])
                                for kc in range(8):
                                    MM(pb[:, 256:512], Wt[:, kc, 512 + h * 128:512 + (h + 1) * 128], hTb[:, kc, :], kc == 0, kc == 7, [b["hTb"]] + wb, [bb])
                                g0 = 7 * (h % 2)
                                G = [bgt[g0 + i] for i in range(7)]
                                sl7 = [gt[:, g0 + i, :] for i in range(7)]
                                ctxs.append((h, pb, bb, G, sl7))
                            for (h, pb, bb, G, (sg, lf, bb_, dd, e1, e2, e3)) in ctxs:
                                ACT(sg, pb[:, 256:512], AF.Sigmoid, [bb], [G[0]])
                            for (h, pb, bb, G, (sg, lf, bb_, dd, e1, e2, e3)) in ctxs:
                                ACT(lf, sg, AF.Ln, [G[0], b["LB"]], [G[1]], bias=LB[:, l, dr, h, 0:1], scale=LB[:, l, dr, h, 1:2])
                            for (h, pb, bb, G, (sg, lf, bb_, dd, e1, e2, e3)) in ctxs:
                                S.op("dve", lambda hh, bb_=bb_, lf=lf: hh.tensor_tensor_scan(out=bb_, data0=rmask[:], data1=lf,
                                                                                          initial=0.0, op0=ALU.mult, op1=ALU.add),
                                     [G[1], b["const"]], [G[2]])
                            for (h, pb, bb, G, (sg, lf, bb_, dd, e1, e2, e3)) in ctxs:
                                b3 = bb_.rearrange("p (c t) -> p c t", t=64)
                                d3 = dd.rearrange("p (c t) -> p c t", t=64)
                                if dr == 0:
                                    TT("pool", d3, b3, b3[:, :, mid:mid + 1].broadcast_to([128, 4, 64]), ALU.subtract, [G[2]], [G[3]])
                                else:
                                    tmp = e1
                                    TT("pool", tmp, lf, bb_, ALU.subtract, [G[1], G[2]], [G[4]])
                                    TT("pool", d3, tmp.rearrange("p (c t) -> p c t", t=64), b3[:, :, 63:64].broadcast_to([128, 4, 64]), ALU.add,
                                       [G[4], G[2]], [G[3]])
                                    TT("pool", b3, d3, d3[:, :, mid:mid + 1].broadcast_to([128, 4, 64]), ALU.subtract, [G[3]], [G[2]])
                            for (h, pb, bb, G, (sg, lf, bb_, dd, e1, e2, e3)) in ctxs:
                                dsrc, dB, bsrc, bB = (dd, G[3], bb_, G[2]) if dr == 0 else (bb_, G[2], dd, G[3])
                                ACT(e1, dsrc, AF.Exp, [dB], [G[4]])
                                ACT(e2, dsrc, AF.Exp, [dB], [G[5]], scale=-1.0)
                                ACT(e3, bsrc, AF.Exp, [bB], [G[6]])
                            for (h, pb, bb, G, (sg, lf, bb_, dd, e1, e2, e3)) in ctxs:
                                kk = lf
                                TS("dve", kk, sg, LB[:, l, dr, h, 2:3], LB[:, l, dr, h, 1:2], ALU.mult, ALU.add, [G[0], b["LB"]], [G[1]])
                                TT("dve", ktT[:, h, :], kk, e2, ALU.mult, [G[1], G[5]], [b["ktT"]])
                                TT("dve", qtT[:, h, :], pb[:, 0:256], e1, ALU.mult, [bb, G[4]], [b["qtT"]])
                                TT("dve", qhT[:, h, :], pb[:, 0:256], e3, ALU.mult, [bb, G[6]], [b["qhT"]])
                                CP("dve", GE[:, h, 0, :], e3.rearrange("p (c t) -> p c t", t=64)[:, :, mid], [G[6]], [b["GE"]])
                                CP("dve", GE[:, h, 1, :], e1.rearrange("p (c t) -> p c t", t=64)[:, :, eidx], [G[4]], [b["GE"]])
                            for (h, pb, bb, G, sl7) in ctxs:
                                TT("dve", dgD[:, h, :, :], ident_b[:].unsqueeze(1).broadcast_to([128, 4, 128]),
                                   GE[:, h, 0, :].unsqueeze(2).broadcast_to([128, 4, 128]), ALU.mult, [b["GE"], b["const"]], [bdg[h]])
                        if first:
                            stage(3.3)
                        for tl in range(2):
                            for h in range(4):
                                TR(pT[:, h * 128:(h + 1) * 128], ktT[:, h, tl * 128:(tl + 1) * 128], ident_b[:], [b["ktT"], b["const"]], [b["pT"]])
                            CP("act", kt_tok[0:64, tl, 0, :], pT[0:64, 0:512], [b["pT"]], [b["kt_tok"]])
                            CP("act", kt_tok[64:128, tl, 1, :], pT[64:128, 0:512], [b["pT"]], [b["kt_tok"]])
                        if first:
                            stage(3.4)
                        corder = range(4) if dr == 0 else range(3, -1, -1)
                        for c in corder:
                            tl, par = c // 2, c % 2
                            cs = slice(c * 64, c * 64 + 64)
                            pSc = pScs[par]
                            for h in range(4):
                                MM(pSc[:, h, :], ktT[:, h, tl * 128:(tl + 1) * 128], qtT[:, h, cs], True, True, [b["ktT"], b["qtT"]], [bpS[par]])
                            S.op("dve", lambda hh, par=par, pSc=pSc: hh.copy_predicated(out=A_sb[:, par, :, :],
                                                                                   mask=maskt[:, dr, par, :].unsqueeze(1).broadcast_to([128, 4, 64]),
                                                                                   data=pSc[:, :, :]),
                                 [bpS[par], b["const"]], [bA[par]])
                            if first and c == 0:
                                stage(3.5)
                            for h in range(4):
                                hs = slice(h * 128, h * 128 + 128)
                                MM(pOv[:, h, cs], v_tok[:, tl, hs], A_sb[:, par, h, :], True, False, [b["v_tok"], bA[par]], [b["pO"]])
                                MM(pOv[:, h, cs], Sst[:, h, :], qhT[:, h, cs], False, True, [bS[h], b["qhT"]], [b["pO"]])
                                MM(Wreg[h][0], dgD[:, h, c, :], Sst[:, h, :], True, False, [bdg[h], bS[h]], [bpW[h]])
                                MM(Wreg[h][0], kt_tok[:, tl, par, hs], v_tok[:, tl, hs], False, True, [b["kt_tok"], b["v_tok"]], [bpW[h]])
                                ACT(Sst[:, h, :], Wreg[h][0], AF.Identity, [bpW[h], b["GE"]], [bS[h]], scale=GE[:, h, 1, c:c + 1])
                        if first:
                            stage(3.6)
                        if dr == 0:
                            CP("act", mix[:, 0:4, t0:t0 + BS], pOv, [b["pO"]], [bmix[blk]])
                            if first:
                                stage(3.7)
                        else:
                            osum = gt[:, 0:4, :]
                            TT("dve", osum, pOv, mix[:, 0:4, t0:t0 + BS], ALU.add, [b["pO"], bmix[blk]], bgt[0:4])
                            osq = gt[:, 4:8, :]
                            osq_b = osq.rearrange("p a t -> p (a t)").bitcast(BF16)[:, 0:1024]
                            ACT(osq_b.rearrange("p (a t) -> p a t", a=4), osum, AF.Square, bgt[0:4], bgt[4:8])
                            for hp in range(2):
                                MM(pO[:, hp * 512:(hp + 1) * 512], ones_b[:], osq_b[:, hp * 512:(hp + 1) * 512], True, True,
                                   bgt[4:8] + [b["const"]], [b["pO"]])
                            ACT(osq.rearrange("p a t -> p (a t)"), pO[:, :], AF.Ln, [b["pO"], b["const"]], bgt[4:8], bias=eps_col, scale=1.0 / 128)
                            ACT(osq.rearrange("p a t -> p (a t)"), osq.rearrange("p a t -> p (a t)"), AF.Exp, bgt[4:8], bgt[4:8], scale=-0.5)
                            TT("dve", osum, osum, osq, ALU.mult, bgt[0:4] + bgt[4:8], bgt[0:4])
                            for h in range(4):
                                pb, bb = (pA, b["pA"]) if h % 2 == 0 else (pB, b["pB"])
                                for kc in range(8):
                                    MM(pb[:, 0:256], bufB[:, kc, 1024 + h * 128:1024 + (h + 1) * 128], hTb[:, kc, :], kc == 0, kc == 7, [b["hTb"]] + WBcur, [bb])
                                ACT(gt[:, 8, :], pb[:, 0:256], AF.Silu, [bb], [bgt[8]])
                                STT(yaT[:, h, :], osum[:, h, :], ong[:, l, h:h + 1], gt[:, 8, :], ALU.mult, ALU.mult, bgt[0:4] + [bgt[8], b["par"]], [b["yaT"]])
                            for ncn in range(8):
                                pb, bb = (pA, b["pA"]) if ncn % 2 == 0 else (pB, b["pB"])
                                for kk_ in range(8):
                                    rhs = yaT[:, kk_, :] if kk_ < 4 else mix[:, kk_, t0:t0 + BS]
                                    MM(pb[:, 0:256], bufB[:, kk_, ncn * 128:(ncn + 1) * 128], rhs, kk_ == 0, kk_ == 7,
                                       [b["yaT"], bmix[blk]] + WBcur, [bb])
                                STT(XT[:, ncn, t0:t0 + BS], pb[:, 0:256], PAR[:, l, j, 2, ncn:ncn + 1], XT[:, ncn, t0:t0 + BS], ALU.mult, ALU.add,
                                    [bb, b["PAR"], bXT[blk]], [bXT[blk]])

                stage(3 + 10 * l)
                for h in range(4):
                    S.op("pool", lambda hh, h=h: hh.memset(Sst[:, h, :], 0.0), (), [bS[h]])
                hgrn_dir(0, list(range(NB)), bufA, [b["bufA"]])
                stage(4 + 10 * l)
                DMA("sp", cc_in[l], Sst[:].rearrange("p h v -> p (h v)"), bS, [b["ccin"]])
                S.dma("pool", lambda hh, l=l: hh.collective_compute("AllGather", ALU.bypass, replica_groups=[[0, 1], [2, 3], [4, 5], [6, 7]],
                                                                    ins=[cc_in[l]], outs=[cc_out[l]]), [b["ccin"]], [b["ccout"]], unit="cc")
                S.dma("pool", lambda hh, l=l: hh.collective_compute("AllGather", ALU.bypass, replica_groups=[[0, 1], [2, 3], [4, 5], [6, 7]],
                                                                    ins=[bar_in[l]], outs=[bar_out[l]]), [b["ccout"]], [b["ccout"]], unit="cc")

                stage(5 + 10 * l)
                load_w(bufA[:, :, 512:1024], l, 1024, 1536, win_d, [b["bufA"]])
                DW1B = [b["qtT"], b["qhT"], b["ktT"], b["kt_tok"]]
                TT("dve", Dw1[:, :, :], ident_b[:].unsqueeze(1).broadcast_to([128, 31, 128]),
                   dww[:, l, 1, :].unsqueeze(2).broadcast_to([128, 31, 128]), ALU.mult, [b["par"], b["const"]], DW1B)
                bc_blocks = list(range(1 if last else 0, NB))
                zp = gt[:, 0:3, :].rearrange("p a t -> p (a t)").bitcast(BF16) if False else None
                for blk in bc_blocks:
                    j = 1 if blk == 0 else 0
                    t0 = blk * BS
                    norm_block(l, 1, blk, hTb, [b["hTb"]])
                    wbB = [b["bufB"]]
                    for ch in range(2):
                        for kc in range(8):
                            MM(pA[:, ch * 256:(ch + 1) * 256], bufB[:, kc, ch * 128:(ch + 1) * 128], hTb[:, kc, :], kc == 0, kc == 7, [b["hTb"]] + wbB, [b["pA"]])
                    gu = gt[:, 0:2, :]
                    ACT(gu, pA[:, :].rearrange("p (c t) -> p c t", c=2), AF.Gelu_apprx_tanh, [b["pA"]], bgt[0:2])
                    TL = []
                    for tl in range(2):
                        sm = small if tl == 0 else small2
                        TL.append(dict(tl=tl, pv=pV[:, tl * 256:(tl + 1) * 256], gv=gt[:, 2 if tl == 0 else 9, :], vn=gt[:, 3 if tl == 0 else 10, :],
                                       vln=gt[:, 4 if tl == 0 else 11, :].bitcast(BF16)[:, 0:256],
                                       bg=bgt[2 if tl == 0 else 9], bn=bgt[3 if tl == 0 else 10], bl=bgt[4 if tl == 0 else 11],
                                       st6=sm[:, 0:24].rearrange("p (g s) -> p g s", g=4), mv=sm[:, 24:32].rearrange("p (g s) -> p g s", g=4),
                                       lnv=sm[:, 36:40], rs4=sm[:, 40:44], bs=b["small"] if tl == 0 else b["small2"]))
                    for t_ in TL:
                        tl = t_["tl"]
                        for kc in range(8):
                            MM(t_["pv"], hTb[:, kc, tl * 128:(tl + 1) * 128], bufB[:, kc, 256:512], kc == 0, kc == 7, [b["hTb"]] + wbB, [b["pV"]])
                    for t_ in TL:
                        ACT(t_["gv"], t_["pv"], AF.Gelu_apprx_tanh, [b["pV"]], [t_["bg"]])
                    for t_ in TL:
                        for g in range(4):
                            S.op("dve", lambda hh, g=g, t_=t_: hh.bn_stats(out=t_["st6"][:, g, :], in_=t_["gv"][:, g * 64:(g + 1) * 64]), [t_["bg"]], [t_["bs"]])
                            S.op("dve", lambda hh, g=g, t_=t_: hh.bn_aggr(out=t_["mv"][:, g, :], in_=t_["st6"][:, g, :]), [t_["bs"]], [t_["bs"]])
                    for t_ in TL:
                        ACT(t_["lnv"], t_["mv"][:, :, 1], AF.Ln, [t_["bs"], b["const"]], [t_["bs"]], bias=eps_col, scale=1.0)
                    for t_ in TL:
                        ACT(t_["rs4"], t_["lnv"], AF.Exp, [t_["bs"]], [t_["bs"]], scale=-0.5)
                    for t_ in TL:
                        for g in range(4):
                            TS("dve", t_["vn"][:, g * 64:(g + 1) * 64], t_["gv"][:, g * 64:(g + 1) * 64], t_["mv"][:, g, 0:1], t_["rs4"][:, g:g + 1],
                               ALU.subtract, ALU.mult, [t_["bg"], t_["bs"]], [t_["bn"]])
                        TT("dve", t_["vn"], t_["vn"], gln[:, 0, 0, :], ALU.mult, [t_["bn"], b["par"], b["lpar"]], [t_["bn"]])
                        TT("dve", t_["vln"], t_["vn"], gln[:, 0, 1, :], ALU.add, [t_["bn"], b["par"], b["lpar"]], [t_["bl"]])
                    for t_ in TL:
                        tl = t_["tl"]
                        for g in range(4):
                            po = (g % 2) * 64
                            MM(pB[po:po + 64, (g // 2) * 256 + tl * 128:(g // 2) * 256 + (tl + 1) * 128], t_["vln"][:, g * 64:(g + 1) * 64], wsT[:, 0, g, :],
                               True, True, [t_["bl"], b["par"], b["lpar"]], [b["pB"]])
                    pB3 = pB[:, :].rearrange("p (c t) -> p c t", c=2)
                    mx = gt[:, 5:7, :]
                    for tl in range(2):
                        TT("dve", mx[:, :, tl * 128:(tl + 1) * 128], pB3[:, :, tl * 128:(tl + 1) * 128], bs_s[:, 0, :, :], ALU.add, [b["pB"], b["par"]], bgt[5:7])
                    TT("dve", mix[:, 4:6, t0:t0 + BS], mx, gu, ALU.mult, bgt[5:7] + bgt[0:2], [bmix[blk]])
                    nseg = 1 if blk == 0 else 4
                    sl_ = BS // nseg
                    pTf = pT[:, :].bitcast(F32)
                    CH = []
                    for ch in range(2):
                        sl = [0, 1, 2, 3, 4, 5] if ch == 0 else [9, 10, 11, 12, 13, 6]
                        zpc = gt[:, 7 + ch, :].bitcast(BF16)
                        CH.append(dict(ch=ch, pin=(pA, b["pA"]) if ch == 0 else (pB, b["pB"]), pcv=(pV, b["pV"]) if ch == 0 else (pTf, b["pT"]),
                                       sgm=gt[:, sl[0], :], cz=gt[:, sl[1], :], var=gt[:, sl[2], :], mean=gt[:, sl[3], :], msq=gt[:, sl[4], :],
                                       czs=gt[:, sl[5], :].bitcast(BF16)[:, 0:256], B=[bgt[i] for i in sl], zp=zpc, zb=bgt[7 + ch],
                                       zpv=zpc[:, 0:nseg * (sl_ + 30)].rearrange("p (s t) -> p s t", s=nseg)))
                    for c_ in CH:
                        ch = c_["ch"]; pin, pinb = c_["pin"]
                        for kc in range(8):
                            MM(pin[:, 0:256], bufB[:, kc, 512 + ch * 128:512 + (ch + 1) * 128], hTb[:, kc, :], kc == 0, kc == 7, [b["hTb"]] + wbB, [pinb])
                        for kc in range(8):
                            MM(pin[:, 256:512], bufB[:, kc, 768 + ch * 128:768 + (ch + 1) * 128], hTb[:, kc, :], kc == 0, kc == 7, [b["hTb"]] + wbB, [pinb])
                    for c_ in CH:
                        ACT(c_["sgm"], c_["pin"][0][:, 256:512], AF.Sigmoid, [c_["pin"][1]], [c_["B"][0]])
                    for c_ in CH:
                        S.op("pool", lambda hh, zp=c_["zp"]: hh.memset(zp[:, 0:512], 0.0), (), [c_["zb"]])
                    for c_ in CH:
                        TT("dve", c_["zpv"][:, :, 15:15 + sl_], c_["pin"][0][:, 0:256].rearrange("p (s t) -> p s t", s=nseg),
                           c_["sgm"].rearrange("p (s t) -> p s t", s=nseg), ALU.mult, [c_["pin"][1], c_["B"][0]], [c_["zb"]])
                    for c_ in CH:
                        ch = c_["ch"]; pcv, pcvb = c_["pcv"]
                        for tp in range(31):
                            dwt = bufB[:, tp // 4, 1024 + (tp % 4) * 128:1024 + (tp % 4 + 1) * 128] if ch == 0 else Dw1[:, tp, :]
                            MM(pcv[:, 0:256].rearrange("p (s t) -> p s t", s=nseg), dwt, c_["zpv"][:, :, tp:tp + sl_], tp == 0, tp == 30,
                               [c_["zb"]] + ([b["Dw0"]] if ch == 0 else DW1B), [pcvb])
                    for c_ in CH:
                        ACT(c_["cz"], c_["pcv"][0][:, 0:256], AF.Identity, [c_["pcv"][1], b["par"]], [c_["B"][1]], bias=cpar[:, l, 0, c_["ch"]:c_["ch"] + 1])
                    for c_ in CH:
                        ACT(c_["var"], c_["cz"], AF.Square, [c_["B"][1]], [c_["B"][2]])
                    for c_ in CH:
                        pcv, pcvb = c_["pcv"]
                        MM(pcv[:, 0:256], gones[:], c_["cz"], True, True, [c_["B"][1], b["const"]], [pcvb])
                        MM(pcv[:, 256:512], gones[:], c_["var"], True, True, [c_["B"][2], b["const"]], [pcvb])
                    for c_ in CH:
                        ACT(c_["mean"], c_["pcv"][0][:, 0:256], AF.Identity, [c_["pcv"][1]], [c_["B"][3]], scale=1.0 / 64)
                    for c_ in CH:
                        ACT(c_["msq"], c_["mean"], AF.Square, [c_["B"][3]], [c_["B"][4]])
                    for c_ in CH:
                        STT(c_["var"], c_["pcv"][0][:, 256:512], 1.0 / 64, c_["msq"], ALU.mult, ALU.subtract, [c_["pcv"][1], c_["B"][4]], [c_["B"][2]])
                    for c_ in CH:
                        ACT(c_["var"], c_["var"], AF.Ln, [c_["B"][2], b["const"]], [c_["B"][2]], bias=eps_col, scale=1.0)
                    for c_ in CH:
                        ACT(c_["var"], c_["var"], AF.Exp, [c_["B"][2]], [c_["B"][2]], scale=-0.5)
                    for c_ in CH:
                        TT("dve", c_["cz"], c_["cz"], c_["mean"], ALU.subtract, [c_["B"][1], c_["B"][3]], [c_["B"][1]])
                        TT("dve", c_["cz"], c_["cz"], c_["var"], ALU.mult, [c_["B"][1], c_["B"][2]], [c_["B"][1]])
                    for c_ in CH:
                        ch = c_["ch"]
                        ACT(c_["czs"], c_["cz"], AF.Silu, [c_["B"][1], b["par"]], [c_["B"][5]], bias=cpar[:, l, 2, ch:ch + 1], scale=cpar[:, l, 1, ch:ch + 1])
                    for co in range(2):
                        for ci in range(2):
                            MM(pB[:, co * 256:(co + 1) * 256], pww[:, 0, ci, co * 128:(co + 1) * 128], CH[ci]["czs"],
                               ci == 0, ci == 1, [CH[0]["B"][5], CH[1]["B"][5], b["par"], b["lpar"]], [b["pB"]])
                        ACT(mix[:, 6 + co, t0:t0 + BS], pB[:, co * 256:(co + 1) * 256], AF.Identity, [b["pB"], b["par"]], [bmix[blk]],
                            bias=cpar[:, l, 3, co:co + 1])

                stage(6 + 10 * l)
                WBcur = [b["bufB"], b["Dw0"]]
                DMA("pool", bufB[:, :, 0:1024], wout_d[l].rearrange("(kc p) n -> p kc n", p=128), (), WBcur)
                load_w(bufB[:, :, 1024:1536], l, 2048, 2560, win_d, WBcur)
                DMA("sp", gath[:], cc_out[l].rearrange("(r p) n -> p r n", p=128), [b["ccout"], bmix[NB - 1], b["pB"]], [b["gath"]])
                TS("dve", stg[:, 0, :], gath[:, 0, :], selt[:, 0:1], None, ALU.mult, None, [b["gath"], b["par"]], STB)
                for h in range(4):
                    STT(Sst[:, h, :], gath[:, 1, h * 128:(h + 1) * 128], selt[:, 1:2], stg[:, 0, h * 128:(h + 1) * 128], ALU.mult, ALU.add,
                        [b["gath"], b["par"]] + STB, [bS[h]])
                hgrn_dir(1, list(range(NB - 1, 0, -1)), bufA, [b["bufA"]])
                if not last:
                    for h in range(4):
                        S.op("pool", lambda hh, h=h: hh.memset(Sst[:, h, :], 0.0), (), [bS[h]])
                    hgrn_dir(1, [0], bufA, [b["bufA"]])

                stage(7 + 10 * l)
                ffn_blocks = list(range(1 if last else 0, NB))
                for blk in ffn_blocks:
                    norm_block(l, 2, blk, mix[:, :, blk * BS:(blk + 1) * BS], [bmix[blk]])
                tb_list = ([] if last else [(0, 256, [0])]) + [(256 + i * 512, 512, [1 + 2 * i, 2 + 2 * i]) for i in range(4)]
                slots = [(bufA, [b["bufA"]]), (bufB, [b["bufB"], b["Dw0"]])]
                pbanks = [(pA, b["pA"]), (pB, b["pB"]), (pV, b["pV"])]
                pyb = [(pSW[:, 0:512], [b["pS0"]]), (pSW[:, 512:1024], [b["pSW1"]]), (pO[:, 0:512], [b["pO"]]), (pT[:, :].bitcast(F32), [b["pT"]])]
                gT = gt[:, 0:4, :].rearrange("p a t -> p (a t)").bitcast(BF16).rearrange("p (g t) -> p g t", g=4)
                gTb = bgt[0:4]
                sa = gt[:, 4:6, :].rearrange("p a t -> p (a t)")
                pi = 0
                yi = 0
                for gi, j0 in enumerate(range(0, 22, 4)):
                    G = min(4, 22 - j0)
                    wt, wtb = slots[gi % 2]
                    wflat = wt[:].rearrange("p k n -> p (k n)")
                    w1 = wflat[:, 0:8192].rearrange("p (k a n) -> p k a n", k=8, a=2)
                    w2v = wflat[:, 8192:12288].rearrange("p (g n) -> p g n", g=4)
                    DMA("pool", w1[:, :, 0, 0:G * 128], w13_d[l, :, j0 * 128:(j0 + G) * 128].rearrange("(kc p) n -> p kc n", p=128), (), wtb)
                    DMA("pool", w1[:, :, 1, 0:G * 128], w13_d[l, :, DFF + j0 * 128:DFF + (j0 + G) * 128].rearrange("(kc p) n -> p kc n", p=128), (), wtb)
                    DMA("pool", w2v[:, 0:G, :], w2_d[l, j0 * 128:(j0 + G) * 128, :].rearrange("(g p) n -> p g n", p=128), (), wtb)
                    for (t0, n, blks) in tb_list:
                        xb = [bXT[i] for i in blks]
                        mb = [bmix[i] for i in blks]
                        for jj in range(G):
                            pa, pab = pbanks[pi % 3]
                            pb2, pbb = pbanks[(pi + 1) % 3]
                            pi += 2
                            for kc in range(8):
                                MM(pa[:, 0:n], w1[:, kc, 0, jj * 128:(jj + 1) * 128], mix[:, kc, t0:t0 + n], kc == 0, kc == 7, mb + wtb, [pab])
                            for kc in range(8):
                                MM(pb2[:, 0:n], w1[:, kc, 1, jj * 128:(jj + 1) * 128], mix[:, kc, t0:t0 + n], kc == 0, kc == 7, mb + wtb, [pbb])
                            ACT(sa[:, 0:n], pa[:, 0:n], AF.Silu, [pab], bgt[4:6])
                            TT("dve", gT[:, jj, 0:n], sa[:, 0:n], pb2[:, 0:n], ALU.mult, bgt[4:6] + [pbb], gTb)
                        for ncn in range(8):
                            py, pyb_ = pyb[yi % 4]
                            yi += 1
                            for jj in range(G):
                                MM(py[:, 0:n], w2v[:, jj, ncn * 128:(ncn + 1) * 128], gT[:, jj, 0:n], jj == 0, jj == G - 1, gTb + wtb, pyb_)
                            jm = 1 if blks == [0] else 0
                            STT(XT[:, ncn, t0:t0 + n], py[:, 0:n], PAR[:, l, jm, 5, ncn:ncn + 1], XT[:, ncn, t0:t0 + n], ALU.mult, ALU.add,
                                pyb_ + [b["PAR"]] + xb, xb)
                relA, relB = [], []

            stage(30)
            outb = []
            for blk in range(1, NB):
                t0 = blk * BS
                xs = XT[:, :, t0:t0 + BS]
                ACT(hTb[:], xs, AF.Square, [bXT[blk]], [b["hTb"]])
                for kc in range(8):
                    MM(pV[:, 0:BS], ones_b[:], hTb[:, kc, :], kc == 0, kc == 7, [b["hTb"], b["const"]], [b["pV"]])
                rsqrt_psum(pV[:, 0:BS], BS, 1.0 / D, [b["pV"], b["const"]], pV[:, 256:512], [b["pV"]], gt[:, 9, :], [bgt[9]])
                TT("dve", nt[:], xs, pV[:, 256:512].unsqueeze(1).broadcast_to([128, 8, BS]), ALU.mult, [bXT[blk], b["pV"]], NTB)
                for kc in range(8):
                    ACT(nt[:, kc, :], nt[:, kc, :], AF.Identity, NTB + [b["par"], b["lpar"]], NTB, scale=fng[:, kc:kc + 1])
                for tl in range(2):
                    for half in range(2):
                        pb, bb = (pA, b["pA"]) if half == 0 else (pB, b["pB"])
                        for k4 in range(4):
                            kc = half * 4 + k4
                            TR(pb[:, k4 * 128:(k4 + 1) * 128], nt[:, kc, tl * 128:(tl + 1) * 128], ident_f[:], NTB + [b["const"]], [bb])
                        ob = mix[:, tl, 0:2048].bitcast(F32)
                        CP("act" if half == 0 else "dve", ob[:, half * 512:(half + 1) * 512], pb[:, :], [bb], bmix + [bosb[tl]])
                    r0 = (blk - 1) * BS + tl * 128
                    DMA("sp", y_d[r0:r0 + 128, :], ob[:], [bosb[tl]], [b["y"]])

        except _Stop:
            pass
        if dumps:
            env = dict(locals())
            ALLB = list(b.values()) + bXT + bmix + bgt + bS + bpW + bdg
            bdst = Buf("dstg")
            off = 0
            for (ap, n) in dumps(env):
                dst = dbg_d[:, off:off + n]
                stv = dstg[:, 0:n]
                if len(ap.shape) == 3:
                    dst = dst.rearrange("p (a t) -> p a t", a=ap.shape[1])
                    stv = stv.rearrange("p (a t) -> p a t", a=ap.shape[1])
                S.op("dve", lambda h, ap=ap, stv=stv: h.tensor_copy(out=stv, in_=ap), ALLB + [bdst], [bdst])
                S.dma("sp", lambda h, stv=stv, dst=dst: h.dma_start(out=dst, in_=stv), [bdst], [b["dbg"], bdst])
                off += n
        S.wait_all("sp", [b["y"]])
        if dbg:
            S.wait_all("sp", [b["dbg"]])
        S.emit(block)
    return nc


def _col(v):
    v = np.asarray(v)
    lead = v.shape[:-1]
    n = v.shape[-1] // 128
    r = v.reshape(lead + (n, 128))
    return np.ascontiguousarray(np.moveaxis(r, -1, 0))


def prep_inputs(inp):
    f = lambda a: np.ascontiguousarray(np.asarray(a, dtype=np.float32))
    x, c, ctx, c_ctx = f(inp["x"]), f(inp["c"]), f(inp["ctx"]), f(inp["c_ctx"])
    w_in = f(inp["w_in"])
    w_in_odd = w_in.copy()
    w_in_odd[:, :, 512:1024] = w_in[:, :, 1024:1536]
    w_in_odd[:, :, 1024:1536] = w_in[:, :, 512:1024]
    lbl = f(inp["hgrn_lb_logits"])
    ws = f(inp["gmlp_w_s"])
    bsv = f(inp["gmlp_b_s"])
    dww = f(inp["conv_dw_w"])
    common = {
        "ada_w": f(inp["ada_w"]),
        "ada_b": _col(f(inp["ada_b"])),
        "n1g": _col(f(inp["norm1_g"])),
        "n2g": _col(f(inp["norm2_g"])),
        "fng": _col(f(inp["final_norm_g"])),
        "ong": _col(f(inp["hgrn_onorm_g"])),
        "gln": np.ascontiguousarray(np.broadcast_to(np.stack([f(inp["gmlp_ln_g"]), f(inp["gmlp_ln_b"])], axis=1)[None], (128, L, 2, 256))),
        "cpar": np.ascontiguousarray(np.stack([_col(f(inp["conv_dw_b"])), _col(f(inp["conv_ln_g"])), _col(f(inp["conv_ln_b"])),
                                               _col(f(inp["conv_pw_b"]))], axis=2)),
        "pww": f(inp["conv_pw_w"]),
        "w_out": f(inp["w_out"]),
        "w13": f(inp["ffn_w13"]),
        "w2": f(inp["ffn_w2"]),
    }
    per_par = []
    for par in range(2):
        m = par == 1
        ws_p = ws[:, :, ::-1, ::-1] if m else ws
        bs_p = bsv[:, :, ::-1] if m else bsv
        dw_p = dww[:, ::-1, :] if m else dww
        lb_p = lbl[:, ::-1, :] if m else lbl
        d = dict(common)
        d["w_in"] = w_in_odd if m else w_in
        d["lbl"] = _col(lb_p)
        d["wsT"] = np.ascontiguousarray(np.transpose(ws_p, (3, 0, 1, 2)))
        bsl = np.zeros((128, L, 2, 128), np.float32)
        for ch in range(2):
            for hh in range(2):
                bsl[hh * 64:(hh + 1) * 64, :, ch, :] = bs_p[:, ch * 2 + hh, :][None]
        d["bs"] = bsl
        d["dww"] = np.ascontiguousarray(np.transpose(dw_p.reshape(L, 31, 2, 128), (3, 0, 2, 1)))
        per_par.append(d)
    ins = []
    for core in range(8):
        bi, par = core // 2, core % 2
        d = dict(per_par[par])
        if par == 0:
            xl = np.concatenate([ctx[bi], x[bi, 0:2048]], axis=0)
        else:
            xl = np.concatenate([ctx[bi, ::-1], x[bi, 4095:2047:-1]], axis=0)
        d["x"] = np.ascontiguousarray(xl)
        d["cc"] = np.ascontiguousarray(np.stack([_col(c[bi]), _col(c_ctx)], axis=-1))
        sel = np.zeros((128, 2), np.float32)
        sel[:, 1 - par] = 1.0
        d["sel"] = sel
        ins.append(d)
    return ins


_NC = None


def kernel(**inputs):
    global _NC
    ins = prep_inputs(inputs)
    if _NC is None:
        _NC = build()
    res = run_bass_kernel_spmd(_NC, ins, core_ids=list(range(8)))
    out = np.zeros((4, 4096, D), np.float32)
    for core in range(8):
        bi, par = core // 2, core % 2
        y = res.results[core]["y"]
        if par == 0:
            out[bi, 0:2048] = y
        else:
            out[bi, 4095:2047:-1] = y
    return out
```
